# Optimizing a Trainium2 kernel written in Bass

```python
import math
import jax, jax.numpy as jnp
from jax import lax
import numpy as np

D_MODEL = 1024
BATCH = 8
SEQ = 2048
DEPTH = 2
DEC_BATCH = 128
DEC_SEQ = 8
PAST_LEN = 16384
PAGE_SIZE = 128

N_AB = (DEPTH + 1) // 2
N_CONV = DEPTH // 2
A_HEADS = 8
A_HEAD_DIM = 64
A_WIDTH = A_HEADS * A_HEAD_DIM
DECAY_LORA = 64
AAA_LORA = 64
GATE_LORA = 128
A_PROJ = 3 * A_WIDTH + DECAY_LORA + AAA_LORA + GATE_LORA
A_SPLITS = (A_WIDTH, 2 * A_WIDTH, 3 * A_WIDTH, 3 * A_WIDTH + DECAY_LORA, 3 * A_WIDTH + DECAY_LORA + AAA_LORA)
A_GN_EPS = 64e-5
B_HEADS = 8
B_HEAD_DIM = 64
B_WIDTH = B_HEADS * B_HEAD_DIM
B_PROJ = 4 * B_WIDTH
RET_CHUNK = 128
ROPE_BASE = 10000.0
B_GN_EPS = 1e-5
CONV_W = 3
D_FF = 4 * D_MODEL
LN_EPS = 1e-5
ALPHA = (2.0 * DEPTH) ** 0.25
BETA = (8.0 * DEPTH) ** -0.25

kernel_name = 'hybrid_rwkv7_retnet_shortconv_step'


def layer_norm(x, g, b):
    xf = x.astype(jnp.float32)
    mu = jnp.mean(xf, -1, keepdims=True)
    var = jnp.mean(jnp.square(xf - mu), -1, keepdims=True)
    return ((xf - mu) * lax.rsqrt(var + LN_EPS)).astype(x.dtype) * g + b


def head_norm(y, g, b, eps):
    Bn, T, H, N = y.shape
    mu = jnp.mean(y, -1, keepdims=True)
    var = jnp.mean(jnp.square(y - mu), -1, keepdims=True)
    yn = ((y - mu) * lax.rsqrt(var + eps)).reshape(Bn, T, H * N)
    return yn * g.astype(jnp.float32) + b.astype(jnp.float32)


def rope(x, pos):
    half = x.shape[-1] // 2
    inv = ROPE_BASE ** (-jnp.arange(half, dtype=jnp.float32) / half)
    ang = pos.astype(jnp.float32)[:, None] * inv[None, :]
    cos = jnp.cos(ang)[None, :, None, :]
    sin = jnp.sin(ang)[None, :, None, :]
    x1, x2 = x[..., :half], x[..., half:]
    return jnp.concatenate([x1 * cos - x2 * sin, x2 * cos + x1 * sin], axis=-1)


def rwkv7_mixer(p, p_last, S0, mu, w0, w2, a0, a2, g2, k_k, k_a, r_k, lnx_g, lnx_b):
    Bn, T, _ = p.shape
    p_prev = jnp.concatenate([p_last[:, None, :].astype(p.dtype), p[:, :-1]], axis=1)
    m = p + (p_prev - p) * mu
    r, k, v, wd, ad, gd = jnp.split(m, A_SPLITS, axis=-1)
    w = -jax.nn.softplus(-(w0 + jnp.tanh(wd) @ w2)) - 0.5
    decay = jnp.exp(-jnp.exp(w.astype(jnp.float32)))
    a = jax.nn.sigmoid(a0 + ad @ a2)
    g = jax.nn.sigmoid(gd) @ g2
    kk = k * k_k
    k = k * (1.0 + (a - 1.0) * k_a)
    heads = lambda t: t.astype(jnp.float32).reshape(Bn, T, A_HEADS, A_HEAD_DIM)
    r, k, v, kk, a, decay = map(heads, (r, k, v, kk, a, decay))
    kk = kk * lax.rsqrt(jnp.maximum(jnp.sum(kk * kk, -1, keepdims=True), 1e-24))

    def step(S, inp):
        r_t, k_t, v_t, kk_t, a_t, w_t = inp
        sa = jnp.einsum('bhvk,bhk->bhv', S, -kk_t)
        S = (S * w_t[:, :, None, :] + sa[..., None] * (kk_t * a_t)[:, :, None, :]
             + v_t[..., None] * k_t[:, :, None, :])
        return S, jnp.einsum('bhvk,bhk->bhv', S, r_t)

    xs = tuple(jnp.swapaxes(t, 0, 1) for t in (r, k, v, kk, a, decay))
    S, o = lax.scan(step, S0.astype(jnp.float32), xs)
    o = jnp.swapaxes(o, 0, 1)
    bonus = (jnp.sum(r * k * r_k.astype(jnp.float32), -1, keepdims=True) * v).reshape(Bn, T, A_WIDTH)
    y = (head_norm(o, lnx_g, lnx_b, A_GN_EPS) + bonus) * g.astype(jnp.float32)
    return y, p[:, -1], S.astype(S0.dtype)


def retention(q, k, v, S0, log_gamma):
    Bn, T, H, N = q.shape
    C = RET_CHUNK if T % RET_CHUNK == 0 else T
    nc = T // C
    chunks = lambda t: t.reshape(Bn, nc, C, H, N).transpose(1, 0, 3, 2, 4)
    idx = jnp.arange(C, dtype=jnp.float32)
    diff = idx[:, None] - idx[None, :]
    lg = log_gamma[:, None, None]
    dmask = jnp.where(diff[None] >= 0, jnp.exp(lg * jnp.maximum(diff, 0.0)[None]), 0.0)
    q_decay = jnp.exp(log_gamma[:, None] * (idx + 1.0)[None, :])[None, :, :, None]
    k_decay = jnp.exp(log_gamma[:, None] * (C - 1.0 - idx)[None, :])[None, :, :, None]
    c_decay = jnp.exp(log_gamma * C)[None, :, None, None]

    def step(S, inp):
        qi, ki, vi = inp
        scores = jnp.einsum('bhin,bhjn->bhij', qi, ki) * dmask
        inner = jnp.einsum('bhij,bhjn->bhin', scores, vi)
        cross = jnp.einsum('bhik,bhkv->bhiv', qi, S) * q_decay
        S = S * c_decay + jnp.einsum('bhjk,bhjv->bhkv', ki * k_decay, vi)
        return S, inner + cross

    S, o = lax.scan(step, S0, (chunks(q), chunks(k), chunks(v)))
    return o.transpose(1, 0, 3, 2, 4).reshape(Bn, T, H, N), S


def ab_mixer(x, shift0, wkv0, ret0, pos, w_in, mu, w0, w2, a0, a2, g2, k_k, k_a, r_k,
             lnx_g, lnx_b, gn_g, gn_b, w_out):
    Bn, T, _ = x.shape
    proj = x @ w_in
    pa, pb = proj[..., :A_PROJ], proj[..., A_PROJ:]
    ya, shift1, wkv1 = rwkv7_mixer(pa, shift0, wkv0, mu, w0, w2, a0, a2, g2, k_k, k_a, r_k, lnx_g, lnx_b)
    q, kr, vr, gate = jnp.split(pb, 4, axis=-1)
    hb = lambda t: t.astype(jnp.float32).reshape(Bn, T, B_HEADS, B_HEAD_DIM)
    q = rope(hb(q), pos)
    kr = rope(hb(kr), pos) * (B_HEAD_DIM ** -0.5)
    log_gamma = jnp.log1p(-jnp.exp2(-5.0 - jnp.arange(B_HEADS, dtype=jnp.float32)))
    ob, ret1 = retention(q, kr, hb(vr), ret0.astype(jnp.float32), log_gamma)
    yb = jax.nn.silu(gate.astype(jnp.float32)) * head_norm(ob, gn_g, gn_b, B_GN_EPS)
    y = jnp.concatenate([ya, yb], axis=-1).astype(x.dtype) @ w_out
    return y, shift1, wkv1, ret1.astype(ret0.dtype)


def conv_mixer(x, buf0, w_in, conv_w, w_out):
    T = x.shape[1]
    bg, cg, h = jnp.split(x @ w_in, 3, axis=-1)
    u = cg * h
    ext = jnp.concatenate([buf0.astype(u.dtype), u], axis=1)
    conv = sum(ext[:, j:j + T] * conv_w[j] for j in range(CONV_W))
    return (bg * conv) @ w_out, ext[:, ext.shape[1] - (CONV_W - 1):]


def sqrelu_mlp(x, w_up, w_down):
    return jnp.square(jax.nn.relu(x @ w_up)) @ w_down


def trunk(x, st_shift, st_wkv, st_ret, st_conv, pos, w_in_ab, mu_a, w0, w2, a0, a2, g2, k_k, k_a, r_k,
          lnx_g, lnx_b, gn_g, gn_b, w_out_ab, w_in_conv, conv_w, w_out_conv,
          ln1_g, ln1_b, ln2_g, ln2_b, w_up, w_down):
    shifts, wkvs, rets, convs = [], [], [], []
    for l in range(DEPTH):
        i = l // 2
        if l % 2 == 0:
            y, s1, wk1, r1 = ab_mixer(x, st_shift[i], st_wkv[i], st_ret[i], pos, w_in_ab[i], mu_a[i], w0[i],
                                      w2[i], a0[i], a2[i], g2[i], k_k[i], k_a[i], r_k[i], lnx_g[i], lnx_b[i],
                                      gn_g[i], gn_b[i], w_out_ab[i])
            shifts.append(s1)
            wkvs.append(wk1)
            rets.append(r1)
        else:
            y, c1 = conv_mixer(x, st_conv[i], w_in_conv[i], conv_w[i], w_out_conv[i])
            convs.append(c1)
        x = layer_norm(ALPHA * x + y, ln1_g[l], ln1_b[l])
        x = layer_norm(ALPHA * x + sqrelu_mlp(x, w_up[l], w_down[l]), ln2_g[l], ln2_b[l])
    return x, jnp.stack(shifts), jnp.stack(wkvs), jnp.stack(rets), jnp.stack(convs)


def setup_inputs(seed: int = 0) -> dict:
    key = jax.random.key(seed)
    ks = jax.random.split(key, 30)
    f32 = jnp.float32
    nrm = lambda k, shape, s: s * jax.random.normal(k, shape, f32)
    uni = lambda k, shape, lo, hi: jax.random.uniform(k, shape, f32, lo, hi)
    return {
        'x_prompt': nrm(ks[0], (BATCH, SEQ, D_MODEL), 1.0),
        'x_sample': nrm(ks[1], (DEC_BATCH, DEC_SEQ, D_MODEL), 1.0),
        'state_shift': nrm(ks[2], (N_AB, DEC_BATCH, A_PROJ), 1.0),
        'state_wkv': nrm(ks[3], (N_AB, DEC_BATCH, A_HEADS, A_HEAD_DIM, A_HEAD_DIM), 0.5),
        'state_ret': nrm(ks[4], (N_AB, DEC_BATCH, B_HEADS, B_HEAD_DIM, B_HEAD_DIM), 1.0),
        'state_conv': nrm(ks[5], (N_CONV, DEC_BATCH, CONV_W - 1, D_MODEL), 1.0),
        'w_in_ab': nrm(ks[6], (N_AB, D_MODEL, A_PROJ + B_PROJ), D_MODEL ** -0.5),
        'mu_a': uni(ks[7], (N_AB, A_PROJ), 0.0, 1.0),
        'w0': uni(ks[8], (N_AB, A_WIDTH), -4.0, 0.0),
        'w2': nrm(ks[9], (N_AB, DECAY_LORA, A_WIDTH), 0.5 * DECAY_LORA ** -0.5),
        'a0': nrm(ks[10], (N_AB, A_WIDTH), 0.1),
        'a2': nrm(ks[11], (N_AB, AAA_LORA, A_WIDTH), 0.5 * AAA_LORA ** -0.5),
        'g2': nrm(ks[12], (N_AB, GATE_LORA, A_WIDTH), GATE_LORA ** -0.5),
        'k_k': 0.85 + nrm(ks[13], (N_AB, A_WIDTH), 0.02),
        'k_a': 1.0 + nrm(ks[14], (N_AB, A_WIDTH), 0.02),
        'r_k': nrm(ks[15], (N_AB, A_HEADS, A_HEAD_DIM), 0.1),
        'lnx_g': 1.0 + nrm(ks[16], (N_AB, A_WIDTH), 0.01),
        'lnx_b': nrm(ks[17], (N_AB, A_WIDTH), 0.01),
        'gn_g': 1.0 + nrm(ks[18], (N_AB, B_WIDTH), 0.01),
        'gn_b': nrm(ks[19], (N_AB, B_WIDTH), 0.01),
        'w_out_ab': nrm(ks[20], (N_AB, A_WIDTH + B_WIDTH, D_MODEL), BETA * (A_WIDTH + B_WIDTH) ** -0.5),
        'w_in_conv': nrm(ks[21], (N_CONV, D_MODEL, 3 * D_MODEL), D_MODEL ** -0.5),
        'conv_w': nrm(ks[22], (N_CONV, CONV_W, D_MODEL), CONV_W ** -0.5),
        'w_out_conv': nrm(ks[23], (N_CONV, D_MODEL, D_MODEL), BETA * D_MODEL ** -0.5),
        'ln1_g': 1.0 + nrm(ks[24], (DEPTH, D_MODEL), 0.01),
        'ln1_b': nrm(ks[25], (DEPTH, D_MODEL), 0.01),
        'ln2_g': 1.0 + nrm(ks[26], (DEPTH, D_MODEL), 0.01),
        'ln2_b': nrm(ks[27], (DEPTH, D_MODEL), 0.01),
        'w_up': nrm(ks[28], (DEPTH, D_MODEL, D_FF), D_MODEL ** -0.5),
        'w_down': nrm(ks[29], (DEPTH, D_FF, D_MODEL), BETA * D_FF ** -0.5),
    }


def reference(x_prompt, x_sample, state_shift, state_wkv, state_ret, state_conv, w_in_ab, mu_a, w0, w2, a0, a2,
              g2, k_k, k_a, r_k, lnx_g, lnx_b, gn_g, gn_b, w_out_ab, w_in_conv, conv_w, w_out_conv,
              ln1_g, ln1_b, ln2_g, ln2_b, w_up, w_down):
    params = (w_in_ab, mu_a, w0, w2, a0, a2, g2, k_k, k_a, r_k, lnx_g, lnx_b, gn_g, gn_b, w_out_ab,
              w_in_conv, conv_w, w_out_conv, ln1_g, ln1_b, ln2_g, ln2_b, w_up, w_down)
    Bp, Tp, _ = x_prompt.shape
    dt = state_wkv.dtype
    z_shift = jnp.zeros((N_AB, Bp, A_PROJ), dt)
    z_wkv = jnp.zeros((N_AB, Bp, A_HEADS, A_HEAD_DIM, A_HEAD_DIM), dt)
    z_ret = jnp.zeros((N_AB, Bp, B_HEADS, B_HEAD_DIM, B_HEAD_DIM), dt)
    z_conv = jnp.zeros((N_CONV, Bp, CONV_W - 1, D_MODEL), dt)
    pos_p = jnp.arange(Tp, dtype=jnp.int32)
    y_prompt, p_shift, p_wkv, p_ret, p_conv = trunk(x_prompt, z_shift, z_wkv, z_ret, z_conv, pos_p, *params)
    pos_s = PAST_LEN + jnp.arange(x_sample.shape[1], dtype=jnp.int32)
    y_sample, s_shift, s_wkv, s_ret, s_conv = trunk(x_sample, state_shift, state_wkv, state_ret, state_conv,
                                                    pos_s, *params)
    return (y_prompt, y_sample, p_shift, p_wkv, p_ret, p_conv, s_shift, s_wkv, s_ret, s_conv)
```

```python
import math
import numpy as np
import concourse.bass as bass
import concourse.mybir as mybir
from concourse.bass_utils import run_bass_kernel_spmd

F32 = mybir.dt.float32
BF16 = mybir.dt.bfloat16
AF = mybir.ActivationFunctionType
ALU = mybir.AluOpType
AX = mybir.AxisListType

D = 1024
SEQ = 2048
NCORE = 8
SB_PER = 16
DEC_SEQ = 8
NTOK = SEQ + SB_PER * DEC_SEQ
A_PROJ = 1792
PAST_LEN = 16384
A_GN_EPS = 64e-5
B_GN_EPS = 1e-5
LN_EPS = 1e-5
ALPHA = 4.0 ** 0.25
EM05 = math.exp(-0.5)

GT = 256
NSLOT = 32

PP_MU, PP_KK, PP_KA, PP_RK, PP_A0 = 0, 14, 18, 22, 26
PP_LN = 30
PP_CW = 94
PP_LXG, PP_LXB, PP_GNG, PP_GNB = 118, 122, 126, 130
PP_N = 134


class Buf:
    def __init__(self):
        self.w = {}
        self.r = {}


class Sy:
    EPOCH = 8000

    def __init__(self, nc):
        self.nc = nc
        self.eng = {'pe': nc.tensor, 'act': nc.scalar, 'dve': nc.vector, 'pool': nc.gpsimd, 'sp': nc.sync}
        self.cnt = {k: 0 for k in self.eng}
        self.sems = {k: [] for k in self.eng}
        self.seen = {k: {} for k in self.eng}
        self.dsem = {}
        self.drr = {k: 0 for k in self.eng}
        self.last = {}
        self.nwait = 0

    def _wait(self, e, ev):
        key, h, v = ev
        if self.seen[e].get(key, 0) >= v:
            return
        self.eng[e].wait_ge(h, v)
        self.nwait += 1
        self.seen[e][key] = v

    def _deps(self, e, reads, writes, dma=False):
        evs = []
        for b in reads:
            evs.extend(b.w.values())
        for b in writes:
            for k, ev in b.w.items():
                if k == e and not dma:
                    continue
                evs.append(ev)
            for k, ev in b.r.items():
                if k == e and not dma:
                    continue
                evs.append(ev)
        for ev in evs:
            self._wait(e, ev)

    def op(self, e, reads, writes, fn, *a, **kw):
        self._deps(e, reads, writes)
        inst = fn(*a, **kw)
        n = self.cnt[e]
        ep, v = divmod(n, self.EPOCH)
        if ep >= len(self.sems[e]):
            self.sems[e].append(self.nc.alloc_semaphore(name=f"s_{e}_{ep}"))
        h = self.sems[e][ep]
        inst.then_inc(h, 1)
        self.cnt[e] = n + 1
        ev = ((e, ep), h, v + 1)
        self.last[e] = ev
        for b in reads:
            b.r[e] = ev
        for b in writes:
            b.w[e] = ev
        return ev

    def dma(self, e, reads, writes, out, in_, **kw):
        if e not in self.dsem:
            self.dsem[e] = [[self.nc.alloc_semaphore(name=f"d_{e}_{i}"), 0] for i in range(40 if e == 'pool' else 8)]
        i = self.drr[e]
        self.drr[e] = (i + 1) % len(self.dsem[e])
        slot = self.dsem[e][i]
        key = ('dma', e, i)
        if slot[1] > 0:
            self._wait(e, (key, slot[0], slot[1]))
        self._deps(e, reads, writes, dma=True)
        inst = self.eng[e].dma_start(out=out, in_=in_, **kw)
        slot[1] += 16
        inst.then_inc(slot[0], 16)
        ev = (key, slot[0], slot[1])
        for b in reads:
            b.r[key] = ev
        for b in writes:
            b.w[key] = ev
        return ev

    def barrier(self):
        evs = list(self.last.values())
        for e2, slots in self.dsem.items():
            for i, s in enumerate(slots):
                if s[1] > 0:
                    evs.append((('dma', e2, i), s[0], s[1]))
        for e in self.eng:
            for ev in evs:
                self._wait(e, ev)


def host_consts():
    c = {}
    ident = np.eye(128, dtype=np.float32)
    ones = np.ones((128, 128), np.float32)
    blk = np.zeros((128, 128), np.float32)
    blk[:64, :64] = 1
    blk[64:, 64:] = 1
    rot = np.zeros((128, 128), np.float32)
    for hp in range(2):
        for n in range(64):
            if n < 32:
                rot[hp * 64 + n + 32, hp * 64 + n] = -1.0
            else:
                rot[hp * 64 + n - 32, hp * 64 + n] = 1.0
    c['c_sq'] = np.stack([ident, ones, blk, rot], 1)
    hm = np.zeros((128, 4), np.float32)
    hm[:64, 0] = 1
    hm[64:, 1] = 1
    hm[:, 2:4] = -hm[:, 0:2]
    c['c_hm'] = hm
    kinds = []
    dms = []
    decs = []
    cdecs = []
    lg = np.log1p(-np.exp2(-5.0 - np.arange(8, dtype=np.float32))).astype(np.float32)
    for nseq in (1, 16):
        C = 128 // nseq
        s = np.arange(128)
        same = (s[:, None] // C) == (s[None, :] // C)
        le = s[:, None] <= s[None, :]
        lt = s[:, None] < s[None, :]
        triI = (same & le).astype(np.float32) * (-EM05)
        triS = (same & lt).astype(np.float32) * (-EM05)
        mTs = (same & lt).astype(np.float32)
        mTi = (same & le).astype(np.float32)
        mLs = mTs.T.copy()
        kinds.append(np.stack([triI, triS, mTs, mTi, mLs], 1))
        diff = (s[None, :] - s[:, None]).astype(np.float32)
        dm = np.zeros((128, 8, 128), np.float32)
        for h in range(8):
            dm[:, h, :] = np.where(same & le, np.exp(lg[h] * np.maximum(diff, 0.0)), 0.0)
        dms.append(dm)
        i_in = (s % C).astype(np.float32)
        dec = np.zeros((128, 2, 4, 128), np.float32)
        cd = np.zeros((128, 4), np.float32)
        for cc in range(4):
            for hp in range(2):
                h = 2 * cc + hp
                dec[hp * 64:(hp + 1) * 64, 0, cc, :] = np.exp(lg[h] * (i_in + 1.0))[None, :]
                dec[hp * 64:(hp + 1) * 64, 1, cc, :] = np.exp(lg[h] * (C - 1.0 - i_in))[None, :]
                cd[hp * 64:(hp + 1) * 64, cc] = np.exp(lg[h] * C)
        decs.append(dec)
        cdecs.append(cd)
    c['c_kind'] = np.stack(kinds, 0)
    c['c_dmask'] = np.stack(dms, 0)
    c['c_dec'] = np.stack(decs, 0)
    c['c_cdec'] = np.stack(cdecs, 0)
    sm = np.zeros((128, 16), np.float32)
    sm[np.arange(128), np.arange(128) // 8] = 1
    c['c_seq'] = sm
    half = 32
    inv = (10000.0 ** (-np.arange(half, dtype=np.float64) / float(half))).astype(np.float32)
    pos = np.concatenate([np.arange(SEQ), np.tile(PAST_LEN + np.arange(DEC_SEQ), SB_PER)]).astype(np.float32)
    ang = (pos[:, None] * inv[None, :]).astype(np.float32)
    cos = np.cos(ang).astype(np.float32)
    sin = np.sin(ang).astype(np.float32)
    fi = np.arange(128) % 32
    cosT = cos[:, fi].T
    sinT = sin[:, fi].T
    rp = np.zeros((17, 128, 4, 128), np.float32)
    for t in range(17):
        rp[t, :, 0] = cosT[:, t * 128:(t + 1) * 128]
        rp[t, :, 1] = sinT[:, t * 128:(t + 1) * 128]
        rp[t, :, 2] = cosT[:, t * 128:(t + 1) * 128] * 0.125
        rp[t, :, 3] = sinT[:, t * 128:(t + 1) * 128] * 0.125
    c['c_rope'] = rp
    return c


WEIGHTS = [('w_in_ab', [D, 3840]), ('w_out_ab', [D, D]), ('w_up', [2, D, 4 * D]), ('w_down', [2, 4 * D, D]),
           ('w_in_conv', [D, 3 * D]), ('w_out_conv', [D, D])]


def build_program(ngroups_limit=None, dbg=None, only_sample=False):
    nc = bass.Bass("TRN2", target_bir_lowering=False)
    sy = Sy(nc)
    dbg = dbg or {}

    def din(name, shape):
        return nc.dram_tensor(name, list(shape), F32, kind="ExternalInput").ap()

    def dout(name, shape):
        return nc.dram_tensor(name, list(shape), F32, kind="ExternalOutput").ap()

    xin = din("xin", [NTOK, D])
    st_shift = din("st_shift", [16, A_PROJ])
    st_wkv = din("st_wkv", [16, 8, 64, 64])
    st_ret = din("st_ret", [16, 8, 64, 64])
    st_conv = din("st_conv", [32, D])
    pp_d = din("pp", [128, PP_N])
    tokb_d = din("tokb", [128, 512])
    smallw_d = din("smallw", [128, 3, 512])
    c_sq_d = din("c_sq", [128, 4, 128])
    c_hm_d = din("c_hm", [128, 4])
    c_kind_d = din("c_kind", [2, 128, 5, 128])
    c_dmask_d = din("c_dmask", [2, 128, 8, 128])
    c_dec_d = din("c_dec", [2, 128, 2, 4, 128])
    c_cdec_d = din("c_cdec", [2, 128, 4])
    c_seq_d = din("c_seq", [128, 16])
    c_rope_d = din("c_rope", [17, 128, 4, 128])
    W = {n: din(n, s) for n, s in WEIGHTS}

    yout = dout("yout", [NTOK, D])
    shift_out = dout("shift_out", [17, A_PROJ])
    wkv_out = dout("wkv_out", [17, 8, 64, 64])
    ret_out = dout("ret_out", [17, 8, 64, 64])
    conv_out = dout("conv_out", [34, D])
    out_evs = []

    def sbt(name, shape, dt=F32):
        return nc.sbuf_tensor("s_" + name, list(shape), dt).__enter__()

    PP_OMU, PP_OKA = PP_N, PP_N + 14
    pp = sbt("pp", [128, PP_N + 18]); t_pp = Buf()
    tokb = sbt("tokb", [128, 512]); t_const = Buf()
    smallw = sbt("smallw", [128, 3, 512])
    csq = sbt("csq", [128, 4, 128])
    chm = sbt("chm", [128, 4])
    cseq = sbt("cseq", [128, 16])
    ckind = sbt("ckind", [128, 5, 128]); t_kc = Buf()
    cdmask = sbt("cdmask", [128, 8, 128])
    cdec = sbt("cdec", [128, 2, 4, 128])
    ccdec = sbt("ccdec", [128, 4])
    rope = sbt("rope", [128, 4, 128]); t_rope = Buf()
    ident = csq[:, 0, :]
    ones = csq[:, 1, :]
    blkones = csq[:, 2, :]
    rotT = csq[:, 3, :]

    xT = sbt("xT", [128, 8, GT]); t_xTc = [Buf() for _ in range(8)]
    yT = sbt("yT", [128, 8, GT], BF16); t_yT = Buf()
    xTb = sbt("xTb", [128, 8, GT], BF16); t_xTbc = [Buf() for _ in range(8)]; t_lnc = [Buf() for _ in range(8)]
    bigflat = sbt("big", [128, NSLOT * GT]); t_big = Buf()
    WR_N = 3
    wring = [sbt(f"wr{i}", [128, 4096], BF16) for i in range(WR_N)]
    bigbf = bigflat[:].bitcast(BF16)
    t_wr = [Buf() for _ in range(WR_N)]
    wr_ctr = [0]
    xtok = sbt("xtok", [128, 1024]); t_xtok = Buf()
    stage, t_stage = xtok, t_xtok
    statsb = sbt("statsb", [128, 3, GT]); t_stat = Buf()
    Hp = sbt("Hp", [128, 4, 64]); t_H = Buf()
    Rp = sbt("Rp", [128, 4, 64]); t_R = Buf()
    Hs = sbt("Hs", [128, 4, 16, 64])
    Rs = bigflat[:, NSLOT * 128:NSLOT * 128 + 4096].rearrange("p (c s v) -> p c s v", c=4, s=16)
    prevP = sbt("prevP", [128, 14, 16]); t_prev = Buf()
    plast = sbt("plast", [128, 14, 16]); t_plast = Buf()
    prevc = sbt("prevc", [128, 8, 16, 2]); t_prevc = Buf()
    clast = sbt("clast", [128, 8, 16, 2]); t_clast = Buf()
    Mx = sbt("Mx", [128, 14, 128]); t_Mx = Buf()
    G = sbt("G", [128, 128]); t_G = Buf()
    sg = sbt("sg", [128, 128]); t_sg = Buf()
    ztok = sbt("ztok", [128, 512]); t_z = Buf()
    cum = sbt("cum", [128, 4, 128]); t_cum = Buf()
    cumx = sbt("cumx", [128, 4, 128]); t_cumx = Buf()
    a_t = sbt("a_t", [128, 4, 128]); t_a = Buf()
    kk = sbt("kk", [128, 4, 128]); t_kk = Buf()
    kp = sbt("kp", [128, 4, 128]); t_kp = Buf()
    beta = sbt("beta", [128, 4, 128]); t_beta = Buf()
    E = sbt("E", [128, 4, 128]); t_E = Buf()
    T1 = sbt("T1", [128, 4, 128]); t_T1 = Buf()
    Az = sbt("Az", [128, 4, 2, 128]); t_Az = Buf()
    Rz = sbt("Rz", [128, 4, 2, 128]); t_Rz = Buf()
    Bt = sbt("Bt", [128, 4, 128]); t_Bt = Buf()
    Kt = sbt("Kt", [128, 4, 128]); t_Kt = Buf()
    Khz = sbt("Khz", [128, 4, 2, 128]); t_Khz = Buf()
    Bhz = sbt("Bhz", [128, 4, 2, 128]); t_Bhz = Buf()
    Vtok = sbt("Vtok", [128, 512]); t_V = Buf()
    gT = sbt("gT", [128, 4, 128]); t_g = Buf()
    gC = sbt("gC", [128, 4, 16]); t_gC = Buf()
    DB = sbt("DB", [128, 5, 8, 128])
    t_Mb = [Buf(), Buf()]; t_MTb = [Buf(), Buf()]; t_PT = Buf()
    DBf = DB[:].rearrange("p a h t -> p (a h t)")
    LK = sbt("LK", [128, 3, 4, 128]); t_LK = Buf()
    Wt = sbt("Wt", [128, 4, 64]); t_W = Buf()
    Uall = sbt("Uall", [128, 8, 64]); t_U = Buf()
    Oall = sbt("Oall", [128, 8, 64]); t_O = Buf()
    Xc = sbt("Xc", [128, 4, 64]); t_Xc = Buf()
    st1 = sbt("st1", [128, 8]); st2 = sbt("st2", [128, 8]); t_st = Buf()

    psb = [nc.psum_tensor(f"ps{i}", [128, 512], F32).__enter__() for i in range(8)]
    t_ps = [Buf() for _ in range(8)]
    ps_ctr = [0]

    def nps():
        i = 2 + ps_ctr[0] % 6
        ps_ctr[0] += 1
        return psb[i], t_ps[i]

    dps_ctr = [0]

    def dps():
        i = dps_ctr[0] % 2
        dps_ctr[0] += 1
        return psb[i], t_ps[i]

    O = sy.op
    V = nc.vector
    Pl = nc.gpsimd
    A = nc.scalar
    T = nc.tensor

    def mm(out, lhsT, rhs, reads, writes, start=True, stop=True):
        return O('pe', reads, writes, T.matmul, out, lhsT, rhs, start=start, stop=stop)

    def tr(out, in_, idn, reads, writes):
        return O('pe', reads, writes, T.transpose, out, in_, idn)

    def bc(ap, shape):
        return ap.to_broadcast(list(shape))

    def dbgdump(name, ap, shape, reads):
        if name in dbg:
            d = dout("dbg_" + name, shape)
            out_evs.append(sy.dma('pool', reads, [], d, ap))

    sy.dma('sp', [], [t_pp], pp[:, 0:PP_N], pp_d)
    sy.dma('sp', [], [t_const], tokb[:], tokb_d)
    sy.dma('sp', [], [t_const], smallw[:], smallw_d)
    sy.dma('sp', [], [t_const], csq[:], c_sq_d)
    sy.dma('sp', [], [t_const], chm[:], c_hm_d)
    sy.dma('sp', [], [t_const], cseq[:], c_seq_d)

    def load_kind_consts(kd):
        sy.dma('sp', [], [t_kc], ckind[:], c_kind_d[kd])
        sy.dma('sp', [], [t_kc], cdmask[:], c_dmask_d[kd])
        sy.dma('sp', [], [t_kc], cdec[:], c_dec_d[kd])
        sy.dma('sp', [], [t_kc], ccdec[:], c_cdec_d[kd])

    O('dve', [t_pp], [t_pp], V.tensor_scalar, pp[:, PP_OMU:PP_OMU + 14], pp[:, PP_MU:PP_MU + 14], -1.0, 1.0, ALU.mult, ALU.add)
    O('dve', [t_pp], [t_pp], V.tensor_scalar, pp[:, PP_OKA:PP_OKA + 4], pp[:, PP_KA:PP_KA + 4], -1.0, 1.0, ALU.mult, ALU.add)
    for t_, tt in ((Hp, t_H), (Rp, t_R), (prevP, t_prev), (prevc, t_prevc), (Khz, t_Khz), (Bhz, t_Bhz)):
        O('pool', [], [tt], Pl.memset, t_[:], 0.0)

    Wb = {}
    t_Wb = {}
    for n_, shp in WEIGHTS:
        Wb[n_] = nc.dram_tensor("wb_" + n_, list(shp), BF16, kind="Internal").ap()
        t_Wb[n_] = Buf()

    def cast_weight(n_, idx=None):
        src = W[n_] if idx is None else W[n_][idx]
        dst = Wb[n_] if idx is None else Wb[n_][idx]
        rows = src.shape[0]
        for r0 in range(0, rows, 256):
            sy.dma('pool', [], [t_Wb[n_]], dst[r0:r0 + 256, :], src[r0:r0 + 256, :])

    cast_weight('w_in_ab'); cast_weight('w_out_ab'); cast_weight('w_up', 0); cast_weight('w_down', 0)
    cast_weight('w_in_conv'); cast_weight('w_out_conv'); cast_weight('w_up', 1); cast_weight('w_down', 1)

    def dense(Wap, t_W_, K, Ocols, in_fn, in_reads, N, out_cb):
        KC = K // 128
        cols = 4096 // KC
        Wv = Wap.rearrange("(kc p) o -> p kc o", p=128)
        blocks = [(c0, min(cols, Ocols - c0)) for c0 in range(0, Ocols, cols)]
        loaded = {}

        def load(bi):
            c0, wd = blocks[bi]
            ri = wr_ctr[0] % WR_N
            wr_ctr[0] += 1
            wv = wring[ri][:, 0:KC * wd].rearrange("p (kc o) -> p kc o", kc=KC)
            sy.dma('sp', [t_W_], [t_wr[ri]], wv, Wv[:, :, c0:c0 + wd])
            loaded[bi] = (wv, t_wr[ri])

        load(0)
        if len(blocks) > 1:
            load(1)
        for bi, (c0, wd) in enumerate(blocks):
            if bi + 2 < len(blocks):
                load(bi + 2)
            wv, tw = loaded.pop(bi)
            for j in range(wd // 128):
                oc = c0 // 128 + j
                ps, tps = dps()
                for kc in range(KC):
                    mm(ps[:, 0:N], wv[:, kc, j * 128:(j + 1) * 128], in_fn(kc), [tw] + in_reads(kc), [tps],
                       start=(kc == 0), stop=(kc == KC - 1))
                out_cb(oc, ps[:, 0:N], tps)

    LNS = DB[:].rearrange("p a h t -> p (a h t)")[:, 0:8 * GT].rearrange("p (c t) -> p c t", c=8)

    def ln_square(kc, N):
        e = 'pool' if kc % 2 else 'dve'
        O(e, [t_xTc[kc]], [t_lnc[kc], t_Mb[kc // 4]], (Pl if kc % 2 else V).tensor_tensor, LNS[:, kc, 0:N],
          xT[:, kc, 0:N], xT[:, kc, 0:N], ALU.mult)

    def layer_norm(N, goff, boff, sq_scratch):
        ps1, tp1 = nps()
        for kc in range(8):
            mm(ps1[:, 0:N], ones, xT[:, kc, 0:N], [t_xTc[kc], t_const], [tp1], start=(kc == 0), stop=(kc == 7))
        ps2, tp2 = nps()
        for kc in range(8):
            mm(ps2[:, 0:N], ones, LNS[:, kc, 0:N], [t_lnc[kc], t_const], [tp2], start=(kc == 0), stop=(kc == 7))
        mean = statsb[:, 0, 0:N]
        msq = statsb[:, 1, 0:N]
        var = statsb[:, 2, 0:N]
        O('act', [tp1], [t_stat], A.mul, mean, ps1[:, 0:N], 1.0 / D)
        O('dve', [t_stat], [t_stat], V.tensor_tensor, msq, mean, mean, ALU.mult)
        O('dve', [tp2, t_stat], [t_stat], V.scalar_tensor_tensor, var, ps2[:, 0:N], 1.0 / D, msq, ALU.mult, ALU.subtract)
        O('dve', [t_stat], [t_stat], V.tensor_scalar, var, var, LN_EPS, None, ALU.add)
        O('act', [t_stat], [t_stat], A.activation, var, var, AF.Ln)
        O('act', [t_stat], [t_stat], A.activation, var, var, AF.Exp, scale=-0.5)
        O('dve', [t_stat], [t_stat], V.tensor_tensor, msq, mean, var, ALU.mult)
        for kc in range(8):
            O('dve', [t_xTc[kc], t_stat], [t_lnc[kc]], V.tensor_tensor, LNS[:, kc, 0:N], xT[:, kc, 0:N], var, ALU.mult)
        for kc in range(8):
            O('pool', [t_lnc[kc], t_stat], [t_lnc[kc]], Pl.tensor_tensor, LNS[:, kc, 0:N], LNS[:, kc, 0:N], msq, ALU.subtract)
        for kc in range(8):
            O('act', [t_lnc[kc], t_Mb[kc // 4], t_pp], [t_xTc[kc]], A.activation, xT[:, kc, 0:N], LNS[:, kc, 0:N], AF.Identity,
              bias=pp[:, boff + kc:boff + kc + 1], scale=pp[:, goff + kc:goff + kc + 1])
        for kc in range(8):
            O('dve', [t_xTc[kc]], [t_xTbc[kc]], V.tensor_copy, xTb[:, kc, 0:N], xT[:, kc, 0:N])

    def head_norm_core(eps):
        sqv = E[:].rearrange("p c (h n) -> p (c h) n", n=64)
        O('dve', [t_O], [t_st], V.tensor_reduce, st1[:], Oall[:], AX.X, ALU.add)
        O('dve', [t_st], [t_st], V.tensor_scalar, st1[:], st1[:], -1.0 / 64, None, ALU.mult)
        O('dve', [t_O, t_st], [t_O], V.tensor_tensor, Oall[:], Oall[:], bc(st1[:].unsqueeze(2), [128, 8, 64]), ALU.add)
        O('pool', [t_O], [t_E], Pl.tensor_tensor, sqv, Oall[:], Oall[:], ALU.mult)
        O('dve', [t_E], [t_st], V.tensor_reduce, st2[:], sqv, AX.X, ALU.add)
        O('dve', [t_st], [t_st], V.tensor_scalar, st2[:], st2[:], 1.0 / 64, eps, ALU.mult, ALU.add)
        O('act', [t_st], [t_st], A.activation, st2[:], st2[:], AF.Ln)
        O('act', [t_st], [t_st], A.activation, st2[:], st2[:], AF.Exp, scale=-0.5)
        O('dve', [t_O, t_st], [t_O], V.tensor_tensor, Oall[:], Oall[:], bc(st2[:].unsqueeze(2), [128, 8, 64]), ALU.mult)

    def norm_out_T(goff, boff):
        Of = Oall[:].rearrange("p h v -> p (h v)")
        ps, tp = nps()
        for c in range(4):
            tr(ps[:, c * 128:(c + 1) * 128], Of[:, c * 128:(c + 1) * 128], ident, [t_O, t_const], [tp])
        for c in range(4):
            O('act', [tp, t_pp], [t_T1], A.activation, T1[:, c, :], ps[:, c * 128:(c + 1) * 128], AF.Identity,
              bias=pp[:, boff + c:boff + c + 1], scale=pp[:, goff + c:goff + c + 1])

    Xt = DBf[:, 0:1024].rearrange("p (s v) -> p s v", v=64)

    def cross_select(lhsT, t_l, St, t_S, c, dst, t_dst):
        for j in range(2):
            ps, tp = nps()
            mm(ps[:], lhsT, St[:, c, 8 * j:8 * j + 8, :].rearrange("p s v -> p (s v)"), [t_l, t_S], [tp])
            O('dve', [tp, t_const], [t_Mb[0]], V.tensor_tensor, Xt[:, 8 * j:8 * j + 8, :],
              ps[:].rearrange("p (s v) -> p s v", v=64),
              bc(cseq[:, 8 * j:8 * j + 8].unsqueeze(2), [128, 8, 64]), ALU.mult)
        O('dve', [t_Mb[0]], [t_dst], V.tensor_reduce, dst, Xt.rearrange("p s v -> p v s"), AX.X, ALU.add)

    def expand_seq(src, t_src, i, eng):
        dst = DBf[:, i * 1024:(i + 1) * 1024].rearrange("p (s v) -> p s v", v=64)
        fn = V.tensor_tensor if eng == 'dve' else Pl.tensor_tensor
        O(eng, [t_src, t_const], [t_Mb[i]], fn, dst, bc(src.unsqueeze(1), [128, 16, 64]),
          bc(cseq[:].unsqueeze(2), [128, 16, 64]), ALU.mult)

    def exflat(i, j):
        return DBf[:, i * 1024 + j * 512:i * 1024 + (j + 1) * 512]

    def rwkv_tile(kind, gtile, ti, big, is_last_prompt):
        nseq = 1 if kind == 0 else 16
        C = 128 // nseq
        nlev = int(math.log2(C))
        cs = slice(ti * 128, (ti + 1) * 128)
        PR = big[:, 0:14, cs]
        PR4 = PR.rearrange("p j (s c) -> p j s c", s=nseq)
        Mx4 = Mx[:].rearrange("p j (s c) -> p j s c", s=nseq)
        St, t_S = (Hp, t_H) if kind == 0 else (Hs, t_H)
        triI, triS, mTs, mTi, mLs = (ckind[:, i, :] for i in range(5))
        mu_b = pp[:, PP_MU:PP_MU + 14]
        omu_b = pp[:, PP_OMU:PP_OMU + 14]
        O('act', [t_big], [t_plast], A.copy, plast[:, :, 0:nseq], PR4[:, :, :, C - 1])
        O('dve', [t_big, t_pp], [t_Mx], V.tensor_tensor, Mx4[:, :, :, 1:C], PR4[:, :, :, 0:C - 1],
          bc(mu_b.unsqueeze(2).unsqueeze(3), [128, 14, nseq, C - 1]), ALU.mult)
        O('pool', [t_prev, t_pp], [t_Mx], Pl.tensor_tensor, Mx4[:, :, :, 0], prevP[:, :, 0:nseq],
          bc(mu_b.unsqueeze(2), [128, 14, nseq]), ALU.mult)
        O('pool', [t_big, t_pp], [t_big], Pl.tensor_tensor, PR, PR, bc(omu_b.unsqueeze(2), [128, 14, 128]), ALU.mult)
        O('dve', [t_big, t_Mx], [t_big], V.tensor_tensor, PR, PR, Mx[:], ALU.add)
        if kind == 0:
            O('act', [t_plast], [t_prev], A.copy, prevP[:, :, 0:1], plast[:, :, 0:1])
        if kind == 1 or is_last_prompt:
            row0 = 0 if kind == 0 else 1
            for (jlo, jhi) in ((0, 8), (8, 14)):
                for j0 in range(jlo, jhi, 4):
                    ps, tp = nps()
                    nj = min(4, jhi - j0)
                    for j in range(nj):
                        tr(ps[0:nseq, j * 128:(j + 1) * 128], plast[:, j0 + j, 0:nseq], ident, [t_plast, t_const], [tp])
                    O('act', [tp], [t_stage], A.copy, stage[0:nseq, (j0 - jlo) * 128:(j0 - jlo + nj) * 128], ps[0:nseq, 0:nj * 128])
                out_evs.append(sy.dma('pool', [t_stage], [], shift_out[row0:row0 + nseq, jlo * 128:jhi * 128],
                                      stage[0:nseq, 0:(jhi - jlo) * 128]))
        Mr = big[:, 0:4, cs]
        Mk = big[:, 4:8, cs]
        Mv = big[:, 8:12, cs]
        M12 = big[:, 12, cs]
        M13 = big[:, 13, cs]
        if gtile in (0, 16):
            dbgdump(f"M{gtile}", PR, [128, 14, 128], [t_big])
        O('act', [t_big], [t_G], A.activation, G[0:64, :], M12[0:64, :], AF.Tanh)
        O('pool', [t_big], [t_G], Pl.tensor_copy, G[64:128, :], M12[64:128, :])
        O('act', [t_big], [t_sg], A.activation, sg[:], M13, AF.Sigmoid)
        ps, tp = nps()
        mm(ps[:], G[:], smallw[:, 0, :], [t_G, t_const], [tp])
        O('dve', [tp, t_const], [t_z], V.tensor_tensor, ztok[:], ps[:], tokb[:], ALU.add)
        O('act', [t_z], [t_z], A.activation, ztok[:], ztok[:], AF.Sigmoid)
        ps, tp = nps()
        for c in range(4):
            mm(ps[:, c * 128:(c + 1) * 128], ztok[:, c * 128:(c + 1) * 128], triI, [t_z, t_kc], [tp])
        O('act', [tp], [t_cum], A.copy, cum[:], ps[:].rearrange("p (c t) -> p c t", c=4))
        ps, tp = nps()
        for c in range(4):
            mm(ps[:, c * 128:(c + 1) * 128], ztok[:, c * 128:(c + 1) * 128], triS, [t_z, t_kc], [tp])
        O('dve', [tp], [t_cumx], V.tensor_copy, cumx[:], ps[:].rearrange("p (c t) -> p c t", c=4))
        ps, tp = nps()
        for c in range(4):
            mm(ps[:, c * 128:(c + 1) * 128], smallw[:, 1, c * 128:(c + 1) * 128], G[:], [t_G, t_const], [tp])
        for c in range(4):
            O('act', [tp, t_pp], [t_a], A.activation, a_t[:, c, :], ps[:, c * 128:(c + 1) * 128], AF.Sigmoid,
              bias=pp[:, PP_A0 + c:PP_A0 + c + 1], scale=1.0)
        ps, tp = nps()
        for c in range(4):
            mm(ps[:, c * 128:(c + 1) * 128], smallw[:, 2, c * 128:(c + 1) * 128], sg[:], [t_sg, t_const], [tp])
        O('act', [tp], [t_g], A.copy, gT[:], ps[:].rearrange("p (c t) -> p c t", c=4))
        b4 = lambda off: bc(pp[:, off:off + 4].unsqueeze(2), [128, 4, 128])
        f2 = lambda t: t[:].rearrange("p c t -> p (c t)")
        O('pool', [t_big, t_pp], [t_kk], Pl.tensor_tensor, kk[:], Mk, b4(PP_KK), ALU.mult)
        O('pool', [t_a, t_pp], [t_T1], Pl.tensor_tensor, T1[:], a_t[:], b4(PP_KA), ALU.mult)
        O('pool', [t_T1, t_pp], [t_T1], Pl.tensor_tensor, T1[:], T1[:], b4(PP_OKA), ALU.add)
        O('dve', [t_big, t_T1], [t_kp], V.tensor_tensor, kp[:], Mk, T1[:], ALU.mult)
        O('pool', [t_kk], [t_T1], Pl.tensor_tensor, T1[:], kk[:], kk[:], ALU.mult)
        ps, tp = nps()
        mm(ps[:], blkones, f2(T1), [t_T1, t_const], [tp])
        O('dve', [tp], [t_T1], V.tensor_scalar, f2(T1), ps[:], 1e-24, None, ALU.max)
        O('act', [t_T1], [t_T1], A.activation, f2(T1), f2(T1), AF.Ln)
        O('act', [t_T1], [t_T1], A.activation, f2(T1), f2(T1), AF.Exp, scale=-0.5)
        O('dve', [t_kk, t_T1], [t_kk], V.tensor_tensor, kk[:], kk[:], T1[:], ALU.mult)
        O('pool', [t_kk, t_a], [t_beta], Pl.tensor_tensor, beta[:], kk[:], a_t[:], ALU.mult)
        O('act', [t_cumx], [t_E], A.activation, E[:], cumx[:], AF.Exp)
        O('dve', [t_kk, t_E], [t_T1], V.tensor_tensor, T1[:], kk[:], E[:], ALU.mult)
        for hp in range(2):
            O('act' if hp else 'dve', [t_T1, t_const], [t_Az], *((A.activation, Az[:, :, hp, :], T1[:], AF.Identity) if hp else (V.tensor_scalar, Az[:, :, hp, :], T1[:], chm[:, 2 + hp:3 + hp], None, ALU.mult)), **(dict(scale=chm[:, 2 + hp:3 + hp]) if hp else dict()))
        O('act', [t_cum], [t_E], A.activation, E[:], cum[:], AF.Exp, scale=-1.0)
        O('dve', [t_beta, t_E], [t_Bt], V.tensor_tensor, Bt[:], beta[:], E[:], ALU.mult)
        O('pool', [t_kp, t_E], [t_Kt], Pl.tensor_tensor, Kt[:], kp[:], E[:], ALU.mult)
        O('act', [t_cum], [t_E], A.activation, E[:], cum[:], AF.Exp)
        O('dve', [t_big, t_E], [t_T1], V.tensor_tensor, T1[:], Mr, E[:], ALU.mult)
        for hp in range(2):
            O('act' if hp else 'dve', [t_T1, t_const], [t_Rz], *((A.activation, Rz[:, :, hp, :], T1[:], AF.Identity) if hp else (V.tensor_scalar, Rz[:, :, hp, :], T1[:], chm[:, hp:hp + 1], None, ALU.mult)), **(dict(scale=chm[:, hp:hp + 1]) if hp else dict()))
        cum4 = cum[:].rearrange("p c (s t) -> p c s t", s=nseq)
        E4 = E[:].rearrange("p c (s t) -> p c s t", s=nseq)
        O('dve', [t_cum], [t_E], V.tensor_tensor, E4, bc(cum4[:, :, :, C - 1:C], [128, 4, nseq, C]), cum4, ALU.subtract)
        O('act', [t_E], [t_E], A.activation, E[:], E[:], AF.Exp)
        O('act', [t_cum], [t_gC], A.activation, gC[:, :, 0:nseq], cum4[:, :, :, C - 1], AF.Exp)
        O('pool', [t_kp, t_E], [t_Mx], Pl.tensor_tensor, Mx[:, 0:4, :], kp[:], E[:], ALU.mult)
        O('dve', [t_beta, t_E], [t_Mx], V.tensor_tensor, Mx[:, 4:8, :], beta[:], E[:], ALU.mult)
        O('pool', [t_big, t_kp], [t_Mx], Pl.tensor_tensor, Mx[:, 8:12, :], Mr, kp[:], ALU.mult)
        O('pool', [t_Mx, t_pp], [t_Mx], Pl.tensor_tensor, Mx[:, 8:12, :], Mx[:, 8:12, :], b4(PP_RK), ALU.mult)
        for (src_lo, dstz, t_d) in ((0, Khz, t_Khz), (4, Bhz, t_Bhz)):
            ps, tp = nps()
            for c in range(4):
                tr(ps[:, c * 128:(c + 1) * 128], Mx[:, src_lo + c, :], ident, [t_Mx, t_const], [tp])
            psv = ps[:].rearrange("p (c f) -> p c f", c=4)
            for hp in range(2):
                e = 'act' if hp == 0 else 'dve'
                fn = A.copy if hp == 0 else V.tensor_copy
                O(e, [tp], [t_d], fn, dstz[:, :, hp, hp * 64:(hp + 1) * 64], psv[:, :, hp * 64:(hp + 1) * 64])
        ps, tp = nps()
        for c in range(4):
            tr(ps[:, c * 128:(c + 1) * 128], big[:, 8 + c, cs], ident, [t_big, t_const], [tp])
        O('act', [tp], [t_V], A.copy, Vtok[:], ps[:])
        ps, tp = nps()
        mm(ps[:], blkones, Mx[:, 8:12, :].rearrange("p c t -> p (c t)"), [t_Mx, t_const], [tp])
        O('dve', [tp, t_big], [t_Mx], V.tensor_tensor, Mx[:, 8:12, :], ps[:].rearrange("p (c t) -> p c t", c=4), Mv, ALU.mult)
        Mb = [DB[:, 0], DB[:, 1]]
        MTb = [DB[:, 2], DB[:, 3]]
        PT = DB[:, 4]
        LakT, NrbT, NrkT = LK[:, 0], LK[:, 1], LK[:, 2]
        v4 = lambda ps_: ps_[:].rearrange("p (h t) -> p h t", h=4)
        mk = lambda m: bc(m.unsqueeze(1), [128, 4, 128])
        hd = lambda hh: [(2 * hh + (hl // 2), hl % 2, 4 * hh + hl, hl) for hl in range(4)]
        for hh in range(2):
            hs = slice(4 * hh, 4 * hh + 4)
            psA, tpA = nps()
            psB, tpB = nps()
            for (c, hp, h, hl) in hd(hh):
                mm(psA[:, hl * 128:(hl + 1) * 128], Bt[:, c, :], Az[:, c, hp, :], [t_Bt, t_Az], [tpA])
            for (c, hp, h, hl) in hd(hh):
                mm(psB[:, hl * 128:(hl + 1) * 128], Az[:, c, hp, :], Bt[:, c, :], [t_Bt, t_Az], [tpB])
            O('dve', [tpA, t_kc], [t_MTb[0]], V.tensor_tensor, MTb[0][:, hs, :], v4(psA), mk(mTs), ALU.mult)
            O('dve', [tpB, t_kc], [t_Mb[0]], V.tensor_tensor, Mb[0][:, hs, :], v4(psB), mk(mLs), ALU.mult)
            O('pool', [t_MTb[0], t_const], [t_PT], Pl.tensor_tensor, PT[:, hs, :], MTb[0][:, hs, :], mk(ident), ALU.add)
        cur = 0
        for lev in range(1, nlev):
            nxt = 1 - cur
            need_mt = lev < nlev - 1
            pM = [nps(), nps()]
            for h in range(8):
                mm(pM[h // 4][0][:, (h % 4) * 128:(h % 4 + 1) * 128], MTb[cur][:, h, :], Mb[cur][:, h, :],
                   [t_MTb[cur], t_Mb[cur]], [pM[h // 4][1]])
            if need_mt:
                pMT = [nps(), nps()]
                for h in range(8):
                    mm(pMT[h // 4][0][:, (h % 4) * 128:(h % 4 + 1) * 128], Mb[cur][:, h, :], MTb[cur][:, h, :],
                       [t_MTb[cur], t_Mb[cur]], [pMT[h // 4][1]])
            O('act', [pM[0][1]], [t_Mb[nxt]], A.copy, Mb[nxt][:, 0:4, :], v4(pM[0][0]))
            O('dve', [pM[1][1]], [t_Mb[nxt]], V.tensor_copy, Mb[nxt][:, 4:8, :], v4(pM[1][0]))
            pP = [nps(), nps()]
            for h in range(8):
                mm(pP[h // 4][0][:, (h % 4) * 128:(h % 4 + 1) * 128], Mb[nxt][:, h, :], PT[:, h, :],
                   [t_Mb[nxt], t_PT], [pP[h // 4][1]])
            if need_mt:
                O('act', [pMT[0][1]], [t_MTb[nxt]], A.copy, MTb[nxt][:, 0:4, :], v4(pMT[0][0]))
                O('dve', [pMT[1][1]], [t_MTb[nxt]], V.tensor_copy, MTb[nxt][:, 4:8, :], v4(pMT[1][0]))
            for q in range(2):
                O('dve', [pP[q][1], t_PT], [t_PT], V.tensor_tensor, PT[:, 4 * q:4 * q + 4, :], v4(pP[q][0]),
                  PT[:, 4 * q:4 * q + 4, :], ALU.add)
            cur = nxt
        for hh in range(2):
            heads = hd(hh)
            prods = [(Kt, t_Kt, Az, t_Az, LakT, mTs), (Bt, t_Bt, Rz, t_Rz, NrbT, mTi), (Kt, t_Kt, Rz, t_Rz, NrkT, mTi)]
            pss = []
            for (L_, tl, R_, tr_, dst_, msk_) in prods:
                ps, tp = nps()
                for (c, hp, h, hl) in heads:
                    mm(ps[:, hl * 128:(hl + 1) * 128], L_[:, c, :], R_[:, c, hp, :], [tl, tr_], [tp])
                pss.append((ps, tp))
            for (ps, tp), (L_, tl, R_, tr_, dst_, msk_) in zip(pss, prods):
                O('dve', [tp, t_kc], [t_LK], V.tensor_tensor, dst_, v4(ps), mk(msk_), ALU.mult)
            Wv = Wt[:].rearrange("p h v -> p (h v)")
            Xcf = Xc[:].rearrange("p h v -> p (h v)")
            if kind == 1:
                for (c, hp, h, hl) in heads:
                    cross_select(Az[:, c, hp, :], t_Az, St, t_S, c, Xc[:, hl, :], t_Xc)
            psW, tpW = nps()
            for (c, hp, h, hl) in heads:
                o_ = psW[:, hl * 64:(hl + 1) * 64]
                if kind == 0:
                    mm(o_, Az[:, c, hp, :], St[:, c, :], [t_Az, t_S], [tpW], start=True, stop=False)
                mm(o_, LakT[:, hl, :], Vtok[:, h * 64:(h + 1) * 64], [t_LK, t_V], [tpW], start=(kind == 1), stop=True)
            if kind == 0:
                O('act', [tpW], [t_W], A.copy, Wv, psW[:, 0:256])
            else:
                O('dve', [tpW, t_Xc], [t_W], V.tensor_tensor, Wv, psW[:, 0:256], Xcf, ALU.add)
            psU, tpU = nps()
            for (c, hp, h, hl) in heads:
                mm(psU[:, hl * 64:(hl + 1) * 64], PT[:, h, :], Wt[:, hl, :], [t_PT, t_W], [tpU])
            O('act', [tpU], [t_U], A.copy, Uall[:, 4 * hh:4 * hh + 4, :].rearrange("p h v -> p (h v)"), psU[:, 0:256])
            if kind == 1:
                for (c, hp, h, hl) in heads:
                    cross_select(Rz[:, c, hp, :], t_Rz, St, t_S, c, Xc[:, hl, :], t_Xc)
            psO, tpO = nps()
            for (c, hp, h, hl) in heads:
                o_ = psO[:, hl * 64:(hl + 1) * 64]
                if kind == 0:
                    mm(o_, Rz[:, c, hp, :], St[:, c, :], [t_Rz, t_S], [tpO], start=True, stop=False)
                mm(o_, NrbT[:, hl, :], Uall[:, h, :], [t_LK, t_U], [tpO], start=(kind == 1), stop=False)
                mm(o_, NrkT[:, hl, :], Vtok[:, h * 64:(h + 1) * 64], [t_LK, t_V], [tpO], start=False, stop=True)
            Ov = Oall[:, 4 * hh:4 * hh + 4, :].rearrange("p h v -> p (h v)")
            if kind == 0:
                O('act', [tpO], [t_O], A.copy, Ov, psO[:, 0:256])
            else:
                O('dve', [tpO, t_Xc], [t_O], V.tensor_tensor, Ov, psO[:, 0:256], Xcf, ALU.add)
            for cc in range(2):
                c = 2 * hh + cc
                if kind == 0:
                    psH, tpH = nps()
                    for hp in range(2):
                        h = 2 * c + hp
                        mm(psH[:, 0:64], Bhz[:, c, hp, :], Uall[:, h, :], [t_Bhz, t_U], [tpH], start=(hp == 0), stop=False)
                        mm(psH[:, 0:64], Khz[:, c, hp, :], Vtok[:, h * 64:(h + 1) * 64], [t_Khz, t_V], [tpH],
                           start=False, stop=(hp == 1))
                    O('dve', [t_S, t_gC, tpH], [t_S], V.scalar_tensor_tensor, St[:, c, :], St[:, c, :], gC[:, c, 0:1],
                      psH[:, 0:64], ALU.mult, ALU.add)
                else:
                    phs = [nps(), nps()]
                    for hp in range(2):
                        h = 2 * c + hp
                        expand_seq(Uall[:, h, :], t_U, 0, 'pool')
                        expand_seq(Vtok[:, h * 64:(h + 1) * 64], t_V, 1, 'dve')
                        for j in range(2):
                            psH, tpH = phs[j]
                            mm(psH[:], Bhz[:, c, hp, :], exflat(0, j), [t_Bhz, t_Mb[0]], [tpH], start=(hp == 0), stop=False)
                            mm(psH[:], Khz[:, c, hp, :], exflat(1, j), [t_Khz, t_Mb[1]], [tpH], start=False, stop=(hp == 1))
                    O('pool', [t_S, t_gC], [t_S], Pl.tensor_tensor, St[:, c], St[:, c],
                      bc(gC[:, c, :].unsqueeze(2), [128, 16, 64]), ALU.mult)
                    for j in range(2):
                        psH, tpH = phs[j]
                        O('dve', [t_S, tpH], [t_S], V.tensor_tensor, St[:, c, 8 * j:8 * j + 8, :], St[:, c, 8 * j:8 * j + 8, :],
                          psH[:].rearrange("p (s v) -> p s v", v=64), ALU.add)
        if gtile in (0, 1, 16):
            dbgdump(f"Oall{gtile}", Oall[:], [128, 8, 64], [t_O])
        head_norm_core(A_GN_EPS)
        norm_out_T(PP_LXG, PP_LXB)
        O('pool', [t_T1, t_Mx], [t_T1], Pl.tensor_tensor, T1[:], T1[:], Mx[:, 8:12, :], ALU.add)
        O('dve', [t_T1, t_g], [t_yT], V.tensor_tensor, yT[:, 0:4, cs], T1[:], gT[:], ALU.mult)

    def ret_tile(kind, gtile, ti, big):
        cs = slice(ti * 128, (ti + 1) * 128)
        St, t_S = (Rp, t_R) if kind == 0 else (Rs, t_R)
        Pq = big[:, 14:18, cs]
        Pk = big[:, 18:22, cs]
        Pg = big[:, 26:30, cs]
        sy.dma('sp', [], [t_rope], rope[:], c_rope_d[gtile])
        Qr, t_Qr = Bt, t_Bt
        Kr, t_Kr = Kt, t_Kt
        Qrz, t_Qrz = Az, t_Az
        Qdz, t_Qdz = Rz, t_Rz
        Kdz, t_Kdz = Khz, t_Khz
        for (src, dst, t_d, ci) in ((Pq, Qr, t_Qr, 0), (Pk, Kr, t_Kr, 2)):
            ps, tp = nps()
            for c in range(4):
                mm(ps[:, c * 128:(c + 1) * 128], rotT, src[:, c, :], [t_big, t_const], [tp])
            O('pool', [t_big, t_rope], [t_T1], Pl.tensor_tensor, T1[:], src, bc(rope[:, ci, :].unsqueeze(1), [128, 4, 128]), ALU.mult)
            O('dve', [tp, t_rope], [t_E], V.tensor_tensor, E[:], ps[:].rearrange("p (c t) -> p c t", c=4),
              bc(rope[:, ci + 1, :].unsqueeze(1), [128, 4, 128]), ALU.mult)
            O('pool', [t_T1, t_E], [t_d], Pl.tensor_tensor, dst[:], T1[:], E[:], ALU.add)
        for hp in range(2):
            O('act' if hp else 'dve', [t_Qr, t_const], [t_Qrz], *((A.activation, Qrz[:, :, hp, :], Qr[:], AF.Identity) if hp else (V.tensor_scalar, Qrz[:, :, hp, :], Qr[:], chm[:, hp:hp + 1], None, ALU.mult)), **(dict(scale=chm[:, hp:hp + 1]) if hp else dict()))
        O('pool', [t_Qr, t_kc], [t_T1], Pl.tensor_tensor, T1[:], Qr[:], cdec[:, 0], ALU.mult)
        for hp in range(2):
            O('act' if hp else 'dve', [t_T1, t_const], [t_Qdz], *((A.activation, Qdz[:, :, hp, :], T1[:], AF.Identity) if hp else (V.tensor_scalar, Qdz[:, :, hp, :], T1[:], chm[:, hp:hp + 1], None, ALU.mult)), **(dict(scale=chm[:, hp:hp + 1]) if hp else dict()))
        O('pool', [t_Kr, t_kc], [t_Mx], Pl.tensor_tensor, Mx[:, 0:4, :], Kr[:], cdec[:, 1], ALU.mult)
        ps, tp = nps()
        for c in range(4):
            tr(ps[:, c * 128:(c + 1) * 128], Mx[:, c, :], ident, [t_Mx, t_const], [tp])
        psv = ps[:].rearrange("p (c f) -> p c f", c=4)
        for hp in range(2):
            e = 'act' if hp == 0 else 'dve'
            fn = A.copy if hp == 0 else V.tensor_copy
            O(e, [tp], [t_Kdz], fn, Kdz[:, :, hp, hp * 64:(hp + 1) * 64], psv[:, :, hp * 64:(hp + 1) * 64])
        ps, tp = nps()
        for c in range(4):
            tr(ps[:, c * 128:(c + 1) * 128], big[:, 22 + c, cs], ident, [t_big, t_const], [tp])
        O('act', [tp], [t_V], A.copy, Vtok[:], ps[:])
        O('act', [t_big], [t_Mx], A.activation, Mx[:, 4:8, :], Pg, AF.Sigmoid)
        O('dve', [t_big, t_Mx], [t_Mx], V.tensor_tensor, Mx[:, 4:8, :], Mx[:, 4:8, :], Pg, ALU.mult)
        scT = LK[:, 0]
        Xcf = Xc[:].rearrange("p h v -> p (h v)")
        for hh in range(2):
            heads = [(2 * hh + (hl // 2), hl % 2, 4 * hh + hl, hl) for hl in range(4)]
            ps, tp = nps()
            for (c, hp, h, hl) in heads:
                mm(ps[:, hl * 128:(hl + 1) * 128], Kr[:, c, :], Qrz[:, c, hp, :], [t_Kr, t_Qrz], [tp])
            O('dve', [tp, t_kc], [t_LK], V.tensor_tensor, scT, ps[:].rearrange("p (h t) -> p h t", h=4),
              cdmask[:, 4 * hh:4 * hh + 4, :], ALU.mult)
            if kind == 1:
                for (c, hp, h, hl) in heads:
                    cross_select(Qdz[:, c, hp, :], t_Qdz, St, t_S, c, Xc[:, hl, :], t_Xc)
            psO, tpO = nps()
            for (c, hp, h, hl) in heads:
                o_ = psO[:, hl * 64:(hl + 1) * 64]
                if kind == 0:
                    mm(o_, Qdz[:, c, hp, :], St[:, c, :], [t_Qdz, t_S], [tpO], start=True, stop=False)
                mm(o_, scT[:, hl, :], Vtok[:, h * 64:(h + 1) * 64], [t_LK, t_V], [tpO], start=(kind == 1), stop=True)
            Ov = Oall[:, 4 * hh:4 * hh + 4, :].rearrange("p h v -> p (h v)")
            if kind == 0:
                O('act', [tpO], [t_O], A.copy, Ov, psO[:, 0:256])
            else:
                O('dve', [tpO, t_Xc], [t_O], V.tensor_tensor, Ov, psO[:, 0:256], Xcf, ALU.add)
            for cc in range(2):
                c = 2 * hh + cc
                if kind == 0:
                    psH, tpH = nps()
                    for hp in range(2):
                        h = 2 * c + hp
                        mm(psH[:, 0:64], Kdz[:, c, hp, :], Vtok[:, h * 64:(h + 1) * 64], [t_Kdz, t_V], [tpH],
                           start=(hp == 0), stop=(hp == 1))
                    O('dve', [t_S, t_kc, tpH], [t_S], V.scalar_tensor_tensor, St[:, c, :], St[:, c, :],
                      ccdec[:, c:c + 1], psH[:, 0:64], ALU.mult, ALU.add)
                else:
                    phs = [nps(), nps()]
                    for hp in range(2):
                        h = 2 * c + hp
                        expand_seq(Vtok[:, h * 64:(h + 1) * 64], t_V, hp, 'dve' if hp else 'pool')
                        for j in range(2):
                            psH, tpH = phs[j]
                            mm(psH[:], Kdz[:, c, hp, :], exflat(hp, j), [t_Kdz, t_Mb[hp]], [tpH], start=(hp == 0), stop=(hp == 1))
                    O('dve', [t_S, t_kc], [t_S], V.tensor_scalar, St[:, c], St[:, c], ccdec[:, c:c + 1], None, ALU.mult)
                    for j in range(2):
                        psH, tpH = phs[j]
                        O('dve', [t_S, tpH], [t_S], V.tensor_tensor, St[:, c, 8 * j:8 * j + 8, :], St[:, c, 8 * j:8 * j + 8, :],
                          psH[:].rearrange("p (s v) -> p s v", v=64), ALU.add)
        if gtile in (0, 1, 16):
            dbgdump(f"Oret{gtile}", Oall[:], [128, 8, 64], [t_O])
        head_norm_core(B_GN_EPS)
        norm_out_T(PP_GNG, PP_GNB)
        O('dve', [t_T1, t_Mx], [t_yT], V.tensor_tensor, yT[:, 4:8, cs], T1[:], Mx[:, 4:8, :], ALU.mult)

    def state_out(kind):
        row0 = 0 if kind == 0 else 1
        nseq = 1 if kind == 0 else 16
        Hst = Hp if kind == 0 else Hs
        Rst = Rp if kind == 0 else Rs
        for c in range(4):
            for hp in range(2):
                h = 2 * c + hp
                if kind == 0:
                    out_evs.append(sy.dma('pool', [t_R], [], ret_out[0, h], Rst[hp * 64:(hp + 1) * 64, c, :]))
                else:
                    out_evs.append(sy.dma('pool', [t_R], [], ret_out[1:17, h].rearrange("s k v -> k s v"),
                                          Rst[hp * 64:(hp + 1) * 64, c, :, :]))
        for s0 in range(0, nseq, 2):
            ns = min(2, nseq - s0)
            pa = [nps(), nps()]
            for si in range(ns):
                for c in range(4):
                    src = Hst[:, c, :] if kind == 0 else Hst[:, c, s0 + si, :]
                    tr(pa[si][0][0:64, c * 128:(c + 1) * 128], src, ident, [t_H, t_const], [pa[si][1]])
            O('act', [pa[0][1]], [t_stage], A.copy, stage[0:64, 0:512], pa[0][0][0:64, :])
            if ns == 2:
                O('dve', [pa[1][1]], [t_stage], V.tensor_copy, stage[0:64, 512:1024], pa[1][0][0:64, :])
            for si in range(ns):
                out_evs.append(sy.dma('pool', [t_stage], [], wkv_out[row0 + s0 + si].rearrange("h v k -> v h k"),
                                      stage[0:64, si * 512:(si + 1) * 512].rearrange("v (h k) -> v h k", k=64)))

    def load_sample_states():
        sy.dma('sp', [], [t_stage], stage[0:16, 0:1024], st_shift[:, 0:1024])
        for j0 in (0, 4):
            ps, tp = nps()
            for j in range(4):
                tr(ps[:, j * 16:(j + 1) * 16], stage[0:16, (j0 + j) * 128:(j0 + j + 1) * 128], ident[0:16, 0:16], [t_stage, t_const], [tp])
            O('act', [tp], [t_prev], A.copy, prevP[:, j0:j0 + 4, :], ps[:, 0:64].rearrange("p (j s) -> p j s", s=16))
        sy.dma('sp', [], [t_stage], stage[0:16, 0:768], st_shift[:, 1024:1792])
        ps, tp = nps()
        for j in range(6):
            tr(ps[:, j * 16:(j + 1) * 16], stage[0:16, j * 128:(j + 1) * 128], ident[0:16, 0:16], [t_stage, t_const], [tp])
        O('act', [tp], [t_prev], A.copy, prevP[:, 8:14, :], ps[:, 0:96].rearrange("p (j s) -> p j s", s=16))
        sy.dma('sp', [], [t_stage], stage[0:32, :], st_conv)
        ps, tp = nps()
        for kc in range(8):
            tr(ps[:, kc * 32:(kc + 1) * 32], stage[0:32, kc * 128:(kc + 1) * 128], ident[0:32, 0:32], [t_stage, t_const], [tp])
        O('act', [tp], [t_prevc], A.copy, prevc[:].rearrange("p c s j -> p c (s j)"), ps[:, 0:256].rearrange("p (c r) -> p c r", c=8))
        for c in range(4):
            for hp in range(2):
                sy.dma('sp', [], [t_R], Rs[hp * 64:(hp + 1) * 64, c, :, :], st_ret[:, 2 * c + hp].rearrange("s k v -> k s v"))
        for s0 in range(0, 16, 2):
            sy.dma('sp', [], [t_stage], stage[0:64, :].rearrange("v (s h k) -> v s h k", s=2, h=8),
                   st_wkv[s0:s0 + 2].rearrange("s h v k -> v s h k"))
            for si in range(2):
                ps, tp = nps()
                for c in range(4):
                    tr(ps[:, c * 64:(c + 1) * 64], stage[0:64, si * 512 + c * 128:si * 512 + (c + 1) * 128], ident[0:64, 0:64],
                       [t_stage, t_const], [tp])
                O('act' if si == 0 else 'dve', [tp], [t_H], A.copy if si == 0 else V.tensor_copy, Hs[:, :, s0 + si, :],
                  ps[:, 0:256].rearrange("p (c v) -> p c v", c=4))

    groups = []
    t0 = 0
    while t0 < SEQ:
        groups.append((0, t0, GT))
        t0 += GT
    groups.append((1, SEQ, 128))
    if ngroups_limit is not None:
        groups = groups[:ngroups_limit]
    if only_sample:
        groups = groups[-1:]
    cur_kind = None

    for gi_, (kind, t0, N) in enumerate(groups):
        ntile = N // 128
        nseq = 1 if kind == 0 else 16
        C = N // nseq
        if kind != cur_kind:
            if kind == 1:
                sy.barrier()
            load_kind_consts(kind)
            cur_kind = kind
        if kind == 0:
            big = bigflat[:].rearrange("p (s t) -> p s t", t=GT)
            hT = bigbf[:, 0:32 * GT].rearrange("p (s t) -> p s t", t=GT)
        else:
            big = bigflat[:, 0:NSLOT * 128].rearrange("p (s t) -> p s t", t=128)
            hT = bigbf[:, 0:32 * 128].rearrange("p (s t) -> p s t", t=128)
            load_sample_states()
        for ti in range(ntile):
            sy.dma('sp', [], [t_xtok], xtok[:], xin[t0 + ti * 128:t0 + (ti + 1) * 128, :])
            for hb in range(2):
                ps, tp = nps()
                for c in range(4):
                    tr(ps[:, c * 128:(c + 1) * 128], xtok[:, (hb * 4 + c) * 128:(hb * 4 + c + 1) * 128], ident, [t_xtok, t_const], [tp])
                O('act' if hb == 0 else 'dve', [tp], t_xTc[hb * 4:hb * 4 + 4], A.copy if hb == 0 else V.tensor_copy,
                  xT[:, hb * 4:hb * 4 + 4, ti * 128:(ti + 1) * 128], ps[:].rearrange("p (c t) -> p c t", c=4))

        O('act', t_xTc, t_xTbc, A.copy, xTb[:, :, 0:N], xT[:, :, 0:N])

        def cb_proj(oc, ps, tps):
            e = 'act' if oc % 2 == 0 else 'dve'
            O(e, [tps], [t_big], A.copy if e == 'act' else V.tensor_copy, big[:, oc, 0:N], ps)

        def cb_res(oc, ps, tps):
            O('dve', [t_xTc[oc], tps], [t_xTc[oc]], V.scalar_tensor_tensor, xT[:, oc, 0:N], xT[:, oc, 0:N], ALPHA, ps, ALU.mult, ALU.add)
            ln_square(oc, N)

        def cb_up(oc, ps, tps):
            sc_ = big[:, 16 + oc % 8, 0:N]
            O('act', [tps], [t_big], A.activation, sc_, ps, AF.Relu)
            O('pool' if oc % 2 else 'dve', [t_big], [t_big], (Pl if oc % 2 else V).tensor_tensor, hT[:, oc, 0:N], sc_, sc_, ALU.mult)

        def mlp(l):
            dense(Wb['w_up'][l], t_Wb['w_up'], D, 4 * D, lambda kc: xTb[:, kc, 0:N], lambda kc: [t_xTbc[kc]], N, cb_up)
            dense(Wb['w_down'][l], t_Wb['w_down'], 4 * D, D, lambda kc: hT[:, kc, 0:N], lambda kc: [t_big], N, cb_res)
            layer_norm(N, PP_LN + 32 + 8 * l, PP_LN + 48 + 8 * l, big)

        dense(Wb['w_in_ab'], t_Wb['w_in_ab'], D, 3840, lambda kc: xTb[:, kc, 0:N], lambda kc: [t_xTbc[kc]], N, cb_proj)
        if gi_ == 0:
            dbgdump("proj", big[:, 0:30, 0:N], [128, 30, N], [t_big])
        for ti in range(ntile):
            gtile = t0 // 128 + ti
            rwkv_tile(kind, gtile, ti, big, is_last_prompt=(kind == 0 and gtile == SEQ // 128 - 1))
            ret_tile(kind, gtile, ti, big)
        if kind == 1 or (t0 + N == SEQ):
            state_out(kind)
        dense(Wb['w_out_ab'], t_Wb['w_out_ab'], D, D, lambda kc: yT[:, kc, 0:N], lambda kc: [t_yT], N, cb_res)
        layer_norm(N, PP_LN + 0, PP_LN + 16, big)
        if gi_ == 0:
            dbgdump("x1", xT[:, :, 0:N], [128, 8, N], t_xTc)
        mlp(0)
        if gi_ == 0:
            dbgdump("x2", xT[:, :, 0:N], [128, 8, N], t_xTc)
        dense(Wb['w_in_conv'], t_Wb['w_in_conv'], D, 3 * D, lambda kc: xTb[:, kc, 0:N], lambda kc: [t_xTbc[kc]], N, cb_proj)
        bg = big[:, 0:8, 0:N]
        u = big[:, 8:16, 0:N]
        hh_ = big[:, 16:24, 0:N]
        tmp = big[:, 24:32, 0:N]
        O('pool', [t_big], [t_big], Pl.tensor_tensor, u, u, hh_, ALU.mult)
        u4 = u.rearrange("p c (s t) -> p c s t", s=nseq)
        acc4 = hh_.rearrange("p c (s t) -> p c s t", s=nseq)
        tmp4 = tmp.rearrange("p c (s t) -> p c s t", s=nseq)

        def cw(j, shp):
            a_ = pp[:, PP_CW + 8 * j:PP_CW + 8 * j + 8].unsqueeze(2)
            if len(shp) == 4:
                a_ = a_.unsqueeze(3)
            return bc(a_, shp)
        O('act', [t_big], [t_clast], A.copy, clast[:, :, 0:nseq, :], u4[:, :, :, C - 2:C])
        O('dve', [t_big, t_pp], [t_big], V.tensor_tensor, acc4, u4, cw(2, [128, 8, nseq, C]), ALU.mult)
        O('pool', [t_big, t_pp], [t_big], Pl.tensor_tensor, tmp4[:, :, :, 1:C], u4[:, :, :, 0:C - 1], cw(1, [128, 8, nseq, C - 1]), ALU.mult)
        O('dve', [t_big], [t_big], V.tensor_tensor, acc4[:, :, :, 1:C], acc4[:, :, :, 1:C], tmp4[:, :, :, 1:C], ALU.add)
        O('pool', [t_big, t_pp], [t_big], Pl.tensor_tensor, tmp4[:, :, :, 2:C], u4[:, :, :, 0:C - 2], cw(0, [128, 8, nseq, C - 2]), ALU.mult)
        O('dve', [t_big], [t_big], V.tensor_tensor, acc4[:, :, :, 2:C], acc4[:, :, :, 2:C], tmp4[:, :, :, 2:C], ALU.add)
        pc = prevc[:, :, 0:nseq, :]
        O('pool', [t_prevc, t_pp], [t_big], Pl.tensor_tensor, tmp4[:, :, :, 0], pc[:, :, :, 1], cw(1, [128, 8, nseq]), ALU.mult)
        O('dve', [t_big], [t_big], V.tensor_tensor, acc4[:, :, :, 0], acc4[:, :, :, 0], tmp4[:, :, :, 0], ALU.add)
        O('pool', [t_prevc, t_pp], [t_big], Pl.tensor_tensor, tmp4[:, :, :, 0], pc[:, :, :, 0], cw(0, [128, 8, nseq]), ALU.mult)
        O('dve', [t_big], [t_big], V.tensor_tensor, acc4[:, :, :, 0], acc4[:, :, :, 0], tmp4[:, :, :, 0], ALU.add)
        O('pool', [t_prevc, t_pp], [t_big], Pl.tensor_tensor, tmp4[:, :, :, 1], pc[:, :, :, 1], cw(0, [128, 8, nseq]), ALU.mult)
        O('dve', [t_big], [t_big], V.tensor_tensor, acc4[:, :, :, 1], acc4[:, :, :, 1], tmp4[:, :, :, 1], ALU.add)
        if kind == 0:
            O('act', [t_clast], [t_prevc], A.copy, prevc[:, :, 0:1, :], clast[:, :, 0:1, :])
        O('dve', [t_big], [t_yT], V.tensor_tensor, yT[:, :, 0:N], bg, hh_, ALU.mult)
        if kind == 1 or (t0 + N == SEQ):
            row0 = 0 if kind == 0 else 2
            nr = 2 * nseq
            pa = [nps(), nps()]
            for kc in range(8):
                pp_, tpp = pa[kc // 4]
                tr(pp_[0:nr, (kc % 4) * 128:(kc % 4 + 1) * 128], clast[:, kc, 0:nseq, :].rearrange("p s j -> p (s j)"), ident,
                   [t_clast, t_const], [tpp])
            O('act', [pa[0][1]], [t_stage], A.copy, stage[0:nr, 0:512], pa[0][0][0:nr, :])
            O('dve', [pa[1][1]], [t_stage], V.tensor_copy, stage[0:nr, 512:1024], pa[1][0][0:nr, :])
            out_evs.append(sy.dma('pool', [t_stage], [], conv_out[row0:row0 + nr, :], stage[0:nr, :]))
        dense(Wb['w_out_conv'], t_Wb['w_out_conv'], D, D, lambda kc: yT[:, kc, 0:N], lambda kc: [t_yT], N, cb_res)
        layer_norm(N, PP_LN + 8, PP_LN + 24, big)
        mlp(1)
        for ti in range(ntile):
            for hb in range(2):
                ps, tp = nps()
                for c in range(4):
                    tr(ps[:, c * 128:(c + 1) * 128], xT[:, hb * 4 + c, ti * 128:(ti + 1) * 128], ident, [t_xTc[hb * 4 + c], t_const], [tp])
                O('act' if hb == 0 else 'dve', [tp], [t_xtok], A.copy if hb == 0 else V.tensor_copy,
                  xtok[:, hb * 512:(hb + 1) * 512], ps[:])
            out_evs.append(sy.dma('pool', [t_xtok], [], yout[t0 + ti * 128:t0 + (ti + 1) * 128, :], xtok[:]))
    for ev in out_evs:
        sy._wait('sp', ev)
    sy.barrier()
    print('sbuf bytes remaining', nc.sbuf_bytes_remaining, sy.cnt, sy.nwait, flush=True)
    return nc, sy


_PROG = {}


def _prep_inputs(inp):
    f = lambda a: np.ascontiguousarray(np.asarray(a, dtype=np.float32))
    consts = host_consts()
    col = lambda v, n: f(v).reshape(n, 128).T
    pp = np.concatenate([
        col(inp['mu_a'][0], 14), col(inp['k_k'][0], 4), col(inp['k_a'][0], 4), col(np.asarray(inp['r_k'][0]).reshape(512), 4),
        col(inp['a0'][0], 4),
        col(inp['ln1_g'][0], 8), col(inp['ln1_g'][1], 8), col(inp['ln1_b'][0], 8), col(inp['ln1_b'][1], 8),
        col(inp['ln2_g'][0], 8), col(inp['ln2_g'][1], 8), col(inp['ln2_b'][0], 8), col(inp['ln2_b'][1], 8),
        col(inp['conv_w'][0][0], 8), col(inp['conv_w'][0][1], 8), col(inp['conv_w'][0][2], 8),
        col(inp['lnx_g'][0], 4), col(inp['lnx_b'][0], 4), col(inp['gn_g'][0], 4), col(inp['gn_b'][0], 4)], axis=1)
    assert pp.shape == (128, PP_N)
    rowb = lambda v: np.broadcast_to(f(v).reshape(1, 512), (128, 512))
    tokb = rowb(inp['w0'][0])
    z64 = np.zeros((64, 512), np.float32)
    smallw = np.stack([np.concatenate([f(inp['w2'][0]), z64], 0), np.concatenate([z64, f(inp['a2'][0])], 0), f(inp['g2'][0])], 1)
    shared = dict(pp=f(pp), tokb=f(tokb), smallw=f(smallw))
    shared.update({k: f(v) for k, v in consts.items()})
    shared['w_in_ab'] = f(inp['w_in_ab'][0])
    shared['w_out_ab'] = f(inp['w_out_ab'][0])
    shared['w_up'] = f(inp['w_up'])
    shared['w_down'] = f(inp['w_down'])
    shared['w_in_conv'] = f(inp['w_in_conv'][0])
    shared['w_out_conv'] = f(inp['w_out_conv'][0])
    xp = f(inp['x_prompt'])
    xs = f(inp['x_sample'])
    in_maps = []
    for c in range(NCORE):
        sl = slice(SB_PER * c, SB_PER * (c + 1))
        m = dict(shared)
        m['xin'] = np.concatenate([xp[c], xs[sl].reshape(SB_PER * DEC_SEQ, D)], 0)
        m['st_shift'] = f(inp['state_shift'][0][sl])
        m['st_wkv'] = f(inp['state_wkv'][0][sl])
        m['st_ret'] = f(inp['state_ret'][0][sl])
        m['st_conv'] = f(inp['state_conv'][0][sl]).reshape(32, D)
        in_maps.append(m)
    return in_maps


def kernel(**inputs):
    if 'nc' not in _PROG:
        _PROG['nc'] = build_program()[0]
    nc = _PROG['nc']
    in_maps = _prep_inputs(inputs)
    res = run_bass_kernel_spmd(nc, in_maps, core_ids=list(range(NCORE)))
    R = res.results
    y_prompt = np.stack([R[c]['yout'][:SEQ] for c in range(NCORE)], 0)
    y_sample = np.concatenate([R[c]['yout'][SEQ:].reshape(SB_PER, DEC_SEQ, D) for c in range(NCORE)], 0)
    p_shift = np.stack([R[c]['shift_out'][0] for c in range(NCORE)], 0)[None]
    s_shift = np.concatenate([R[c]['shift_out'][1:] for c in range(NCORE)], 0)[None]
    p_wkv = np.stack([R[c]['wkv_out'][0] for c in range(NCORE)], 0)[None]
    s_wkv = np.concatenate([R[c]['wkv_out'][1:] for c in range(NCORE)], 0)[None]
    p_ret = np.stack([R[c]['ret_out'][0] for c in range(NCORE)], 0)[None]
    s_ret = np.concatenate([R[c]['ret_out'][1:] for c in range(NCORE)], 0)[None]
    p_conv = np.stack([R[c]['conv_out'][0:2] for c in range(NCORE)], 0)[None]
    s_conv = np.concatenate([R[c]['conv_out'][2:].reshape(SB_PER, 2, D) for c in range(NCORE)], 0)[None]
    outs = (y_prompt, y_sample, p_shift, p_wkv, p_ret, p_conv, s_shift, s_wkv, s_ret, s_conv)
    return tuple(np.ascontiguousarray(o, dtype=np.float32) for o in outs)
```

```python
import math
import numpy as np
import concourse.bass as bass
import concourse.mybir as mybir
from concourse.bass_utils import run_bass_kernel_spmd

F32 = mybir.dt.float32
BF16 = mybir.dt.bfloat16
AF = mybir.ActivationFunctionType
ALU = mybir.AluOpType
AX = mybir.AxisListType

D = 1024
SEQ = 2048
NCORE = 8
SB_PER = 16
DEC_SEQ = 8
NTOK = SEQ + SB_PER * DEC_SEQ
A_PROJ = 1792
PAST_LEN = 16384
A_GN_EPS = 64e-5
B_GN_EPS = 1e-5
LN_EPS = 1e-5
ALPHA = 4.0 ** 0.25
EM05 = math.exp(-0.5)

GT = 256
NSLOT = 32

PP_MU, PP_KK, PP_KA, PP_RK, PP_A0 = 0, 14, 18, 22, 26
PP_LN = 30
PP_CW = 94
PP_LXG, PP_LXB, PP_GNG, PP_GNB = 118, 122, 126, 130
PP_N = 134


class Buf:
    def __init__(self):
        self.w = {}
        self.r = {}


class Sy:
    EPOCH = 8000

    def __init__(self, nc):
        self.nc = nc
        self.eng = {'pe': nc.tensor, 'act': nc.scalar, 'dve': nc.vector, 'pool': nc.gpsimd, 'sp': nc.sync}
        self.cnt = {k: 0 for k in self.eng}
        self.sems = {k: [] for k in self.eng}
        self.seen = {k: {} for k in self.eng}
        self.dsem = {}
        self.drr = {k: 0 for k in self.eng}
        self.last = {}
        self.nwait = 0

    def _wait(self, e, ev):
        key, h, v = ev
        if self.seen[e].get(key, 0) >= v:
            return
        self.eng[e].wait_ge(h, v)
        self.nwait += 1
        self.seen[e][key] = v

    def _deps(self, e, reads, writes, dma=False):
        evs = []
        for b in reads:
            evs.extend(b.w.values())
        for b in writes:
            for k, ev in b.w.items():
                if k == e and not dma:
                    continue
                evs.append(ev)
            for k, ev in b.r.items():
                if k == e and not dma:
                    continue
                evs.append(ev)
        for ev in evs:
            self._wait(e, ev)

    def op(self, e, reads, writes, fn, *a, **kw):
        self._deps(e, reads, writes)
        inst = fn(*a, **kw)
        n = self.cnt[e]
        ep, v = divmod(n, self.EPOCH)
        if ep >= len(self.sems[e]):
            self.sems[e].append(self.nc.alloc_semaphore(name=f"s_{e}_{ep}"))
        h = self.sems[e][ep]
        inst.then_inc(h, 1)
        self.cnt[e] = n + 1
        ev = ((e, ep), h, v + 1)
        self.last[e] = ev
        for b in reads:
            b.r[e] = ev
        for b in writes:
            b.w[e] = ev
        return ev

    def dma(self, e, reads, writes, out, in_, **kw):
        if e not in self.dsem:
            self.dsem[e] = [[self.nc.alloc_semaphore(name=f"d_{e}_{i}"), 0] for i in range(40 if e == 'pool' else 8)]
        i = self.drr[e]
        self.drr[e] = (i + 1) % len(self.dsem[e])
        slot = self.dsem[e][i]
        key = ('dma', e, i)
        if slot[1] > 0:
            self._wait(e, (key, slot[0], slot[1]))
        self._deps(e, reads, writes, dma=True)
        inst = self.eng[e].dma_start(out=out, in_=in_, **kw)
        slot[1] += 16
        inst.then_inc(slot[0], 16)
        ev = (key, slot[0], slot[1])
        for b in reads:
            b.r[key] = ev
        for b in writes:
            b.w[key] = ev
        return ev

    def barrier(self):
        evs = list(self.last.values())
        for e2, slots in self.dsem.items():
            for i, s in enumerate(slots):
                if s[1] > 0:
                    evs.append((('dma', e2, i), s[0], s[1]))
        for e in self.eng:
            for ev in evs:
                self._wait(e, ev)


def host_consts():
    c = {}
    ident = np.eye(128, dtype=np.float32)
    ones = np.ones((128, 128), np.float32)
    blk = np.zeros((128, 128), np.float32)
    blk[:64, :64] = 1
    blk[64:, 64:] = 1
    rot = np.zeros((128, 128), np.float32)
    for hp in range(2):
        for n in range(64):
            if n < 32:
                rot[hp * 64 + n + 32, hp * 64 + n] = -1.0
            else:
                rot[hp * 64 + n - 32, hp * 64 + n] = 1.0
    c['c_sq'] = np.stack([ident, ones, blk, rot], 1)
    hm = np.zeros((128, 4), np.float32)
    hm[:64, 0] = 1
    hm[64:, 1] = 1
    hm[:, 2:4] = -hm[:, 0:2]
    c['c_hm'] = hm
    kinds = []
    dms = []
    decs = []
    cdecs = []
    lg = np.log1p(-np.exp2(-5.0 - np.arange(8, dtype=np.float32))).astype(np.float32)
    for nseq in (1, 16):
        C = 128 // nseq
        s = np.arange(128)
        same = (s[:, None] // C) == (s[None, :] // C)
        le = s[:, None] <= s[None, :]
        lt = s[:, None] < s[None, :]
        triI = (same & le).astype(np.float32) * (-EM05)
        triS = (same & lt).astype(np.float32) * (-EM05)
        mTs = (same & lt).astype(np.float32)
        mTi = (same & le).astype(np.float32)
        mLs = mTs.T.copy()
        kinds.append(np.stack([triI, triS, mTs, mTi, mLs], 1))
        diff = (s[None, :] - s[:, None]).astype(np.float32)
        dm = np.zeros((128, 8, 128), np.float32)
        for h in range(8):
            dm[:, h, :] = np.where(same & le, np.exp(lg[h] * np.maximum(diff, 0.0)), 0.0)
        dms.append(dm)
        i_in = (s % C).astype(np.float32)
        dec = np.zeros((128, 2, 4, 128), np.float32)
        cd = np.zeros((128, 4), np.float32)
        for cc in range(4):
            for hp in range(2):
                h = 2 * cc + hp
                dec[hp * 64:(hp + 1) * 64, 0, cc, :] = np.exp(lg[h] * (i_in + 1.0))[None, :]
                dec[hp * 64:(hp + 1) * 64, 1, cc, :] = np.exp(lg[h] * (C - 1.0 - i_in))[None, :]
                cd[hp * 64:(hp + 1) * 64, cc] = np.exp(lg[h] * C)
        decs.append(dec)
        cdecs.append(cd)
    c['c_kind'] = np.stack(kinds, 0)
    c['c_dmask'] = np.stack(dms, 0)
    c['c_dec'] = np.stack(decs, 0)
    c['c_cdec'] = np.stack(cdecs, 0)
    sm = np.zeros((128, 16), np.float32)
    sm[np.arange(128), np.arange(128) // 8] = 1
    c['c_seq'] = sm
    half = 32
    inv = (10000.0 ** (-np.arange(half, dtype=np.float64) / float(half))).astype(np.float32)
    pos = np.concatenate([np.arange(SEQ), np.tile(PAST_LEN + np.arange(DEC_SEQ), SB_PER)]).astype(np.float32)
    ang = (pos[:, None] * inv[None, :]).astype(np.float32)
    cos = np.cos(ang).astype(np.float32)
    sin = np.sin(ang).astype(np.float32)
    fi = np.arange(128) % 32
    cosT = cos[:, fi].T
    sinT = sin[:, fi].T
    rp = np.zeros((17, 128, 4, 128), np.float32)
    for t in range(17):
        rp[t, :, 0] = cosT[:, t * 128:(t + 1) * 128]
        rp[t, :, 1] = sinT[:, t * 128:(t + 1) * 128]
        rp[t, :, 2] = cosT[:, t * 128:(t + 1) * 128] * 0.125
        rp[t, :, 3] = sinT[:, t * 128:(t + 1) * 128] * 0.125
    c['c_rope'] = rp
    return c


WEIGHTS = [('w_in_ab', [D, 3840]), ('w_out_ab', [D, D]), ('w_up', [2, D, 4 * D]), ('w_down', [2, 4 * D, D]),
           ('w_in_conv', [D, 3 * D]), ('w_out_conv', [D, D])]


def build_program(ngroups_limit=None, dbg=None, only_sample=False):
    nc = bass.Bass("TRN2", target_bir_lowering=False)
    sy = Sy(nc)
    dbg = dbg or {}

    def din(name, shape):
        return nc.dram_tensor(name, list(shape), F32, kind="ExternalInput").ap()

    def dout(name, shape):
        return nc.dram_tensor(name, list(shape), F32, kind="ExternalOutput").ap()

    xin = din("xin", [NTOK, D])
    st_shift = din("st_shift", [16, A_PROJ])
    st_wkv = din("st_wkv", [16, 8, 64, 64])
    st_ret = din("st_ret", [16, 8, 64, 64])
    st_conv = din("st_conv", [32, D])
    pp_d = din("pp", [128, PP_N])
    tokb_d = din("tokb", [128, 512])
    smallw_d = din("smallw", [128, 3, 512])
    c_sq_d = din("c_sq", [128, 4, 128])
    c_hm_d = din("c_hm", [128, 4])
    c_kind_d = din("c_kind", [2, 128, 5, 128])
    c_dmask_d = din("c_dmask", [2, 128, 8, 128])
    c_dec_d = din("c_dec", [2, 128, 2, 4, 128])
    c_cdec_d = din("c_cdec", [2, 128, 4])
    c_seq_d = din("c_seq", [128, 16])
    c_rope_d = din("c_rope", [17, 128, 4, 128])
    W = {n: din(n, s) for n, s in WEIGHTS}

    yout = dout("yout", [NTOK, D])
    shift_out = dout("shift_out", [17, A_PROJ])
    wkv_out = dout("wkv_out", [17, 8, 64, 64])
    ret_out = dout("ret_out", [17, 8, 64, 64])
    conv_out = dout("conv_out", [34, D])
    out_evs = []

    def sbt(name, shape, dt=F32):
        return nc.sbuf_tensor("s_" + name, list(shape), dt).__enter__()

    PP_OMU, PP_OKA = PP_N, PP_N + 14
    pp = sbt("pp", [128, PP_N + 18]); t_pp = Buf()
    tokb = sbt("tokb", [128, 512]); t_const = Buf()
    smallw = sbt("smallw", [128, 3, 512])
    csq = sbt("csq", [128, 4, 128])
    chm = sbt("chm", [128, 4])
    cseq = sbt("cseq", [128, 16])
    ckind = sbt("ckind", [128, 5, 128]); t_kc = Buf()
    cdmask = sbt("cdmask", [128, 8, 128])
    cdec = sbt("cdec", [128, 2, 4, 128])
    ccdec = sbt("ccdec", [128, 4])
    rope = sbt("rope", [128, 4, 128]); t_rope = Buf()
    ident = csq[:, 0, :]
    ones = csq[:, 1, :]
    blkones = csq[:, 2, :]
    rotT = csq[:, 3, :]

    xT = sbt("xT", [128, 8, GT]); t_xTc = [Buf() for _ in range(8)]
    yT = sbt("yT", [128, 8, GT], BF16); t_yT = Buf()
    xTb = sbt("xTb", [128, 8, GT], BF16); t_xTbc = [Buf() for _ in range(8)]; t_lnc = [Buf() for _ in range(8)]
    bigflat = sbt("big", [128, NSLOT * GT]); t_big = Buf()
    WR_N = 3
    wring = [sbt(f"wr{i}", [128, 4096], BF16) for i in range(WR_N)]
    bigbf = bigflat[:].bitcast(BF16)
    t_wr = [Buf() for _ in range(WR_N)]
    wr_ctr = [0]
    xtok = sbt("xtok", [128, 1024]); t_xtok = Buf()
    stage, t_stage = xtok, t_xtok
    statsb = sbt("statsb", [128, 3, GT]); t_stat = Buf()
    Hp = sbt("Hp", [128, 4, 64]); t_H = Buf()
    Rp = sbt("Rp", [128, 4, 64]); t_R = Buf()
    Hs = sbt("Hs", [128, 4, 16, 64])
    Rs = bigflat[:, NSLOT * 128:NSLOT * 128 + 4096].rearrange("p (c s v) -> p c s v", c=4, s=16)
    prevP = sbt("prevP", [128, 14, 16]); t_prev = Buf()
    plast = sbt("plast", [128, 14, 16]); t_plast = Buf()
    prevc = sbt("prevc", [128, 8, 16, 2]); t_prevc = Buf()
    clast = sbt("clast", [128, 8, 16, 2]); t_clast = Buf()
    Mx = sbt("Mx", [128, 14, 128]); t_Mx = Buf()
    G = sbt("G", [128, 128]); t_G = Buf()
    sg = sbt("sg", [128, 128]); t_sg = Buf()
    ztok = sbt("ztok", [128, 512]); t_z = Buf()
    cum = sbt("cum", [128, 4, 128]); t_cum = Buf()
    cumx = sbt("cumx", [128, 4, 128]); t_cumx = Buf()
    a_t = sbt("a_t", [128, 4, 128]); t_a = Buf()
    kk = sbt("kk", [128, 4, 128]); t_kk = Buf()
    kp = sbt("kp", [128, 4, 128]); t_kp = Buf()
    beta = sbt("beta", [128, 4, 128]); t_beta = Buf()
    E = sbt("E", [128, 4, 128]); t_E = Buf()
    T1 = sbt("T1", [128, 4, 128]); t_T1 = Buf()
    Az = sbt("Az", [128, 4, 2, 128]); t_Az = Buf()
    Rz = sbt("Rz", [128, 4, 2, 128]); t_Rz = Buf()
    Bt = sbt("Bt", [128, 4, 128]); t_Bt = Buf()
    Kt = sbt("Kt", [128, 4, 128]); t_Kt = Buf()
    Khz = sbt("Khz", [128, 4, 2, 128]); t_Khz = Buf()
    Bhz = sbt("Bhz", [128, 4, 2, 128]); t_Bhz = Buf()
    Vtok = sbt("Vtok", [128, 512]); t_V = Buf()
    gT = sbt("gT", [128, 4, 128]); t_g = Buf()
    gC = sbt("gC", [128, 4, 16]); t_gC = Buf()
    DB = sbt("DB", [128, 5, 8, 128])
    t_Mb = [Buf(), Buf()]; t_MTb = [Buf(), Buf()]; t_PT = Buf()
    DBf = DB[:].rearrange("p a h t -> p (a h t)")
    LK = sbt("LK", [128, 3, 4, 128]); t_LK = Buf()
    XS = [(DBf[:, 2048:3072], t_MTb[0]), (DBf[:, 3072:4096], t_MTb[1])]
    OS = [(xtok, t_xtok), (DBf[:, 4096:5120], t_PT)]
    Wt = sbt("Wt", [128, 4, 64]); t_W = Buf()
    Uall = sbt("Uall", [128, 8, 64]); t_U = Buf()
    Oall = sbt("Oall", [128, 8, 64]); t_O = Buf()
    Xc = sbt("Xc", [128, 4, 64]); t_Xc = Buf()
    st1 = sbt("st1", [128, 8]); st2 = sbt("st2", [128, 8]); t_st = Buf()

    psb = [nc.psum_tensor(f"ps{i}", [128, 512], F32).__enter__() for i in range(8)]
    t_ps = [Buf() for _ in range(8)]
    ps_ctr = [0]

    def nps():
        i = 2 + ps_ctr[0] % 6
        ps_ctr[0] += 1
        return psb[i], t_ps[i]

    dps_ctr = [0]

    def dps():
        i = dps_ctr[0] % 2
        dps_ctr[0] += 1
        return psb[i], t_ps[i]

    O = sy.op
    V = nc.vector
    Pl = nc.gpsimd
    A = nc.scalar
    T = nc.tensor

    def mm(out, lhsT, rhs, reads, writes, start=True, stop=True):
        return O('pe', reads, writes, T.matmul, out, lhsT, rhs, start=start, stop=stop)

    def tr(out, in_, idn, reads, writes):
        return O('pe', reads, writes, T.transpose, out, in_, idn)

    def bc(ap, shape):
        return ap.to_broadcast(list(shape))

    def dbgdump(name, ap, shape, reads):
        if name in dbg:
            d = dout("dbg_" + name, shape)
            out_evs.append(sy.dma('pool', reads, [], d, ap))

    sy.dma('sp', [], [t_pp], pp[:, 0:PP_N], pp_d)
    sy.dma('sp', [], [t_const], tokb[:], tokb_d)
    sy.dma('sp', [], [t_const], smallw[:], smallw_d)
    sy.dma('sp', [], [t_const], csq[:], c_sq_d)
    sy.dma('sp', [], [t_const], chm[:], c_hm_d)
    sy.dma('sp', [], [t_const], cseq[:], c_seq_d)

    def load_kind_consts(kd):
        sy.dma('sp', [], [t_kc], ckind[:], c_kind_d[kd])
        sy.dma('sp', [], [t_kc], cdmask[:], c_dmask_d[kd])
        sy.dma('sp', [], [t_kc], cdec[:], c_dec_d[kd])
        sy.dma('sp', [], [t_kc], ccdec[:], c_cdec_d[kd])

    O('dve', [t_pp], [t_pp], V.tensor_scalar, pp[:, PP_OMU:PP_OMU + 14], pp[:, PP_MU:PP_MU + 14], -1.0, 1.0, ALU.mult, ALU.add)
    O('dve', [t_pp], [t_pp], V.tensor_scalar, pp[:, PP_OKA:PP_OKA + 4], pp[:, PP_KA:PP_KA + 4], -1.0, 1.0, ALU.mult, ALU.add)
    for t_, tt in ((Hp, t_H), (Rp, t_R), (prevP, t_prev), (prevc, t_prevc), (Khz, t_Khz), (Bhz, t_Bhz)):
        O('pool', [], [tt], Pl.memset, t_[:], 0.0)

    Wb = {}
    t_Wb = {}
    for n_, shp in WEIGHTS:
        Wb[n_] = nc.dram_tensor("wb_" + n_, list(shp), BF16, kind="Internal").ap()
        t_Wb[n_] = Buf()

    def cast_weight(n_, idx=None):
        src = W[n_] if idx is None else W[n_][idx]
        dst = Wb[n_] if idx is None else Wb[n_][idx]
        rows = src.shape[0]
        for r0 in range(0, rows, 256):
            sy.dma('pool', [], [t_Wb[n_]], dst[r0:r0 + 256, :], src[r0:r0 + 256, :])

    cast_weight('w_in_ab'); cast_weight('w_out_ab'); cast_weight('w_up', 0); cast_weight('w_down', 0)
    cast_weight('w_in_conv'); cast_weight('w_out_conv'); cast_weight('w_up', 1); cast_weight('w_down', 1)

    def dense(Wap, t_W_, K, Ocols, in_fn, in_reads, N, out_cb):
        KC = K // 128
        cols = 4096 // KC
        Wv = Wap.rearrange("(kc p) o -> p kc o", p=128)
        blocks = [(c0, min(cols, Ocols - c0)) for c0 in range(0, Ocols, cols)]
        loaded = {}

        def load(bi):
            c0, wd = blocks[bi]
            ri = wr_ctr[0] % WR_N
            wr_ctr[0] += 1
            wv = wring[ri][:, 0:KC * wd].rearrange("p (kc o) -> p kc o", kc=KC)
            sy.dma('sp', [t_W_], [t_wr[ri]], wv, Wv[:, :, c0:c0 + wd])
            loaded[bi] = (wv, t_wr[ri])

        load(0)
        if len(blocks) > 1:
            load(1)
        for bi, (c0, wd) in enumerate(blocks):
            if bi + 2 < len(blocks):
                load(bi + 2)
            wv, tw = loaded.pop(bi)
            for j in range(wd // 128):
                oc = c0 // 128 + j
                ps, tps = dps()
                for kc in range(KC):
                    mm(ps[:, 0:N], wv[:, kc, j * 128:(j + 1) * 128], in_fn(kc), [tw] + in_reads(kc), [tps],
                       start=(kc == 0), stop=(kc == KC - 1))
                out_cb(oc, ps[:, 0:N], tps)

    LNS = DB[:].rearrange("p a h t -> p (a h t)")[:, 0:8 * GT].rearrange("p (c t) -> p c t", c=8)

    def ln_square(kc, N):
        e = 'pool' if kc % 2 else 'dve'
        O(e, [t_xTc[kc]], [t_lnc[kc], t_Mb[kc // 4]], (Pl if kc % 2 else V).tensor_tensor, LNS[:, kc, 0:N],
          xT[:, kc, 0:N], xT[:, kc, 0:N], ALU.mult)

    def layer_norm(N, goff, boff, sq_scratch):
        ps1, tp1 = nps()
        for kc in range(8):
            mm(ps1[:, 0:N], ones, xT[:, kc, 0:N], [t_xTc[kc], t_const], [tp1], start=(kc == 0), stop=(kc == 7))
        ps2, tp2 = nps()
        for kc in range(8):
            mm(ps2[:, 0:N], ones, LNS[:, kc, 0:N], [t_lnc[kc], t_const], [tp2], start=(kc == 0), stop=(kc == 7))
        mean = statsb[:, 0, 0:N]
        msq = statsb[:, 1, 0:N]
        var = statsb[:, 2, 0:N]
        O('act', [tp1], [t_stat], A.mul, mean, ps1[:, 0:N], 1.0 / D)
        O('dve', [t_stat], [t_stat], V.tensor_tensor, msq, mean, mean, ALU.mult)
        O('dve', [tp2, t_stat], [t_stat], V.scalar_tensor_tensor, var, ps2[:, 0:N], 1.0 / D, msq, ALU.mult, ALU.subtract)
        O('dve', [t_stat], [t_stat], V.tensor_scalar, var, var, LN_EPS, None, ALU.add)
        O('act', [t_stat], [t_stat], A.activation, var, var, AF.Ln)
        O('act', [t_stat], [t_stat], A.activation, var, var, AF.Exp, scale=-0.5)
        O('dve', [t_stat], [t_stat], V.tensor_tensor, msq, mean, var, ALU.mult)
        for kc in range(8):
            O('dve', [t_xTc[kc], t_stat], [t_lnc[kc]], V.tensor_tensor, LNS[:, kc, 0:N], xT[:, kc, 0:N], var, ALU.mult)
        for kc in range(8):
            O('pool', [t_lnc[kc], t_stat], [t_lnc[kc]], Pl.tensor_tensor, LNS[:, kc, 0:N], LNS[:, kc, 0:N], msq, ALU.subtract)
        for kc in range(8):
            O('act', [t_lnc[kc], t_Mb[kc // 4], t_pp], [t_xTc[kc]], A.activation, xT[:, kc, 0:N], LNS[:, kc, 0:N], AF.Identity,
              bias=pp[:, boff + kc:boff + kc + 1], scale=pp[:, goff + kc:goff + kc + 1])
        for kc in range(8):
            O('dve', [t_xTc[kc]], [t_xTbc[kc]], V.tensor_copy, xTb[:, kc, 0:N], xT[:, kc, 0:N])

    def head_norm_core(eps):
        sqv = E[:].rearrange("p c (h n) -> p (c h) n", n=64)
        O('dve', [t_O], [t_st], V.tensor_reduce, st1[:], Oall[:], AX.X, ALU.add)
        O('dve', [t_st], [t_st], V.tensor_scalar, st1[:], st1[:], -1.0 / 64, None, ALU.mult)
        O('dve', [t_O, t_st], [t_O], V.tensor_tensor, Oall[:], Oall[:], bc(st1[:].unsqueeze(2), [128, 8, 64]), ALU.add)
        O('pool', [t_O], [t_E], Pl.tensor_tensor, sqv, Oall[:], Oall[:], ALU.mult)
        O('dve', [t_E], [t_st], V.tensor_reduce, st2[:], sqv, AX.X, ALU.add)
        O('dve', [t_st], [t_st], V.tensor_scalar, st2[:], st2[:], 1.0 / 64, eps, ALU.mult, ALU.add)
        O('act', [t_st], [t_st], A.activation, st2[:], st2[:], AF.Ln)
        O('act', [t_st], [t_st], A.activation, st2[:], st2[:], AF.Exp, scale=-0.5)
        O('dve', [t_O, t_st], [t_O], V.tensor_tensor, Oall[:], Oall[:], bc(st2[:].unsqueeze(2), [128, 8, 64]), ALU.mult)

    def norm_out_T(goff, boff):
        Of = Oall[:].rearrange("p h v -> p (h v)")
        ps, tp = nps()
        for c in range(4):
            tr(ps[:, c * 128:(c + 1) * 128], Of[:, c * 128:(c + 1) * 128], ident, [t_O, t_const], [tp])
        for c in range(4):
            O('act', [tp, t_pp], [t_T1], A.activation, T1[:, c, :], ps[:, c * 128:(c + 1) * 128], AF.Identity,
              bias=pp[:, boff + c:boff + c + 1], scale=pp[:, goff + c:goff + c + 1])

    Xt = DBf[:, 0:1024].rearrange("p (s v) -> p s v", v=64)

    def cross_select(lhsT, t_l, St, t_S, c, dst, t_dst):
        for j in range(2):
            ps, tp = nps()
            mm(ps[:], lhsT, St[:, c, 8 * j:8 * j + 8, :].rearrange("p s v -> p (s v)"), [t_l, t_S], [tp])
            O('dve', [tp, t_const], [t_Mb[0]], V.tensor_tensor, Xt[:, 8 * j:8 * j + 8, :],
              ps[:].rearrange("p (s v) -> p s v", v=64),
              bc(cseq[:, 8 * j:8 * j + 8].unsqueeze(2), [128, 8, 64]), ALU.mult)
        O('dve', [t_Mb[0]], [t_dst], V.tensor_reduce, dst, Xt.rearrange("p s v -> p v s"), AX.X, ALU.add)

    def expand_seq(src, t_src, i, eng):
        dst = DBf[:, i * 1024:(i + 1) * 1024].rearrange("p (s v) -> p s v", v=64)
        fn = V.tensor_tensor if eng == 'dve' else Pl.tensor_tensor
        O(eng, [t_src, t_const], [t_Mb[i]], fn, dst, bc(src.unsqueeze(1), [128, 16, 64]),
          bc(cseq[:].unsqueeze(2), [128, 16, 64]), ALU.mult)

    def exflat(i, j):
        return DBf[:, i * 1024 + j * 512:i * 1024 + (j + 1) * 512]

    def rwkv_tile(kind, gtile, ti, big, is_last_prompt):
        nseq = 1 if kind == 0 else 16
        C = 128 // nseq
        nlev = int(math.log2(C))
        cs = slice(ti * 128, (ti + 1) * 128)
        PR = big[:, 0:14, cs]
        PR4 = PR.rearrange("p j (s c) -> p j s c", s=nseq)
        Mx4 = Mx[:].rearrange("p j (s c) -> p j s c", s=nseq)
        St, t_S = (Hp, t_H) if kind == 0 else (Hs, t_H)
        triI, triS, mTs, mTi, mLs = (ckind[:, i, :] for i in range(5))
        mu_b = pp[:, PP_MU:PP_MU + 14]
        omu_b = pp[:, PP_OMU:PP_OMU + 14]
        O('act', [t_big], [t_plast], A.copy, plast[:, :, 0:nseq], PR4[:, :, :, C - 1])
        O('dve', [t_big, t_pp], [t_Mx], V.tensor_tensor, Mx4[:, :, :, 1:C], PR4[:, :, :, 0:C - 1],
          bc(mu_b.unsqueeze(2).unsqueeze(3), [128, 14, nseq, C - 1]), ALU.mult)
        O('pool', [t_prev, t_pp], [t_Mx], Pl.tensor_tensor, Mx4[:, :, :, 0], prevP[:, :, 0:nseq],
          bc(mu_b.unsqueeze(2), [128, 14, nseq]), ALU.mult)
        O('pool', [t_big, t_pp], [t_big], Pl.tensor_tensor, PR, PR, bc(omu_b.unsqueeze(2), [128, 14, 128]), ALU.mult)
        O('dve', [t_big, t_Mx], [t_big], V.tensor_tensor, PR, PR, Mx[:], ALU.add)
        if kind == 0:
            O('act', [t_plast], [t_prev], A.copy, prevP[:, :, 0:1], plast[:, :, 0:1])
        if kind == 1 or is_last_prompt:
            row0 = 0 if kind == 0 else 1
            for (jlo, jhi) in ((0, 8), (8, 14)):
                for j0 in range(jlo, jhi, 4):
                    ps, tp = nps()
                    nj = min(4, jhi - j0)
                    for j in range(nj):
                        tr(ps[0:nseq, j * 128:(j + 1) * 128], plast[:, j0 + j, 0:nseq], ident, [t_plast, t_const], [tp])
                    O('act', [tp], [t_stage], A.copy, stage[0:nseq, (j0 - jlo) * 128:(j0 - jlo + nj) * 128], ps[0:nseq, 0:nj * 128])
                out_evs.append(sy.dma('pool', [t_stage], [], shift_out[row0:row0 + nseq, jlo * 128:jhi * 128],
                                      stage[0:nseq, 0:(jhi - jlo) * 128]))
        Mr = big[:, 0:4, cs]
        Mk = big[:, 4:8, cs]
        Mv = big[:, 8:12, cs]
        M12 = big[:, 12, cs]
        M13 = big[:, 13, cs]
        if gtile in (0, 16):
            dbgdump(f"M{gtile}", PR, [128, 14, 128], [t_big])
        O('act', [t_big], [t_G], A.activation, G[0:64, :], M12[0:64, :], AF.Tanh)
        O('pool', [t_big], [t_G], Pl.tensor_copy, G[64:128, :], M12[64:128, :])
        O('act', [t_big], [t_sg], A.activation, sg[:], M13, AF.Sigmoid)
        ps, tp = nps()
        mm(ps[:], G[:], smallw[:, 0, :], [t_G, t_const], [tp])
        O('dve', [tp, t_const], [t_z], V.tensor_tensor, ztok[:], ps[:], tokb[:], ALU.add)
        O('act', [t_z], [t_z], A.activation, ztok[:], ztok[:], AF.Sigmoid)
        ps, tp = nps()
        for c in range(4):
            mm(ps[:, c * 128:(c + 1) * 128], ztok[:, c * 128:(c + 1) * 128], triI, [t_z, t_kc], [tp])
        O('act', [tp], [t_cum], A.copy, cum[:], ps[:].rearrange("p (c t) -> p c t", c=4))
        ps, tp = nps()
        for c in range(4):
            mm(ps[:, c * 128:(c + 1) * 128], ztok[:, c * 128:(c + 1) * 128], triS, [t_z, t_kc], [tp])
        O('dve', [tp], [t_cumx], V.tensor_copy, cumx[:], ps[:].rearrange("p (c t) -> p c t", c=4))
        ps, tp = nps()
        for c in range(4):
            mm(ps[:, c * 128:(c + 1) * 128], smallw[:, 1, c * 128:(c + 1) * 128], G[:], [t_G, t_const], [tp])
        for c in range(4):
            O('act', [tp, t_pp], [t_a], A.activation, a_t[:, c, :], ps[:, c * 128:(c + 1) * 128], AF.Sigmoid,
              bias=pp[:, PP_A0 + c:PP_A0 + c + 1], scale=1.0)
        ps, tp = nps()
        for c in range(4):
            mm(ps[:, c * 128:(c + 1) * 128], smallw[:, 2, c * 128:(c + 1) * 128], sg[:], [t_sg, t_const], [tp])
        O('act', [tp], [t_g], A.copy, gT[:], ps[:].rearrange("p (c t) -> p c t", c=4))
        b4 = lambda off: bc(pp[:, off:off + 4].unsqueeze(2), [128, 4, 128])
        f2 = lambda t: t[:].rearrange("p c t -> p (c t)")
        O('pool', [t_big, t_pp], [t_kk], Pl.tensor_tensor, kk[:], Mk, b4(PP_KK), ALU.mult)
        O('pool', [t_a, t_pp], [t_T1], Pl.tensor_tensor, T1[:], a_t[:], b4(PP_KA), ALU.mult)
        O('pool', [t_T1, t_pp], [t_T1], Pl.tensor_tensor, T1[:], T1[:], b4(PP_OKA), ALU.add)
        O('dve', [t_big, t_T1], [t_kp], V.tensor_tensor, kp[:], Mk, T1[:], ALU.mult)
        O('pool', [t_kk], [t_T1], Pl.tensor_tensor, T1[:], kk[:], kk[:], ALU.mult)
        ps, tp = nps()
        mm(ps[:], blkones, f2(T1), [t_T1, t_const], [tp])
        O('dve', [tp], [t_T1], V.tensor_scalar, f2(T1), ps[:], 1e-24, None, ALU.max)
        O('act', [t_T1], [t_T1], A.activation, f2(T1), f2(T1), AF.Ln)
        O('act', [t_T1], [t_T1], A.activation, f2(T1), f2(T1), AF.Exp, scale=-0.5)
        O('dve', [t_kk, t_T1], [t_kk], V.tensor_tensor, kk[:], kk[:], T1[:], ALU.mult)
        O('pool', [t_kk, t_a], [t_beta], Pl.tensor_tensor, beta[:], kk[:], a_t[:], ALU.mult)
        O('act', [t_cumx], [t_E], A.activation, E[:], cumx[:], AF.Exp)
        O('dve', [t_kk, t_E], [t_T1], V.tensor_tensor, T1[:], kk[:], E[:], ALU.mult)
        for hp in range(2):
            O('act' if hp else 'dve', [t_T1, t_const], [t_Az], *((A.activation, Az[:, :, hp, :], T1[:], AF.Identity) if hp else (V.tensor_scalar, Az[:, :, hp, :], T1[:], chm[:, 2 + hp:3 + hp], None, ALU.mult)), **(dict(scale=chm[:, 2 + hp:3 + hp]) if hp else dict()))
        O('act', [t_cum], [t_E], A.activation, E[:], cum[:], AF.Exp, scale=-1.0)
        O('dve', [t_beta, t_E], [t_Bt], V.tensor_tensor, Bt[:], beta[:], E[:], ALU.mult)
        O('pool', [t_kp, t_E], [t_Kt], Pl.tensor_tensor, Kt[:], kp[:], E[:], ALU.mult)
        O('act', [t_cum], [t_E], A.activation, E[:], cum[:], AF.Exp)
        O('dve', [t_big, t_E], [t_T1], V.tensor_tensor, T1[:], Mr, E[:], ALU.mult)
        for hp in range(2):
            O('act' if hp else 'dve', [t_T1, t_const], [t_Rz], *((A.activation, Rz[:, :, hp, :], T1[:], AF.Identity) if hp else (V.tensor_scalar, Rz[:, :, hp, :], T1[:], chm[:, hp:hp + 1], None, ALU.mult)), **(dict(scale=chm[:, hp:hp + 1]) if hp else dict()))
        cum4 = cum[:].rearrange("p c (s t) -> p c s t", s=nseq)
        E4 = E[:].rearrange("p c (s t) -> p c s t", s=nseq)
        O('dve', [t_cum], [t_E], V.tensor_tensor, E4, bc(cum4[:, :, :, C - 1:C], [128, 4, nseq, C]), cum4, ALU.subtract)
        O('act', [t_E], [t_E], A.activation, E[:], E[:], AF.Exp)
        O('act', [t_cum], [t_gC], A.activation, gC[:, :, 0:nseq], cum4[:, :, :, C - 1], AF.Exp)
        O('pool', [t_kp, t_E], [t_Mx], Pl.tensor_tensor, Mx[:, 0:4, :], kp[:], E[:], ALU.mult)
        O('dve', [t_beta, t_E], [t_Mx], V.tensor_tensor, Mx[:, 4:8, :], beta[:], E[:], ALU.mult)
        O('pool', [t_big, t_kp], [t_Mx], Pl.tensor_tensor, Mx[:, 8:12, :], Mr, kp[:], ALU.mult)
        O('pool', [t_Mx, t_pp], [t_Mx], Pl.tensor_tensor, Mx[:, 8:12, :], Mx[:, 8:12, :], b4(PP_RK), ALU.mult)
        for (src_lo, dstz, t_d) in ((0, Khz, t_Khz), (4, Bhz, t_Bhz)):
            ps, tp = nps()
            for c in range(4):
                tr(ps[:, c * 128:(c + 1) * 128], Mx[:, src_lo + c, :], ident, [t_Mx, t_const], [tp])
            psv = ps[:].rearrange("p (c f) -> p c f", c=4)
            for hp in range(2):
                e = 'act' if hp == 0 else 'dve'
                fn = A.copy if hp == 0 else V.tensor_copy
                O(e, [tp], [t_d], fn, dstz[:, :, hp, hp * 64:(hp + 1) * 64], psv[:, :, hp * 64:(hp + 1) * 64])
        ps, tp = nps()
        for c in range(4):
            tr(ps[:, c * 128:(c + 1) * 128], big[:, 8 + c, cs], ident, [t_big, t_const], [tp])
        O('act', [tp], [t_V], A.copy, Vtok[:], ps[:])
        ps, tp = nps()
        mm(ps[:], blkones, Mx[:, 8:12, :].rearrange("p c t -> p (c t)"), [t_Mx, t_const], [tp])
        O('dve', [tp, t_big], [t_Mx], V.tensor_tensor, Mx[:, 8:12, :], ps[:].rearrange("p (c t) -> p c t", c=4), Mv, ALU.mult)
        Mb = [DB[:, 0], DB[:, 1]]
        MTb = [DB[:, 2], DB[:, 3]]
        PT = DB[:, 4]
        LakT, NrbT, NrkT = LK[:, 0], LK[:, 1], LK[:, 2]
        v4 = lambda ps_: ps_[:].rearrange("p (h t) -> p h t", h=4)
        mk = lambda m: bc(m.unsqueeze(1), [128, 4, 128])
        hd = lambda hh: [(2 * hh + (hl // 2), hl % 2, 4 * hh + hl, hl) for hl in range(4)]
        for hh in range(2):
            hs = slice(4 * hh, 4 * hh + 4)
            psA, tpA = nps()
            psB, tpB = nps()
            for (c, hp, h, hl) in hd(hh):
                mm(psA[:, hl * 128:(hl + 1) * 128], Bt[:, c, :], Az[:, c, hp, :], [t_Bt, t_Az], [tpA])
            for (c, hp, h, hl) in hd(hh):
                mm(psB[:, hl * 128:(hl + 1) * 128], Az[:, c, hp, :], Bt[:, c, :], [t_Bt, t_Az], [tpB])
            O('dve', [tpA, t_kc], [t_MTb[0]], V.tensor_tensor, MTb[0][:, hs, :], v4(psA), mk(mTs), ALU.mult)
            O('dve', [tpB, t_kc], [t_Mb[0]], V.tensor_tensor, Mb[0][:, hs, :], v4(psB), mk(mLs), ALU.mult)
            O('pool', [t_MTb[0], t_const], [t_PT], Pl.tensor_tensor, PT[:, hs, :], MTb[0][:, hs, :], mk(ident), ALU.add)
        cur = 0
        for lev in range(1, nlev):
            nxt = 1 - cur
            need_mt = lev < nlev - 1
            pM = [nps(), nps()]
            for h in range(8):
                mm(pM[h // 4][0][:, (h % 4) * 128:(h % 4 + 1) * 128], MTb[cur][:, h, :], Mb[cur][:, h, :],
                   [t_MTb[cur], t_Mb[cur]], [pM[h // 4][1]])
            if need_mt:
                pMT = [nps(), nps()]
                for h in range(8):
                    mm(pMT[h // 4][0][:, (h % 4) * 128:(h % 4 + 1) * 128], Mb[cur][:, h, :], MTb[cur][:, h, :],
                       [t_MTb[cur], t_Mb[cur]], [pMT[h // 4][1]])
            O('act', [pM[0][1]], [t_Mb[nxt]], A.copy, Mb[nxt][:, 0:4, :], v4(pM[0][0]))
            O('dve', [pM[1][1]], [t_Mb[nxt]], V.tensor_copy, Mb[nxt][:, 4:8, :], v4(pM[1][0]))
            pP = [nps(), nps()]
            for h in range(8):
                mm(pP[h // 4][0][:, (h % 4) * 128:(h % 4 + 1) * 128], Mb[nxt][:, h, :], PT[:, h, :],
                   [t_Mb[nxt], t_PT], [pP[h // 4][1]])
            if need_mt:
                O('act', [pMT[0][1]], [t_MTb[nxt]], A.copy, MTb[nxt][:, 0:4, :], v4(pMT[0][0]))
                O('dve', [pMT[1][1]], [t_MTb[nxt]], V.tensor_copy, MTb[nxt][:, 4:8, :], v4(pMT[1][0]))
            for q in range(2):
                O('dve', [pP[q][1], t_PT], [t_PT], V.tensor_tensor, PT[:, 4 * q:4 * q + 4, :], v4(pP[q][0]),
                  PT[:, 4 * q:4 * q + 4, :], ALU.add)
            cur = nxt
        for hh in range(2):
            heads = hd(hh)
            prods = [(Kt, t_Kt, Az, t_Az, LakT, mTs), (Bt, t_Bt, Rz, t_Rz, NrbT, mTi), (Kt, t_Kt, Rz, t_Rz, NrkT, mTi)]
            pss = []
            for (L_, tl, R_, tr_, dst_, msk_) in prods:
                ps, tp = nps()
                for (c, hp, h, hl) in heads:
                    mm(ps[:, hl * 128:(hl + 1) * 128], L_[:, c, :], R_[:, c, hp, :], [tl, tr_], [tp])
                pss.append((ps, tp))
            for (ps, tp), (L_, tl, R_, tr_, dst_, msk_) in zip(pss, prods):
                O('dve', [tp, t_kc], [t_LK], V.tensor_tensor, dst_, v4(ps), mk(msk_), ALU.mult)
            Wv = Wt[:].rearrange("p h v -> p (h v)")
            Xcf = Xc[:].rearrange("p h v -> p (h v)")
            if kind == 1:
                for (c, hp, h, hl) in heads:
                    cross_select(Az[:, c, hp, :], t_Az, St, t_S, c, Xc[:, hl, :], t_Xc)
            psW, tpW = nps()
            for (c, hp, h, hl) in heads:
                o_ = psW[:, hl * 64:(hl + 1) * 64]
                if kind == 0:
                    mm(o_, Az[:, c, hp, :], St[:, c, :], [t_Az, t_S], [tpW], start=True, stop=False)
                mm(o_, LakT[:, hl, :], Vtok[:, h * 64:(h + 1) * 64], [t_LK, t_V], [tpW], start=(kind == 1), stop=True)
            if kind == 0:
                O('act', [tpW], [t_W], A.copy, Wv, psW[:, 0:256])
            else:
                O('dve', [tpW, t_Xc], [t_W], V.tensor_tensor, Wv, psW[:, 0:256], Xcf, ALU.add)
            psU, tpU = nps()
            for (c, hp, h, hl) in heads:
                mm(psU[:, hl * 64:(hl + 1) * 64], PT[:, h, :], Wt[:, hl, :], [t_PT, t_W], [tpU])
            O('act', [tpU], [t_U], A.copy, Uall[:, 4 * hh:4 * hh + 4, :].rearrange("p h v -> p (h v)"), psU[:, 0:256])
            if kind == 1:
                for (c, hp, h, hl) in heads:
                    cross_select(Rz[:, c, hp, :], t_Rz, St, t_S, c, Xc[:, hl, :], t_Xc)
            psO, tpO = nps()
            for (c, hp, h, hl) in heads:
                o_ = psO[:, hl * 64:(hl + 1) * 64]
                if kind == 0:
                    mm(o_, Rz[:, c, hp, :], St[:, c, :], [t_Rz, t_S], [tpO], start=True, stop=False)
                mm(o_, NrbT[:, hl, :], Uall[:, h, :], [t_LK, t_U], [tpO], start=(kind == 1), stop=False)
                mm(o_, NrkT[:, hl, :], Vtok[:, h * 64:(h + 1) * 64], [t_LK, t_V], [tpO], start=False, stop=True)
            Ov = Oall[:, 4 * hh:4 * hh + 4, :].rearrange("p h v -> p (h v)")
            if kind == 0:
                O('act', [tpO], [t_O], A.copy, Ov, psO[:, 0:256])
            else:
                O('dve', [tpO, t_Xc], [t_O], V.tensor_tensor, Ov, psO[:, 0:256], Xcf, ALU.add)
            for cc in range(2):
                c = 2 * hh + cc
                if kind == 0:
                    psH, tpH = nps()
                    for hp in range(2):
                        h = 2 * c + hp
                        mm(psH[:, 0:64], Bhz[:, c, hp, :], Uall[:, h, :], [t_Bhz, t_U], [tpH], start=(hp == 0), stop=False)
                        mm(psH[:, 0:64], Khz[:, c, hp, :], Vtok[:, h * 64:(h + 1) * 64], [t_Khz, t_V], [tpH],
                           start=False, stop=(hp == 1))
                    O('dve', [t_S, t_gC, tpH], [t_S], V.scalar_tensor_tensor, St[:, c, :], St[:, c, :], gC[:, c, 0:1],
                      psH[:, 0:64], ALU.mult, ALU.add)
                else:
                    phs = [nps(), nps()]
                    for hp in range(2):
                        h = 2 * c + hp
                        expand_seq(Uall[:, h, :], t_U, 0, 'pool')
                        expand_seq(Vtok[:, h * 64:(h + 1) * 64], t_V, 1, 'dve')
                        for j in range(2):
                            psH, tpH = phs[j]
                            mm(psH[:], Bhz[:, c, hp, :], exflat(0, j), [t_Bhz, t_Mb[0]], [tpH], start=(hp == 0), stop=False)
                            mm(psH[:], Khz[:, c, hp, :], exflat(1, j), [t_Khz, t_Mb[1]], [tpH], start=False, stop=(hp == 1))
                    O('pool', [t_S, t_gC], [t_S], Pl.tensor_tensor, St[:, c], St[:, c],
                      bc(gC[:, c, :].unsqueeze(2), [128, 16, 64]), ALU.mult)
                    for j in range(2):
                        psH, tpH = phs[j]
                        O('dve', [t_S, tpH], [t_S], V.tensor_tensor, St[:, c, 8 * j:8 * j + 8, :], St[:, c, 8 * j:8 * j + 8, :],
                          psH[:].rearrange("p (s v) -> p s v", v=64), ALU.add)
        if gtile in (0, 1, 16):
            dbgdump(f"Oall{gtile}", Oall[:], [128, 8, 64], [t_O])
        head_norm_core(A_GN_EPS)
        norm_out_T(PP_LXG, PP_LXB)
        O('pool', [t_T1, t_Mx], [t_T1], Pl.tensor_tensor, T1[:], T1[:], Mx[:, 8:12, :], ALU.add)
        O('dve', [t_T1, t_g], [t_yT], V.tensor_tensor, yT[:, 0:4, cs], T1[:], gT[:], ALU.mult)

    def ret_tile(kind, gtile, ti, big):
        cs = slice(ti * 128, (ti + 1) * 128)
        St, t_S = (Rp, t_R) if kind == 0 else (Rs, t_R)
        Pq = big[:, 14:18, cs]
        Pk = big[:, 18:22, cs]
        Pg = big[:, 26:30, cs]
        sy.dma('sp', [], [t_rope], rope[:], c_rope_d[gtile])
        Qr, t_Qr = Bt, t_Bt
        Kr, t_Kr = Kt, t_Kt
        Qrz, t_Qrz = Az, t_Az
        Qdz, t_Qdz = Rz, t_Rz
        Kdz, t_Kdz = Khz, t_Khz
        for (src, dst, t_d, ci) in ((Pq, Qr, t_Qr, 0), (Pk, Kr, t_Kr, 2)):
            ps, tp = nps()
            for c in range(4):
                mm(ps[:, c * 128:(c + 1) * 128], rotT, src[:, c, :], [t_big, t_const], [tp])
            O('pool', [t_big, t_rope], [t_T1], Pl.tensor_tensor, T1[:], src, bc(rope[:, ci, :].unsqueeze(1), [128, 4, 128]), ALU.mult)
            O('dve', [tp, t_rope], [t_E], V.tensor_tensor, E[:], ps[:].rearrange("p (c t) -> p c t", c=4),
              bc(rope[:, ci + 1, :].unsqueeze(1), [128, 4, 128]), ALU.mult)
            O('pool', [t_T1, t_E], [t_d], Pl.tensor_tensor, dst[:], T1[:], E[:], ALU.add)
        for hp in range(2):
            O('act' if hp else 'dve', [t_Qr, t_const], [t_Qrz], *((A.activation, Qrz[:, :, hp, :], Qr[:], AF.Identity) if hp else (V.tensor_scalar, Qrz[:, :, hp, :], Qr[:], chm[:, hp:hp + 1], None, ALU.mult)), **(dict(scale=chm[:, hp:hp + 1]) if hp else dict()))
        O('pool', [t_Qr, t_kc], [t_T1], Pl.tensor_tensor, T1[:], Qr[:], cdec[:, 0], ALU.mult)
        for hp in range(2):
            O('act' if hp else 'dve', [t_T1, t_const], [t_Qdz], *((A.activation, Qdz[:, :, hp, :], T1[:], AF.Identity) if hp else (V.tensor_scalar, Qdz[:, :, hp, :], T1[:], chm[:, hp:hp + 1], None, ALU.mult)), **(dict(scale=chm[:, hp:hp + 1]) if hp else dict()))
        O('pool', [t_Kr, t_kc], [t_Mx], Pl.tensor_tensor, Mx[:, 0:4, :], Kr[:], cdec[:, 1], ALU.mult)
        ps, tp = nps()
        for c in range(4):
            tr(ps[:, c * 128:(c + 1) * 128], Mx[:, c, :], ident, [t_Mx, t_const], [tp])
        psv = ps[:].rearrange("p (c f) -> p c f", c=4)
        for hp in range(2):
            e = 'act' if hp == 0 else 'dve'
            fn = A.copy if hp == 0 else V.tensor_copy
            O(e, [tp], [t_Kdz], fn, Kdz[:, :, hp, hp * 64:(hp + 1) * 64], psv[:, :, hp * 64:(hp + 1) * 64])
        ps, tp = nps()
        for c in range(4):
            tr(ps[:, c * 128:(c + 1) * 128], big[:, 22 + c, cs], ident, [t_big, t_const], [tp])
        O('act', [tp], [t_V], A.copy, Vtok[:], ps[:])
        O('act', [t_big], [t_Mx], A.activation, Mx[:, 4:8, :], Pg, AF.Sigmoid)
        O('dve', [t_big, t_Mx], [t_Mx], V.tensor_tensor, Mx[:, 4:8, :], Mx[:, 4:8, :], Pg, ALU.mult)
        scT = LK[:, 0]
        Xcf = Xc[:].rearrange("p h v -> p (h v)")
        for hh in range(2):
            heads = [(2 * hh + (hl // 2), hl % 2, 4 * hh + hl, hl) for hl in range(4)]
            ps, tp = nps()
            for (c, hp, h, hl) in heads:
                mm(ps[:, hl * 128:(hl + 1) * 128], Kr[:, c, :], Qrz[:, c, hp, :], [t_Kr, t_Qrz], [tp])
            O('dve', [tp, t_kc], [t_LK], V.tensor_tensor, scT, ps[:].rearrange("p (h t) -> p h t", h=4),
              cdmask[:, 4 * hh:4 * hh + 4, :], ALU.mult)
            if kind == 1:
                for (c, hp, h, hl) in heads:
                    cross_select(Qdz[:, c, hp, :], t_Qdz, St, t_S, c, Xc[:, hl, :], t_Xc)
            psO, tpO = nps()
            for (c, hp, h, hl) in heads:
                o_ = psO[:, hl * 64:(hl + 1) * 64]
                if kind == 0:
                    mm(o_, Qdz[:, c, hp, :], St[:, c, :], [t_Qdz, t_S], [tpO], start=True, stop=False)
                mm(o_, scT[:, hl, :], Vtok[:, h * 64:(h + 1) * 64], [t_LK, t_V], [tpO], start=(kind == 1), stop=True)
            Ov = Oall[:, 4 * hh:4 * hh + 4, :].rearrange("p h v -> p (h v)")
            if kind == 0:
                O('act', [tpO], [t_O], A.copy, Ov, psO[:, 0:256])
            else:
                O('dve', [tpO, t_Xc], [t_O], V.tensor_tensor, Ov, psO[:, 0:256], Xcf, ALU.add)
            for cc in range(2):
                c = 2 * hh + cc
                if kind == 0:
                    psH, tpH = nps()
                    for hp in range(2):
                        h = 2 * c + hp
                        mm(psH[:, 0:64], Kdz[:, c, hp, :], Vtok[:, h * 64:(h + 1) * 64], [t_Kdz, t_V], [tpH],
                           start=(hp == 0), stop=(hp == 1))
                    O('dve', [t_S, t_kc, tpH], [t_S], V.scalar_tensor_tensor, St[:, c, :], St[:, c, :],
                      ccdec[:, c:c + 1], psH[:, 0:64], ALU.mult, ALU.add)
                else:
                    phs = [nps(), nps()]
                    for hp in range(2):
                        h = 2 * c + hp
                        expand_seq(Vtok[:, h * 64:(h + 1) * 64], t_V, hp, 'dve' if hp else 'pool')
                        for j in range(2):
                            psH, tpH = phs[j]
                            mm(psH[:], Kdz[:, c, hp, :], exflat(hp, j), [t_Kdz, t_Mb[hp]], [tpH], start=(hp == 0), stop=(hp == 1))
                    O('dve', [t_S, t_kc], [t_S], V.tensor_scalar, St[:, c], St[:, c], ccdec[:, c:c + 1], None, ALU.mult)
                    for j in range(2):
                        psH, tpH = phs[j]
                        O('dve', [t_S, tpH], [t_S], V.tensor_tensor, St[:, c, 8 * j:8 * j + 8, :], St[:, c, 8 * j:8 * j + 8, :],
                          psH[:].rearrange("p (s v) -> p s v", v=64), ALU.add)
        if gtile in (0, 1, 16):
            dbgdump(f"Oret{gtile}", Oall[:], [128, 8, 64], [t_O])
        head_norm_core(B_GN_EPS)
        norm_out_T(PP_GNG, PP_GNB)
        O('dve', [t_T1, t_Mx], [t_yT], V.tensor_tensor, yT[:, 4:8, cs], T1[:], Mx[:, 4:8, :], ALU.mult)

    def state_out(kind):
        row0 = 0 if kind == 0 else 1
        nseq = 1 if kind == 0 else 16
        Hst = Hp if kind == 0 else Hs
        Rst = Rp if kind == 0 else Rs
        for c in range(4):
            for hp in range(2):
                h = 2 * c + hp
                if kind == 0:
                    out_evs.append(sy.dma('pool', [t_R], [], ret_out[0, h], Rst[hp * 64:(hp + 1) * 64, c, :]))
                else:
                    out_evs.append(sy.dma('pool', [t_R], [], ret_out[1:17, h].rearrange("s k v -> k s v"),
                                          Rst[hp * 64:(hp + 1) * 64, c, :, :]))
        for s0 in range(0, nseq, 2):
            ns = min(2, nseq - s0)
            pa = [nps(), nps()]
            for si in range(ns):
                for c in range(4):
                    src = Hst[:, c, :] if kind == 0 else Hst[:, c, s0 + si, :]
                    tr(pa[si][0][0:64, c * 128:(c + 1) * 128], src, ident, [t_H, t_const], [pa[si][1]])
            O('act', [pa[0][1]], [t_stage], A.copy, stage[0:64, 0:512], pa[0][0][0:64, :])
            if ns == 2:
                O('dve', [pa[1][1]], [t_stage], V.tensor_copy, stage[0:64, 512:1024], pa[1][0][0:64, :])
            for si in range(ns):
                out_evs.append(sy.dma('pool', [t_stage], [], wkv_out[row0 + s0 + si].rearrange("h v k -> v h k"),
                                      stage[0:64, si * 512:(si + 1) * 512].rearrange("v (h k) -> v h k", k=64)))

    def load_sample_states():
        sy.dma('sp', [], [t_stage], stage[0:16, 0:1024], st_shift[:, 0:1024])
        for j0 in (0, 4):
            ps, tp = nps()
            for j in range(4):
                tr(ps[:, j * 16:(j + 1) * 16], stage[0:16, (j0 + j) * 128:(j0 + j + 1) * 128], ident[0:16, 0:16], [t_stage, t_const], [tp])
            O('act', [tp], [t_prev], A.copy, prevP[:, j0:j0 + 4, :], ps[:, 0:64].rearrange("p (j s) -> p j s", s=16))
        sy.dma('sp', [], [t_stage], stage[0:16, 0:768], st_shift[:, 1024:1792])
        ps, tp = nps()
        for j in range(6):
            tr(ps[:, j * 16:(j + 1) * 16], stage[0:16, j * 128:(j + 1) * 128], ident[0:16, 0:16], [t_stage, t_const], [tp])
        O('act', [tp], [t_prev], A.copy, prevP[:, 8:14, :], ps[:, 0:96].rearrange("p (j s) -> p j s", s=16))
        sy.dma('sp', [], [t_stage], stage[0:32, :], st_conv)
        ps, tp = nps()
        for kc in range(8):
            tr(ps[:, kc * 32:(kc + 1) * 32], stage[0:32, kc * 128:(kc + 1) * 128], ident[0:32, 0:32], [t_stage, t_const], [tp])
        O('act', [tp], [t_prevc], A.copy, prevc[:].rearrange("p c s j -> p c (s j)"), ps[:, 0:256].rearrange("p (c r) -> p c r", c=8))
        for c in range(4):
            for hp in range(2):
                sy.dma('sp', [], [t_R], Rs[hp * 64:(hp + 1) * 64, c, :, :], st_ret[:, 2 * c + hp].rearrange("s k v -> k s v"))
        for s0 in range(0, 16, 2):
            sy.dma('sp', [], [t_stage], stage[0:64, :].rearrange("v (s h k) -> v s h k", s=2, h=8),
                   st_wkv[s0:s0 + 2].rearrange("s h v k -> v s h k"))
            for si in range(2):
                ps, tp = nps()
                for c in range(4):
                    tr(ps[:, c * 64:(c + 1) * 64], stage[0:64, si * 512 + c * 128:si * 512 + (c + 1) * 128], ident[0:64, 0:64],
                       [t_stage, t_const], [tp])
                O('act' if si == 0 else 'dve', [tp], [t_H], A.copy if si == 0 else V.tensor_copy, Hs[:, :, s0 + si, :],
                  ps[:, 0:256].rearrange("p (c v) -> p c v", c=4))

    groups = []
    t0 = 0
    while t0 < SEQ:
        groups.append((0, t0, GT))
        t0 += GT
    groups.append((1, SEQ, 128))
    if ngroups_limit is not None:
        groups = groups[:ngroups_limit]
    if only_sample:
        groups = groups[-1:]
    cur_kind = None

    def load_x(gidx):
        k_, t0_, N_ = groups[gidx]
        for ti_ in range(N_ // 128):
            xs_, t_xs = XS[ti_]
            sy.dma('sp', [], [t_xs], xs_, xin[t0_ + ti_ * 128:t0_ + (ti_ + 1) * 128, :])

    for gi_, (kind, t0, N) in enumerate(groups):
        ntile = N // 128
        nseq = 1 if kind == 0 else 16
        C = N // nseq
        if kind != cur_kind:
            if kind == 1:
                sy.barrier()
            load_kind_consts(kind)
            cur_kind = kind
        if kind == 0:
            big = bigflat[:].rearrange("p (s t) -> p s t", t=GT)
            hT = bigbf[:, 0:32 * GT].rearrange("p (s t) -> p s t", t=GT)
        else:
            big = bigflat[:, 0:NSLOT * 128].rearrange("p (s t) -> p s t", t=128)
            hT = bigbf[:, 0:32 * 128].rearrange("p (s t) -> p s t", t=128)
            load_sample_states()
        if gi_ == 0:
            load_x(0)
        for ti in range(ntile):
            xs_, t_xs = XS[ti]
            for hb in range(2):
                ps, tp = nps()
                for c in range(4):
                    tr(ps[:, c * 128:(c + 1) * 128], xs_[:, (hb * 4 + c) * 128:(hb * 4 + c + 1) * 128], ident, [t_xs, t_const], [tp])
                O('act' if hb == 0 else 'dve', [tp], t_xTc[hb * 4:hb * 4 + 4], A.copy if hb == 0 else V.tensor_copy,
                  xT[:, hb * 4:hb * 4 + 4, ti * 128:(ti + 1) * 128], ps[:].rearrange("p (c t) -> p c t", c=4))

        O('act', t_xTc, t_xTbc, A.copy, xTb[:, :, 0:N], xT[:, :, 0:N])

        def cb_proj(oc, ps, tps):
            e = 'act' if oc % 2 == 0 else 'dve'
            O(e, [tps], [t_big], A.copy if e == 'act' else V.tensor_copy, big[:, oc, 0:N], ps)

        def cb_res(oc, ps, tps):
            O('dve', [t_xTc[oc], tps], [t_xTc[oc]], V.scalar_tensor_tensor, xT[:, oc, 0:N], xT[:, oc, 0:N], ALPHA, ps, ALU.mult, ALU.add)
            ln_square(oc, N)

        def cb_up(oc, ps, tps):
            sc_ = big[:, 16 + oc % 8, 0:N]
            O('act', [tps], [t_big], A.activation, sc_, ps, AF.Relu)
            O('pool' if oc % 2 else 'dve', [t_big], [t_big], (Pl if oc % 2 else V).tensor_tensor, hT[:, oc, 0:N], sc_, sc_, ALU.mult)

        def mlp(l):
            dense(Wb['w_up'][l], t_Wb['w_up'], D, 4 * D, lambda kc: xTb[:, kc, 0:N], lambda kc: [t_xTbc[kc]], N, cb_up)
            dense(Wb['w_down'][l], t_Wb['w_down'], 4 * D, D, lambda kc: hT[:, kc, 0:N], lambda kc: [t_big], N, cb_res)
            layer_norm(N, PP_LN + 32 + 8 * l, PP_LN + 48 + 8 * l, big)

        dense(Wb['w_in_ab'], t_Wb['w_in_ab'], D, 3840, lambda kc: xTb[:, kc, 0:N], lambda kc: [t_xTbc[kc]], N, cb_proj)
        if gi_ == 0:
            dbgdump("proj", big[:, 0:30, 0:N], [128, 30, N], [t_big])
        for ti in range(ntile):
            gtile = t0 // 128 + ti
            rwkv_tile(kind, gtile, ti, big, is_last_prompt=(kind == 0 and gtile == SEQ // 128 - 1))
            ret_tile(kind, gtile, ti, big)
        if kind == 1 or (t0 + N == SEQ):
            state_out(kind)
        if gi_ + 1 < len(groups):
            load_x(gi_ + 1)
        dense(Wb['w_out_ab'], t_Wb['w_out_ab'], D, D, lambda kc: yT[:, kc, 0:N], lambda kc: [t_yT], N, cb_res)
        layer_norm(N, PP_LN + 0, PP_LN + 16, big)
        if gi_ == 0:
            dbgdump("x1", xT[:, :, 0:N], [128, 8, N], t_xTc)
        mlp(0)
        if gi_ == 0:
            dbgdump("x2", xT[:, :, 0:N], [128, 8, N], t_xTc)
        dense(Wb['w_in_conv'], t_Wb['w_in_conv'], D, 3 * D, lambda kc: xTb[:, kc, 0:N], lambda kc: [t_xTbc[kc]], N, cb_proj)
        bg = big[:, 0:8, 0:N]
        u = big[:, 8:16, 0:N]
        hh_ = big[:, 16:24, 0:N]
        tmp = big[:, 24:32, 0:N]
        O('pool', [t_big], [t_big], Pl.tensor_tensor, u, u, hh_, ALU.mult)
        u4 = u.rearrange("p c (s t) -> p c s t", s=nseq)
        acc4 = hh_.rearrange("p c (s t) -> p c s t", s=nseq)
        tmp4 = tmp.rearrange("p c (s t) -> p c s t", s=nseq)

        def cw(j, shp):
            a_ = pp[:, PP_CW + 8 * j:PP_CW + 8 * j + 8].unsqueeze(2)
            if len(shp) == 4:
                a_ = a_.unsqueeze(3)
            return bc(a_, shp)
        O('act', [t_big], [t_clast], A.copy, clast[:, :, 0:nseq, :], u4[:, :, :, C - 2:C])
        O('dve', [t_big, t_pp], [t_big], V.tensor_tensor, acc4, u4, cw(2, [128, 8, nseq, C]), ALU.mult)
        O('pool', [t_big, t_pp], [t_big], Pl.tensor_tensor, tmp4[:, :, :, 1:C], u4[:, :, :, 0:C - 1], cw(1, [128, 8, nseq, C - 1]), ALU.mult)
        O('dve', [t_big], [t_big], V.tensor_tensor, acc4[:, :, :, 1:C], acc4[:, :, :, 1:C], tmp4[:, :, :, 1:C], ALU.add)
        O('pool', [t_big, t_pp], [t_big], Pl.tensor_tensor, tmp4[:, :, :, 2:C], u4[:, :, :, 0:C - 2], cw(0, [128, 8, nseq, C - 2]), ALU.mult)
        O('dve', [t_big], [t_big], V.tensor_tensor, acc4[:, :, :, 2:C], acc4[:, :, :, 2:C], tmp4[:, :, :, 2:C], ALU.add)
        pc = prevc[:, :, 0:nseq, :]
        O('pool', [t_prevc, t_pp], [t_big], Pl.tensor_tensor, tmp4[:, :, :, 0], pc[:, :, :, 1], cw(1, [128, 8, nseq]), ALU.mult)
        O('dve', [t_big], [t_big], V.tensor_tensor, acc4[:, :, :, 0], acc4[:, :, :, 0], tmp4[:, :, :, 0], ALU.add)
        O('pool', [t_prevc, t_pp], [t_big], Pl.tensor_tensor, tmp4[:, :, :, 0], pc[:, :, :, 0], cw(0, [128, 8, nseq]), ALU.mult)
        O('dve', [t_big], [t_big], V.tensor_tensor, acc4[:, :, :, 0], acc4[:, :, :, 0], tmp4[:, :, :, 0], ALU.add)
        O('pool', [t_prevc, t_pp], [t_big], Pl.tensor_tensor, tmp4[:, :, :, 1], pc[:, :, :, 1], cw(0, [128, 8, nseq]), ALU.mult)
        O('dve', [t_big], [t_big], V.tensor_tensor, acc4[:, :, :, 1], acc4[:, :, :, 1], tmp4[:, :, :, 1], ALU.add)
        if kind == 0:
            O('act', [t_clast], [t_prevc], A.copy, prevc[:, :, 0:1, :], clast[:, :, 0:1, :])
        O('dve', [t_big], [t_yT], V.tensor_tensor, yT[:, :, 0:N], bg, hh_, ALU.mult)
        if kind == 1 or (t0 + N == SEQ):
            row0 = 0 if kind == 0 else 2
            nr = 2 * nseq
            pa = [nps(), nps()]
            for kc in range(8):
                pp_, tpp = pa[kc // 4]
                tr(pp_[0:nr, (kc % 4) * 128:(kc % 4 + 1) * 128], clast[:, kc, 0:nseq, :].rearrange("p s j -> p (s j)"), ident,
                   [t_clast, t_const], [tpp])
            O('act', [pa[0][1]], [t_stage], A.copy, stage[0:nr, 0:512], pa[0][0][0:nr, :])
            O('dve', [pa[1][1]], [t_stage], V.tensor_copy, stage[0:nr, 512:1024], pa[1][0][0:nr, :])
            out_evs.append(sy.dma('pool', [t_stage], [], conv_out[row0:row0 + nr, :], stage[0:nr, :]))
        dense(Wb['w_out_conv'], t_Wb['w_out_conv'], D, D, lambda kc: yT[:, kc, 0:N], lambda kc: [t_yT], N, cb_res)
        layer_norm(N, PP_LN + 8, PP_LN + 24, big)
        mlp(1)
        for ti in range(ntile):
            for hb in range(2):
                ps, tp = nps()
                for c in range(4):
                    tr(ps[:, c * 128:(c + 1) * 128], xT[:, hb * 4 + c, ti * 128:(ti + 1) * 128], ident, [t_xTc[hb * 4 + c], t_const], [tp])
                O('act' if hb == 0 else 'dve', [tp], [OS[ti][1]], A.copy if hb == 0 else V.tensor_copy,
                  OS[ti][0][:, hb * 512:(hb + 1) * 512], ps[:])
            out_evs.append(sy.dma('pool', [OS[ti][1]], [], yout[t0 + ti * 128:t0 + (ti + 1) * 128, :], OS[ti][0][:, 0:1024]))
    for ev in out_evs:
        sy._wait('sp', ev)
    sy.barrier()
    print('sbuf bytes remaining', nc.sbuf_bytes_remaining, sy.cnt, sy.nwait, flush=True)
    return nc, sy


_PROG = {}


def _prep_inputs(inp):
    f = lambda a: np.ascontiguousarray(np.asarray(a, dtype=np.float32))
    consts = host_consts()
    col = lambda v, n: f(v).reshape(n, 128).T
    pp = np.concatenate([
        col(inp['mu_a'][0], 14), col(inp['k_k'][0], 4), col(inp['k_a'][0], 4), col(np.asarray(inp['r_k'][0]).reshape(512), 4),
        col(inp['a0'][0], 4),
        col(inp['ln1_g'][0], 8), col(inp['ln1_g'][1], 8), col(inp['ln1_b'][0], 8), col(inp['ln1_b'][1], 8),
        col(inp['ln2_g'][0], 8), col(inp['ln2_g'][1], 8), col(inp['ln2_b'][0], 8), col(inp['ln2_b'][1], 8),
        col(inp['conv_w'][0][0], 8), col(inp['conv_w'][0][1], 8), col(inp['conv_w'][0][2], 8),
        col(inp['lnx_g'][0], 4), col(inp['lnx_b'][0], 4), col(inp['gn_g'][0], 4), col(inp['gn_b'][0], 4)], axis=1)
    assert pp.shape == (128, PP_N)
    rowb = lambda v: np.broadcast_to(f(v).reshape(1, 512), (128, 512))
    tokb = rowb(inp['w0'][0])
    z64 = np.zeros((64, 512), np.float32)
    smallw = np.stack([np.concatenate([f(inp['w2'][0]), z64], 0), np.concatenate([z64, f(inp['a2'][0])], 0), f(inp['g2'][0])], 1)
    shared = dict(pp=f(pp), tokb=f(tokb), smallw=f(smallw))
    shared.update({k: f(v) for k, v in consts.items()})
    shared['w_in_ab'] = f(inp['w_in_ab'][0])
    shared['w_out_ab'] = f(inp['w_out_ab'][0])
    shared['w_up'] = f(inp['w_up'])
    shared['w_down'] = f(inp['w_down'])
    shared['w_in_conv'] = f(inp['w_in_conv'][0])
    shared['w_out_conv'] = f(inp['w_out_conv'][0])
    xp = f(inp['x_prompt'])
    xs = f(inp['x_sample'])
    in_maps = []
    for c in range(NCORE):
        sl = slice(SB_PER * c, SB_PER * (c + 1))
        m = dict(shared)
        m['xin'] = np.concatenate([xp[c], xs[sl].reshape(SB_PER * DEC_SEQ, D)], 0)
        m['st_shift'] = f(inp['state_shift'][0][sl])
        m['st_wkv'] = f(inp['state_wkv'][0][sl])
        m['st_ret'] = f(inp['state_ret'][0][sl])
        m['st_conv'] = f(inp['state_conv'][0][sl]).reshape(32, D)
        in_maps.append(m)
    return in_maps


def kernel(**inputs):
    if 'nc' not in _PROG:
        _PROG['nc'] = build_program()[0]
    nc = _PROG['nc']
    in_maps = _prep_inputs(inputs)
    res = run_bass_kernel_spmd(nc, in_maps, core_ids=list(range(NCORE)))
    R = res.results
    y_prompt = np.stack([R[c]['yout'][:SEQ] for c in range(NCORE)], 0)
    y_sample = np.concatenate([R[c]['yout'][SEQ:].reshape(SB_PER, DEC_SEQ, D) for c in range(NCORE)], 0)
    p_shift = np.stack([R[c]['shift_out'][0] for c in range(NCORE)], 0)[None]
    s_shift = np.concatenate([R[c]['shift_out'][1:] for c in range(NCORE)], 0)[None]
    p_wkv = np.stack([R[c]['wkv_out'][0] for c in range(NCORE)], 0)[None]
    s_wkv = np.concatenate([R[c]['wkv_out'][1:] for c in range(NCORE)], 0)[None]
    p_ret = np.stack([R[c]['ret_out'][0] for c in range(NCORE)], 0)[None]
    s_ret = np.concatenate([R[c]['ret_out'][1:] for c in range(NCORE)], 0)[None]
    p_conv = np.stack([R[c]['conv_out'][0:2] for c in range(NCORE)], 0)[None]
    s_conv = np.concatenate([R[c]['conv_out'][2:].reshape(SB_PER, 2, D) for c in range(NCORE)], 0)[None]
    outs = (y_prompt, y_sample, p_shift, p_wkv, p_ret, p_conv, s_shift, s_wkv, s_ret, s_conv)
    return tuple(np.ascontiguousarray(o, dtype=np.float32) for o in outs)
```

```python
import math
import numpy as np
import concourse.bass as bass
import concourse.mybir as mybir
from concourse.bass_utils import run_bass_kernel_spmd

F32 = mybir.dt.float32
BF16 = mybir.dt.bfloat16
F32R = mybir.dt.float32r
AF = mybir.ActivationFunctionType
ALU = mybir.AluOpType
AX = mybir.AxisListType

D = 1024
SEQ = 2048
NCORE = 8
SB_PER = 16
DEC_SEQ = 8
NTOK = SEQ + SB_PER * DEC_SEQ
A_PROJ = 1792
PAST_LEN = 16384
A_GN_EPS = 64e-5
B_GN_EPS = 1e-5
LN_EPS = 1e-5
ALPHA = 4.0 ** 0.25
EM05 = math.exp(-0.5)

GT = 256
NSLOT = 32

PP_MU, PP_KK, PP_KA, PP_RK, PP_A0 = 0, 14, 18, 22, 26
PP_LN = 30
PP_CW = 94
PP_LXG, PP_LXB, PP_GNG, PP_GNB = 118, 122, 126, 130
PP_N = 134


class Buf:
    def __init__(self):
        self.w = {}
        self.r = {}


class Sy:
    EPOCH = 8000

    def __init__(self, nc):
        self.nc = nc
        self.eng = {'pe': nc.tensor, 'act': nc.scalar, 'dve': nc.vector, 'pool': nc.gpsimd, 'sp': nc.sync}
        self.cnt = {k: 0 for k in self.eng}
        self.sems = {k: [] for k in self.eng}
        self.seen = {k: {} for k in self.eng}
        self.dsem = {}
        self.drr = {k: 0 for k in self.eng}
        self.last = {}
        self.nwait = 0

    def _wait(self, e, ev):
        key, h, v = ev
        if self.seen[e].get(key, 0) >= v:
            return
        self.eng[e].wait_ge(h, v)
        self.nwait += 1
        self.seen[e][key] = v

    def _deps(self, e, reads, writes, dma=False):
        evs = []
        for b in reads:
            evs.extend(b.w.values())
        for b in writes:
            for k, ev in b.w.items():
                if k == e and not dma:
                    continue
                evs.append(ev)
            for k, ev in b.r.items():
                if k == e and not dma:
                    continue
                evs.append(ev)
        for ev in evs:
            self._wait(e, ev)

    def op(self, e, reads, writes, fn, *a, **kw):
        self._deps(e, reads, writes)
        inst = fn(*a, **kw)
        n = self.cnt[e]
        ep, v = divmod(n, self.EPOCH)
        if ep >= len(self.sems[e]):
            self.sems[e].append(self.nc.alloc_semaphore(name=f"s_{e}_{ep}"))
        h = self.sems[e][ep]
        inst.then_inc(h, 1)
        self.cnt[e] = n + 1
        ev = ((e, ep), h, v + 1)
        self.last[e] = ev
        for b in reads:
            b.r[e] = ev
        for b in writes:
            b.w[e] = ev
        return ev

    def dma(self, e, reads, writes, out, in_, **kw):
        if e not in self.dsem:
            self.dsem[e] = [[self.nc.alloc_semaphore(name=f"d_{e}_{i}"), 0] for i in range(40 if e == 'pool' else 8)]
        i = self.drr[e]
        self.drr[e] = (i + 1) % len(self.dsem[e])
        slot = self.dsem[e][i]
        key = ('dma', e, i)
        if slot[1] > 0:
            self._wait(e, (key, slot[0], slot[1]))
        self._deps(e, reads, writes, dma=True)
        inst = self.eng[e].dma_start(out=out, in_=in_, **kw)
        slot[1] += 16
        inst.then_inc(slot[0], 16)
        ev = (key, slot[0], slot[1])
        for b in reads:
            b.r[key] = ev
        for b in writes:
            b.w[key] = ev
        return ev

    def barrier(self):
        evs = list(self.last.values())
        for e2, slots in self.dsem.items():
            for i, s in enumerate(slots):
                if s[1] > 0:
                    evs.append((('dma', e2, i), s[0], s[1]))
        for e in self.eng:
            for ev in evs:
                self._wait(e, ev)


def host_consts():
    c = {}
    ident = np.eye(128, dtype=np.float32)
    ones = np.ones((128, 128), np.float32)
    blk = np.zeros((128, 128), np.float32)
    blk[:64, :64] = 1
    blk[64:, 64:] = 1
    rot = np.zeros((128, 128), np.float32)
    for hp in range(2):
        for n in range(64):
            if n < 32:
                rot[hp * 64 + n + 32, hp * 64 + n] = -1.0
            else:
                rot[hp * 64 + n - 32, hp * 64 + n] = 1.0
    c['c_sq'] = np.stack([ident, ones, blk, rot], 1)
    hm = np.zeros((128, 4), np.float32)
    hm[:64, 0] = 1
    hm[64:, 1] = 1
    hm[:, 2:4] = -hm[:, 0:2]
    c['c_hm'] = hm
    kinds = []
    dms = []
    decs = []
    cdecs = []
    lg = np.log1p(-np.exp2(-5.0 - np.arange(8, dtype=np.float32))).astype(np.float32)
    for nseq in (1, 16):
        C = 128 // nseq
        s = np.arange(128)
        same = (s[:, None] // C) == (s[None, :] // C)
        le = s[:, None] <= s[None, :]
        lt = s[:, None] < s[None, :]
        triI = (same & le).astype(np.float32) * (-EM05)
        triS = (same & lt).astype(np.float32) * (-EM05)
        mTs = (same & lt).astype(np.float32)
        mTi = (same & le).astype(np.float32)
        mLs = mTs.T.copy()
        kinds.append(np.stack([triI, triS, mTs, mTi, mLs], 1))
        diff = (s[None, :] - s[:, None]).astype(np.float32)
        dm = np.zeros((128, 8, 128), np.float32)
        for h in range(8):
            dm[:, h, :] = np.where(same & le, np.exp(lg[h] * np.maximum(diff, 0.0)), 0.0)
        dms.append(dm)
        i_in = (s % C).astype(np.float32)
        dec = np.zeros((128, 2, 4, 128), np.float32)
        cd = np.zeros((128, 4), np.float32)
        for cc in range(4):
            for hp in range(2):
                h = 2 * cc + hp
                dec[hp * 64:(hp + 1) * 64, 0, cc, :] = np.exp(lg[h] * (i_in + 1.0))[None, :]
                dec[hp * 64:(hp + 1) * 64, 1, cc, :] = np.exp(lg[h] * (C - 1.0 - i_in))[None, :]
                cd[hp * 64:(hp + 1) * 64, cc] = np.exp(lg[h] * C)
        decs.append(dec)
        cdecs.append(cd)
    c['c_kind'] = np.stack(kinds, 0)
    c['c_dmask'] = np.stack(dms, 0)
    c['c_dec'] = np.stack(decs, 0)
    c['c_cdec'] = np.stack(cdecs, 0)
    sm = np.zeros((128, 16), np.float32)
    sm[np.arange(128), np.arange(128) // 8] = 1
    c['c_seq'] = sm
    half = 32
    inv = (10000.0 ** (-np.arange(half, dtype=np.float64) / float(half))).astype(np.float32)
    pos = np.concatenate([np.arange(SEQ), np.tile(PAST_LEN + np.arange(DEC_SEQ), SB_PER)]).astype(np.float32)
    ang = (pos[:, None] * inv[None, :]).astype(np.float32)
    cos = np.cos(ang).astype(np.float32)
    sin = np.sin(ang).astype(np.float32)
    fi = np.arange(128) % 32
    cosT = cos[:, fi].T
    sinT = sin[:, fi].T
    rp = np.zeros((17, 128, 4, 128), np.float32)
    for t in range(17):
        rp[t, :, 0] = cosT[:, t * 128:(t + 1) * 128]
        rp[t, :, 1] = sinT[:, t * 128:(t + 1) * 128]
        rp[t, :, 2] = cosT[:, t * 128:(t + 1) * 128] * 0.125
        rp[t, :, 3] = sinT[:, t * 128:(t + 1) * 128] * 0.125
    c['c_rope'] = rp
    return c


WEIGHTS = [('w_in_ab', [D, 3840]), ('w_out_ab', [D, D]), ('w_up', [2, D, 4 * D]), ('w_down', [2, 4 * D, D]),
           ('w_in_conv', [D, 3 * D]), ('w_out_conv', [D, D])]


def build_program(ngroups_limit=None, dbg=None, only_sample=False):
    nc = bass.Bass("TRN2", target_bir_lowering=False)
    sy = Sy(nc)
    dbg = dbg or {}

    def din(name, shape):
        return nc.dram_tensor(name, list(shape), F32, kind="ExternalInput").ap()

    def dout(name, shape):
        return nc.dram_tensor(name, list(shape), F32, kind="ExternalOutput").ap()

    xin = din("xin", [NTOK, D])
    st_shift = din("st_shift", [16, A_PROJ])
    st_wkv = din("st_wkv", [16, 8, 64, 64])
    st_ret = din("st_ret", [16, 8, 64, 64])
    st_conv = din("st_conv", [32, D])
    pp_d = din("pp", [128, PP_N])
    tokb_d = din("tokb", [128, 512])
    smallw_d = din("smallw", [128, 3, 512])
    c_sq_d = din("c_sq", [128, 4, 128])
    c_hm_d = din("c_hm", [128, 4])
    c_kind_d = din("c_kind", [2, 128, 5, 128])
    c_dmask_d = din("c_dmask", [2, 128, 8, 128])
    c_dec_d = din("c_dec", [2, 128, 2, 4, 128])
    c_cdec_d = din("c_cdec", [2, 128, 4])
    c_seq_d = din("c_seq", [128, 16])
    c_rope_d = din("c_rope", [17, 128, 4, 128])
    W = {n: din(n, s) for n, s in WEIGHTS}

    yout = dout("yout", [NTOK, D])
    shift_out = dout("shift_out", [17, A_PROJ])
    wkv_out = dout("wkv_out", [17, 8, 64, 64])
    ret_out = dout("ret_out", [17, 8, 64, 64])
    conv_out = dout("conv_out", [34, D])
    out_evs = []

    def sbt(name, shape, dt=F32):
        return nc.sbuf_tensor("s_" + name, list(shape), dt).__enter__()

    PP_OMU, PP_OKA = PP_N, PP_N + 14
    pp = sbt("pp", [128, PP_N + 18]); t_pp = Buf()
    tokb = sbt("tokb", [128, 512]); t_const = Buf()
    smallw = sbt("smallw", [128, 3, 512])
    csq = sbt("csq", [128, 4, 128])
    chm = sbt("chm", [128, 4])
    cseq = sbt("cseq", [128, 16])
    ckind = sbt("ckind", [128, 5, 128]); t_kc = Buf()
    cdmask = sbt("cdmask", [128, 8, 128])
    cdec = sbt("cdec", [128, 2, 4, 128])
    ccdec = sbt("ccdec", [128, 4])
    rope = sbt("rope", [128, 4, 128]); t_rope = Buf()
    ident = csq[:, 0, :]
    ones = csq[:, 1, :]
    blkones = csq[:, 2, :]
    rotT = csq[:, 3, :]

    xT = sbt("xT", [128, 8, GT]); t_xTc = [Buf() for _ in range(8)]
    yT = sbt("yT", [128, 8, GT], BF16); t_yT = Buf()
    xTb = sbt("xTb", [128, 8, GT], BF16); t_xTbc = [Buf() for _ in range(8)]; t_lnc = [Buf() for _ in range(8)]
    bigflat = sbt("big", [128, NSLOT * GT]); t_big = Buf()
    WR_N = 3
    wring = [sbt(f"wr{i}", [128, 4096], BF16) for i in range(WR_N)]
    bigbf = bigflat[:].bitcast(BF16)
    t_wr = [Buf() for _ in range(WR_N)]
    wr_ctr = [0]
    xtok = sbt("xtok", [128, 1024]); t_xtok = Buf()
    stage, t_stage = xtok, t_xtok
    statsb = sbt("statsb", [128, 3, GT]); t_stat = Buf()
    Hp = sbt("Hp", [128, 4, 64]); t_H = Buf()
    Rp = sbt("Rp", [128, 4, 64]); t_R = Buf()
    Hs = sbt("Hs", [128, 4, 16, 64])
    Rs = bigflat[:, NSLOT * 128:NSLOT * 128 + 4096].rearrange("p (c s v) -> p c s v", c=4, s=16)
    prevP = sbt("prevP", [128, 14, 16]); t_prev = Buf()
    plast = sbt("plast", [128, 14, 16]); t_plast = Buf()
    prevc = sbt("prevc", [128, 8, 16, 2]); t_prevc = Buf()
    clast = sbt("clast", [128, 8, 16, 2]); t_clast = Buf()
    Mx = sbt("Mx", [128, 14, 128]); t_Mx = Buf()
    G = sbt("G", [128, 128]); t_G = Buf()
    sg = sbt("sg", [128, 128]); t_sg = Buf()
    ztok = sbt("ztok", [128, 512]); t_z = Buf()
    cum = sbt("cum", [128, 4, 128]); t_cum = Buf()
    cumx = sbt("cumx", [128, 4, 128]); t_cumx = Buf()
    a_t = sbt("a_t", [128, 4, 128]); t_a = Buf()
    kk = sbt("kk", [128, 4, 128]); t_kk = Buf()
    kp = sbt("kp", [128, 4, 128]); t_kp = Buf()
    beta = sbt("beta", [128, 4, 128]); t_beta = Buf()
    E = sbt("E", [128, 4, 128]); t_E = Buf()
    T1 = sbt("T1", [128, 4, 128]); t_T1 = Buf()
    Az = sbt("Az", [128, 4, 2, 128]); t_Az = Buf()
    Rz = sbt("Rz", [128, 4, 2, 128]); t_Rz = Buf()
    Bt = sbt("Bt", [128, 4, 128]); t_Bt = Buf()
    Kt = sbt("Kt", [128, 4, 128]); t_Kt = Buf()
    Khz = sbt("Khz", [128, 4, 2, 128]); t_Khz = Buf()
    Bhz = sbt("Bhz", [128, 4, 2, 128]); t_Bhz = Buf()
    Vtok = sbt("Vtok", [128, 512]); t_V = Buf()
    gT = sbt("gT", [128, 4, 128]); t_g = Buf()
    gC = sbt("gC", [128, 4, 16]); t_gC = Buf()
    DB = sbt("DB", [128, 5, 8, 128])
    t_Mb = [Buf(), Buf()]; t_MTb = [Buf(), Buf()]; t_PT = Buf()
    DBf = DB[:].rearrange("p a h t -> p (a h t)")
    LK = sbt("LK", [128, 3, 4, 128]); t_LK = Buf()
    XS = [(Mx[:].rearrange("p j t -> p (j t)")[:, 0:1024], t_Mx), (LK[:].rearrange("p a h t -> p (a h t)")[:, 0:1024], t_LK)]
    OS = [(xtok, t_xtok), (Az[:].rearrange("p c h t -> p (c h t)"), t_Az)]
    Wt = sbt("Wt", [128, 4, 64]); t_W = Buf()
    Uall = sbt("Uall", [128, 8, 64]); t_U = Buf()
    Oall = sbt("Oall", [128, 8, 64]); t_O = Buf()
    Xc = sbt("Xc", [128, 4, 64]); t_Xc = Buf()
    st1 = sbt("st1", [128, 8]); st2 = sbt("st2", [128, 8]); t_st = Buf()

    psb = [nc.psum_tensor(f"ps{i}", [128, 512], F32).__enter__() for i in range(8)]
    t_ps = [Buf() for _ in range(8)]
    ps_ctr = [0]

    def nps():
        i = 2 + ps_ctr[0] % 6
        ps_ctr[0] += 1
        return psb[i], t_ps[i]

    dps_ctr = [0]

    def dps():
        i = dps_ctr[0] % 2
        dps_ctr[0] += 1
        return psb[i], t_ps[i]

    O = sy.op
    V = nc.vector
    Pl = nc.gpsimd
    A = nc.scalar
    T = nc.tensor

    def mm(out, lhsT, rhs, reads, writes, start=True, stop=True):
        return O('pe', reads, writes, T.matmul, out, lhsT, rhs, start=start, stop=stop)

    def tr(out, in_, idn, reads, writes):
        return O('pe', reads, writes, T.transpose, out, in_, idn)

    def bc(ap, shape):
        return ap.to_broadcast(list(shape))

    def dbgdump(name, ap, shape, reads):
        if name in dbg:
            d = dout("dbg_" + name, shape)
            out_evs.append(sy.dma('pool', reads, [], d, ap))

    sy.dma('sp', [], [t_pp], pp[:, 0:PP_N], pp_d)
    sy.dma('sp', [], [t_const], tokb[:], tokb_d)
    sy.dma('sp', [], [t_const], smallw[:], smallw_d)
    sy.dma('sp', [], [t_const], csq[:], c_sq_d)
    sy.dma('sp', [], [t_const], chm[:], c_hm_d)
    sy.dma('sp', [], [t_const], cseq[:], c_seq_d)

    def load_kind_consts(kd):
        sy.dma('sp', [], [t_kc], ckind[:], c_kind_d[kd])
        sy.dma('sp', [], [t_kc], cdmask[:], c_dmask_d[kd])
        sy.dma('sp', [], [t_kc], cdec[:], c_dec_d[kd])
        sy.dma('sp', [], [t_kc], ccdec[:], c_cdec_d[kd])

    O('dve', [t_pp], [t_pp], V.tensor_scalar, pp[:, PP_OMU:PP_OMU + 14], pp[:, PP_MU:PP_MU + 14], -1.0, 1.0, ALU.mult, ALU.add)
    O('dve', [t_pp], [t_pp], V.tensor_scalar, pp[:, PP_OKA:PP_OKA + 4], pp[:, PP_KA:PP_KA + 4], -1.0, 1.0, ALU.mult, ALU.add)
    for t_, tt in ((Hp, t_H), (Rp, t_R), (prevP, t_prev), (prevc, t_prevc), (Khz, t_Khz), (Bhz, t_Bhz)):
        O('pool', [], [tt], Pl.memset, t_[:], 0.0)

    Wb = {}
    t_Wb = {}
    for n_, shp in WEIGHTS:
        Wb[n_] = nc.dram_tensor("wb_" + n_, list(shp), BF16, kind="Internal").ap()
        t_Wb[n_] = Buf()

    def cast_weight(n_, idx=None):
        src = W[n_] if idx is None else W[n_][idx]
        dst = Wb[n_] if idx is None else Wb[n_][idx]
        rows = src.shape[0]
        for r0 in range(0, rows, 256):
            sy.dma('pool', [], [t_Wb[n_]], dst[r0:r0 + 256, :], src[r0:r0 + 256, :])

    cast_weight('w_in_ab'); cast_weight('w_out_ab'); cast_weight('w_up', 0); cast_weight('w_down', 0)
    cast_weight('w_in_conv'); cast_weight('w_out_conv'); cast_weight('w_up', 1); cast_weight('w_down', 1)

    def dense(Wap, t_W_, K, Ocols, in_fn, in_reads, N, out_cb):
        KC = K // 128
        cols = 4096 // KC
        Wv = Wap.rearrange("(kc p) o -> p kc o", p=128)
        blocks = [(c0, min(cols, Ocols - c0)) for c0 in range(0, Ocols, cols)]
        loaded = {}

        def load(bi):
            c0, wd = blocks[bi]
            ri = wr_ctr[0] % WR_N
            wr_ctr[0] += 1
            wv = wring[ri][:, 0:KC * wd].rearrange("p (kc o) -> p kc o", kc=KC)
            sy.dma('sp', [t_W_], [t_wr[ri]], wv, Wv[:, :, c0:c0 + wd])
            loaded[bi] = (wv, t_wr[ri])

        load(0)
        if len(blocks) > 1:
            load(1)
        for bi, (c0, wd) in enumerate(blocks):
            if bi + 2 < len(blocks):
                load(bi + 2)
            wv, tw = loaded.pop(bi)
            for j in range(wd // 128):
                oc = c0 // 128 + j
                ps, tps = dps()
                for kc in range(KC):
                    mm(ps[:, 0:N], wv[:, kc, j * 128:(j + 1) * 128], in_fn(kc), [tw] + in_reads(kc), [tps],
                       start=(kc == 0), stop=(kc == KC - 1))
                out_cb(oc, ps[:, 0:N], tps)

    _lns_home = [(Rz[:].rearrange("p c h t -> p (c h t)"), t_Rz, 4), (Bt[:].rearrange("p c t -> p (c t)"), t_Bt, 2),
                 (Kt[:].rearrange("p c t -> p (c t)"), t_Kt, 2)]

    def lns(kc, N):
        k = kc
        for ap_, tok_, n_ in _lns_home:
            if k < n_:
                return ap_[:, k * GT:k * GT + N], tok_
            k -= n_

    def ln_square(kc, N):
        e = 'pool' if kc % 2 else 'dve'
        la, lt = lns(kc, N)
        O(e, [t_xTc[kc]], [t_lnc[kc], lt], (Pl if kc % 2 else V).tensor_tensor, la,
          xT[:, kc, 0:N], xT[:, kc, 0:N], ALU.mult)

    def layer_norm(N, goff, boff, sq_scratch):
        ps1, tp1 = nps()
        for kc in range(8):
            mm(ps1[:, 0:N], ones, xT[:, kc, 0:N], [t_xTc[kc], t_const], [tp1], start=(kc == 0), stop=(kc == 7))
        ps2, tp2 = nps()
        for kc in range(8):
            mm(ps2[:, 0:N], ones, lns(kc, N)[0], [t_lnc[kc], t_const], [tp2], start=(kc == 0), stop=(kc == 7))
        mean = statsb[:, 0, 0:N]
        msq = statsb[:, 1, 0:N]
        var = statsb[:, 2, 0:N]
        O('act', [tp1], [t_stat], A.mul, mean, ps1[:, 0:N], 1.0 / D)
        O('dve', [t_stat], [t_stat], V.tensor_tensor, msq, mean, mean, ALU.mult)
        O('dve', [tp2, t_stat], [t_stat], V.scalar_tensor_tensor, var, ps2[:, 0:N], 1.0 / D, msq, ALU.mult, ALU.subtract)
        O('dve', [t_stat], [t_stat], V.tensor_scalar, var, var, LN_EPS, None, ALU.add)
        O('act', [t_stat], [t_stat], A.activation, var, var, AF.Ln)
        O('act', [t_stat], [t_stat], A.activation, var, var, AF.Exp, scale=-0.5)
        O('dve', [t_stat], [t_stat], V.tensor_tensor, msq, mean, var, ALU.mult)
        for kc in range(8):
            O('dve', [t_xTc[kc], t_stat], [t_lnc[kc]], V.tensor_tensor, lns(kc, N)[0], xT[:, kc, 0:N], var, ALU.mult)
        for kc in range(8):
            O('pool', [t_lnc[kc], t_stat], [t_lnc[kc]], Pl.tensor_tensor, lns(kc, N)[0], lns(kc, N)[0], msq, ALU.subtract)
        for kc in range(8):
            O('act', [t_lnc[kc], lns(kc, N)[1], t_pp], [t_xTc[kc]], A.activation, xT[:, kc, 0:N], lns(kc, N)[0], AF.Identity,
              bias=pp[:, boff + kc:boff + kc + 1], scale=pp[:, goff + kc:goff + kc + 1])
        for kc in range(8):
            O('dve', [t_xTc[kc]], [t_xTbc[kc]], V.tensor_copy, xTb[:, kc, 0:N], xT[:, kc, 0:N])

    def head_norm_core(eps):
        sqv = E[:].rearrange("p c (h n) -> p (c h) n", n=64)
        O('dve', [t_O], [t_st], V.tensor_reduce, st1[:], Oall[:], AX.X, ALU.add)
        O('dve', [t_st], [t_st], V.tensor_scalar, st1[:], st1[:], -1.0 / 64, None, ALU.mult)
        O('dve', [t_O, t_st], [t_O], V.tensor_tensor, Oall[:], Oall[:], bc(st1[:].unsqueeze(2), [128, 8, 64]), ALU.add)
        O('pool', [t_O], [t_E], Pl.tensor_tensor, sqv, Oall[:], Oall[:], ALU.mult)
        O('dve', [t_E], [t_st], V.tensor_reduce, st2[:], sqv, AX.X, ALU.add)
        O('dve', [t_st], [t_st], V.tensor_scalar, st2[:], st2[:], 1.0 / 64, eps, ALU.mult, ALU.add)
        O('act', [t_st], [t_st], A.activation, st2[:], st2[:], AF.Ln)
        O('act', [t_st], [t_st], A.activation, st2[:], st2[:], AF.Exp, scale=-0.5)
        O('dve', [t_O, t_st], [t_O], V.tensor_tensor, Oall[:], Oall[:], bc(st2[:].unsqueeze(2), [128, 8, 64]), ALU.mult)

    def norm_out_T(goff, boff):
        Of = Oall[:].rearrange("p h v -> p (h v)")
        ps, tp = nps()
        for c in range(4):
            tr(ps[:, c * 128:(c + 1) * 128], Of[:, c * 128:(c + 1) * 128], ident, [t_O, t_const], [tp])
        for c in range(4):
            O('act', [tp, t_pp], [t_T1], A.activation, T1[:, c, :], ps[:, c * 128:(c + 1) * 128], AF.Identity,
              bias=pp[:, boff + c:boff + c + 1], scale=pp[:, goff + c:goff + c + 1])

    Xt = DBf[:, 0:1024].rearrange("p (s v) -> p s v", v=64)
    DBfr = DB[:].bitcast(F32R).rearrange("p a h t -> p (a h t)")
    Xtr = DBfr[:, 0:1024].rearrange("p (s v) -> p s v", v=64)

    def cross_select(lhsT, t_l, St, t_S, c, dst, t_dst):
        for j in range(2):
            ps, tp = nps()
            mm(ps[:], lhsT, St[:, c, 8 * j:8 * j + 8, :].rearrange("p s v -> p (s v)"), [t_l, t_S], [tp])
            O('dve', [tp, t_const], [t_Mb[0]], V.tensor_tensor, Xtr[:, 8 * j:8 * j + 8, :],
              ps[:].rearrange("p (s v) -> p s v", v=64),
              bc(cseq[:, 8 * j:8 * j + 8].unsqueeze(2), [128, 8, 64]), ALU.mult)
        O('dve', [t_Mb[0]], [t_dst], V.tensor_reduce, dst, Xt.rearrange("p s v -> p v s"), AX.X, ALU.add)

    def expand_seq(src, t_src, i, eng):
        dst = DBfr[:, i * 1024:(i + 1) * 1024].rearrange("p (s v) -> p s v", v=64)
        fn = V.tensor_tensor if eng == 'dve' else Pl.tensor_tensor
        O(eng, [t_src, t_const], [t_Mb[i]], fn, dst, bc(src.unsqueeze(1), [128, 16, 64]),
          bc(cseq[:].unsqueeze(2), [128, 16, 64]), ALU.mult)

    def exflat(i, j):
        return DBf[:, i * 1024 + j * 512:i * 1024 + (j + 1) * 512]

    def rwkv_tile(kind, gtile, ti, big, is_last_prompt):
        nseq = 1 if kind == 0 else 16
        C = 128 // nseq
        nlev = int(math.log2(C))
        cs = slice(ti * 128, (ti + 1) * 128)
        PR = big[:, 0:14, cs]
        PR4 = PR.rearrange("p j (s c) -> p j s c", s=nseq)
        Mx4 = Mx[:].rearrange("p j (s c) -> p j s c", s=nseq)
        St, t_S = (Hp, t_H) if kind == 0 else (Hs, t_H)
        triI, triS, mTs, mTi, mLs = (ckind[:, i, :] for i in range(5))
        mu_b = pp[:, PP_MU:PP_MU + 14]
        omu_b = pp[:, PP_OMU:PP_OMU + 14]
        O('act', [t_big], [t_plast], A.copy, plast[:, :, 0:nseq], PR4[:, :, :, C - 1])
        O('dve', [t_big, t_pp], [t_Mx], V.tensor_tensor, Mx4[:, :, :, 1:C], PR4[:, :, :, 0:C - 1],
          bc(mu_b.unsqueeze(2).unsqueeze(3), [128, 14, nseq, C - 1]), ALU.mult)
        O('pool', [t_prev, t_pp], [t_Mx], Pl.tensor_tensor, Mx4[:, :, :, 0], prevP[:, :, 0:nseq],
          bc(mu_b.unsqueeze(2), [128, 14, nseq]), ALU.mult)
        O('pool', [t_big, t_pp], [t_big], Pl.tensor_tensor, PR, PR, bc(omu_b.unsqueeze(2), [128, 14, 128]), ALU.mult)
        O('dve', [t_big, t_Mx], [t_big], V.tensor_tensor, PR, PR, Mx[:], ALU.add)
        if kind == 0:
            O('act', [t_plast], [t_prev], A.copy, prevP[:, :, 0:1], plast[:, :, 0:1])
        if kind == 1 or is_last_prompt:
            row0 = 0 if kind == 0 else 1
            for (jlo, jhi) in ((0, 8), (8, 14)):
                for j0 in range(jlo, jhi, 4):
                    ps, tp = nps()
                    nj = min(4, jhi - j0)
                    for j in range(nj):
                        tr(ps[0:nseq, j * 128:(j + 1) * 128], plast[:, j0 + j, 0:nseq], ident, [t_plast, t_const], [tp])
                    O('act', [tp], [t_stage], A.copy, stage[0:nseq, (j0 - jlo) * 128:(j0 - jlo + nj) * 128], ps[0:nseq, 0:nj * 128])
                out_evs.append(sy.dma('pool', [t_stage], [], shift_out[row0:row0 + nseq, jlo * 128:jhi * 128],
                                      stage[0:nseq, 0:(jhi - jlo) * 128]))
        Mr = big[:, 0:4, cs]
        Mk = big[:, 4:8, cs]
        Mv = big[:, 8:12, cs]
        M12 = big[:, 12, cs]
        M13 = big[:, 13, cs]
        if gtile in (0, 16):
            dbgdump(f"M{gtile}", PR, [128, 14, 128], [t_big])
        O('act', [t_big], [t_G], A.activation, G[0:64, :], M12[0:64, :], AF.Tanh)
        O('pool', [t_big], [t_G], Pl.tensor_copy, G[64:128, :], M12[64:128, :])
        O('act', [t_big], [t_sg], A.activation, sg[:], M13, AF.Sigmoid)
        ps, tp = nps()
        mm(ps[:], G[:], smallw[:, 0, :], [t_G, t_const], [tp])
        O('dve', [tp, t_const], [t_z], V.tensor_tensor, ztok[:], ps[:], tokb[:], ALU.add)
        O('act', [t_z], [t_z], A.activation, ztok[:], ztok[:], AF.Sigmoid)
        ps, tp = nps()
        for c in range(4):
            mm(ps[:, c * 128:(c + 1) * 128], ztok[:, c * 128:(c + 1) * 128], triI, [t_z, t_kc], [tp])
        O('act', [tp], [t_cum], A.copy, cum[:], ps[:].rearrange("p (c t) -> p c t", c=4))
        ps, tp = nps()
        for c in range(4):
            mm(ps[:, c * 128:(c + 1) * 128], ztok[:, c * 128:(c + 1) * 128], triS, [t_z, t_kc], [tp])
        O('dve', [tp], [t_cumx], V.tensor_copy, cumx[:], ps[:].rearrange("p (c t) -> p c t", c=4))
        ps, tp = nps()
        for c in range(4):
            mm(ps[:, c * 128:(c + 1) * 128], smallw[:, 1, c * 128:(c + 1) * 128], G[:], [t_G, t_const], [tp])
        for c in range(4):
            O('act', [tp, t_pp], [t_a], A.activation, a_t[:, c, :], ps[:, c * 128:(c + 1) * 128], AF.Sigmoid,
              bias=pp[:, PP_A0 + c:PP_A0 + c + 1], scale=1.0)
        ps, tp = nps()
        for c in range(4):
            mm(ps[:, c * 128:(c + 1) * 128], smallw[:, 2, c * 128:(c + 1) * 128], sg[:], [t_sg, t_const], [tp])
        O('act', [tp], [t_g], A.copy, gT[:], ps[:].rearrange("p (c t) -> p c t", c=4))
        b4 = lambda off: bc(pp[:, off:off + 4].unsqueeze(2), [128, 4, 128])
        f2 = lambda t: t[:].rearrange("p c t -> p (c t)")
        O('pool', [t_big, t_pp], [t_kk], Pl.tensor_tensor, kk[:], Mk, b4(PP_KK), ALU.mult)
        O('pool', [t_a, t_pp], [t_T1], Pl.tensor_tensor, T1[:], a_t[:], b4(PP_KA), ALU.mult)
        O('pool', [t_T1, t_pp], [t_T1], Pl.tensor_tensor, T1[:], T1[:], b4(PP_OKA), ALU.add)
        O('dve', [t_big, t_T1], [t_kp], V.tensor_tensor, kp[:], Mk, T1[:], ALU.mult)
        O('pool', [t_kk], [t_T1], Pl.tensor_tensor, T1[:], kk[:], kk[:], ALU.mult)
        ps, tp = nps()
        mm(ps[:], blkones, f2(T1), [t_T1, t_const], [tp])
        O('dve', [tp], [t_T1], V.tensor_scalar, f2(T1), ps[:], 1e-24, None, ALU.max)
        O('act', [t_T1], [t_T1], A.activation, f2(T1), f2(T1), AF.Ln)
        O('act', [t_T1], [t_T1], A.activation, f2(T1), f2(T1), AF.Exp, scale=-0.5)
        O('dve', [t_kk, t_T1], [t_kk], V.tensor_tensor, kk[:], kk[:], T1[:], ALU.mult)
        O('pool', [t_kk, t_a], [t_beta], Pl.tensor_tensor, beta[:], kk[:], a_t[:], ALU.mult)
        O('act', [t_cumx], [t_E], A.activation, E[:], cumx[:], AF.Exp)
        O('dve', [t_kk, t_E], [t_T1], V.tensor_tensor, T1[:], kk[:], E[:], ALU.mult)
        for hp in range(2):
            O('act' if hp else 'dve', [t_T1, t_const], [t_Az], *((A.activation, Az[:, :, hp, :], T1[:], AF.Identity) if hp else (V.tensor_scalar, Az[:, :, hp, :], T1[:], chm[:, 2 + hp:3 + hp], None, ALU.mult)), **(dict(scale=chm[:, 2 + hp:3 + hp]) if hp else dict()))
        O('act', [t_cum], [t_E], A.activation, E[:], cum[:], AF.Exp, scale=-1.0)
        O('dve', [t_beta, t_E], [t_Bt], V.tensor_tensor, Bt[:], beta[:], E[:], ALU.mult)
        O('pool', [t_kp, t_E], [t_Kt], Pl.tensor_tensor, Kt[:], kp[:], E[:], ALU.mult)
        O('act', [t_cum], [t_E], A.activation, E[:], cum[:], AF.Exp)
        O('dve', [t_big, t_E], [t_T1], V.tensor_tensor, T1[:], Mr, E[:], ALU.mult)
        for hp in range(2):
            O('act' if hp else 'dve', [t_T1, t_const], [t_Rz], *((A.activation, Rz[:, :, hp, :], T1[:], AF.Identity) if hp else (V.tensor_scalar, Rz[:, :, hp, :], T1[:], chm[:, hp:hp + 1], None, ALU.mult)), **(dict(scale=chm[:, hp:hp + 1]) if hp else dict()))
        cum4 = cum[:].rearrange("p c (s t) -> p c s t", s=nseq)
        E4 = E[:].rearrange("p c (s t) -> p c s t", s=nseq)
        O('dve', [t_cum], [t_E], V.tensor_tensor, E4, bc(cum4[:, :, :, C - 1:C], [128, 4, nseq, C]), cum4, ALU.subtract)
        O('act', [t_E], [t_E], A.activation, E[:], E[:], AF.Exp)
        O('act', [t_cum], [t_gC], A.activation, gC[:, :, 0:nseq], cum4[:, :, :, C - 1], AF.Exp)
        O('pool', [t_kp, t_E], [t_Mx], Pl.tensor_tensor, Mx[:, 0:4, :], kp[:], E[:], ALU.mult)
        O('dve', [t_beta, t_E], [t_Mx], V.tensor_tensor, Mx[:, 4:8, :], beta[:], E[:], ALU.mult)
        O('pool', [t_big, t_kp], [t_Mx], Pl.tensor_tensor, Mx[:, 8:12, :], Mr, kp[:], ALU.mult)
        O('pool', [t_Mx, t_pp], [t_Mx], Pl.tensor_tensor, Mx[:, 8:12, :], Mx[:, 8:12, :], b4(PP_RK), ALU.mult)
        for (src_lo, dstz, t_d) in ((0, Khz, t_Khz), (4, Bhz, t_Bhz)):
            ps, tp = nps()
            for c in range(4):
                tr(ps[:, c * 128:(c + 1) * 128], Mx[:, src_lo + c, :], ident, [t_Mx, t_const], [tp])
            psv = ps[:].rearrange("p (c f) -> p c f", c=4)
            for hp in range(2):
                e = 'act' if hp == 0 else 'dve'
                fn = A.copy if hp == 0 else V.tensor_copy
                O(e, [tp], [t_d], fn, dstz[:, :, hp, hp * 64:(hp + 1) * 64], psv[:, :, hp * 64:(hp + 1) * 64])
        ps, tp = nps()
        for c in range(4):
            tr(ps[:, c * 128:(c + 1) * 128], big[:, 8 + c, cs], ident, [t_big, t_const], [tp])
        O('act', [tp], [t_V], A.copy, Vtok[:], ps[:])
        ps, tp = nps()
        mm(ps[:], blkones, Mx[:, 8:12, :].rearrange("p c t -> p (c t)"), [t_Mx, t_const], [tp])
        O('dve', [tp, t_big], [t_Mx], V.tensor_tensor, Mx[:, 8:12, :], ps[:].rearrange("p (c t) -> p c t", c=4), Mv, ALU.mult)
        Mb = [DB[:, 0], DB[:, 1]]
        MTb = [DB[:, 2], DB[:, 3]]
        PT = DB[:, 4]
        DBr = DB[:].bitcast(F32R)
        Mbr = [DBr[:, 0], DBr[:, 1]]
        MTbr = [DBr[:, 2], DBr[:, 3]]
        PTr = DBr[:, 4]
        Wtr = Wt[:].bitcast(F32R)
        LakT, NrbT, NrkT = LK[:, 0], LK[:, 1], LK[:, 2]
        v4 = lambda ps_: ps_[:].rearrange("p (h t) -> p h t", h=4)
        mk = lambda m: bc(m.unsqueeze(1), [128, 4, 128])
        hd = lambda hh: [(2 * hh + (hl // 2), hl % 2, 4 * hh + hl, hl) for hl in range(4)]
        for hh in range(2):
            hs = slice(4 * hh, 4 * hh + 4)
            psA, tpA = nps()
            psB, tpB = nps()
            for (c, hp, h, hl) in hd(hh):
                mm(psA[:, hl * 128:(hl + 1) * 128], Bt[:, c, :], Az[:, c, hp, :], [t_Bt, t_Az], [tpA])
            for (c, hp, h, hl) in hd(hh):
                mm(psB[:, hl * 128:(hl + 1) * 128], Az[:, c, hp, :], Bt[:, c, :], [t_Bt, t_Az], [tpB])
            O('dve', [tpA, t_kc], [t_MTb[0]], V.tensor_tensor, MTbr[0][:, hs, :], v4(psA), mk(mTs), ALU.mult)
            O('dve', [tpB, t_kc], [t_Mb[0]], V.tensor_tensor, Mbr[0][:, hs, :], v4(psB), mk(mLs), ALU.mult)
            O('pool', [t_MTb[0], t_const], [t_PT], Pl.tensor_tensor, PTr[:, hs, :], MTb[0][:, hs, :], mk(ident), ALU.add)
        cur = 0
        for lev in range(1, nlev):
            nxt = 1 - cur
            need_mt = lev < nlev - 1
            pM = [nps(), nps()]
            for h in range(8):
                mm(pM[h // 4][0][:, (h % 4) * 128:(h % 4 + 1) * 128], MTbr[cur][:, h, :], Mbr[cur][:, h, :],
                   [t_MTb[cur], t_Mb[cur]], [pM[h // 4][1]])
            if need_mt:
                pMT = [nps(), nps()]
                for h in range(8):
                    mm(pMT[h // 4][0][:, (h % 4) * 128:(h % 4 + 1) * 128], Mbr[cur][:, h, :], MTbr[cur][:, h, :],
                       [t_MTb[cur], t_Mb[cur]], [pMT[h // 4][1]])
            O('act', [pM[0][1]], [t_Mb[nxt]], A.copy, Mbr[nxt][:, 0:4, :], v4(pM[0][0]))
            O('dve', [pM[1][1]], [t_Mb[nxt]], V.tensor_copy, Mbr[nxt][:, 4:8, :], v4(pM[1][0]))
            pP = [nps(), nps()]
            for h in range(8):
                mm(pP[h // 4][0][:, (h % 4) * 128:(h % 4 + 1) * 128], Mbr[nxt][:, h, :], PTr[:, h, :],
                   [t_Mb[nxt], t_PT], [pP[h // 4][1]])
            if need_mt:
                O('act', [pMT[0][1]], [t_MTb[nxt]], A.copy, MTbr[nxt][:, 0:4, :], v4(pMT[0][0]))
                O('dve', [pMT[1][1]], [t_MTb[nxt]], V.tensor_copy, MTbr[nxt][:, 4:8, :], v4(pMT[1][0]))
            for q in range(2):
                O('dve', [pP[q][1], t_PT], [t_PT], V.tensor_tensor, PTr[:, 4 * q:4 * q + 4, :], v4(pP[q][0]),
                  PT[:, 4 * q:4 * q + 4, :], ALU.add)
            cur = nxt
        for hh in range(2):
            heads = hd(hh)
            prods = [(Kt, t_Kt, Az, t_Az, LakT, mTs), (Bt, t_Bt, Rz, t_Rz, NrbT, mTi), (Kt, t_Kt, Rz, t_Rz, NrkT, mTi)]
            pss = []
            for (L_, tl, R_, tr_, dst_, msk_) in prods:
                ps, tp = nps()
                for (c, hp, h, hl) in heads:
                    mm(ps[:, hl * 128:(hl + 1) * 128], L_[:, c, :], R_[:, c, hp, :], [tl, tr_], [tp])
                pss.append((ps, tp))
            for (ps, tp), (L_, tl, R_, tr_, dst_, msk_) in zip(pss, prods):
                O('dve', [tp, t_kc], [t_LK], V.tensor_tensor, dst_, v4(ps), mk(msk_), ALU.mult)
            Wv = Wtr.rearrange("p h v -> p (h v)")
            Xcf = Xc[:].rearrange("p h v -> p (h v)")
            if kind == 1:
                for (c, hp, h, hl) in heads:
                    cross_select(Az[:, c, hp, :], t_Az, St, t_S, c, Xc[:, hl, :], t_Xc)
            psW, tpW = nps()
            for (c, hp, h, hl) in heads:
                o_ = psW[:, hl * 64:(hl + 1) * 64]
                if kind == 0:
                    mm(o_, Az[:, c, hp, :], St[:, c, :], [t_Az, t_S], [tpW], start=True, stop=False)
                mm(o_, LakT[:, hl, :], Vtok[:, h * 64:(h + 1) * 64], [t_LK, t_V], [tpW], start=(kind == 1), stop=True)
            if kind == 0:
                O('act', [tpW], [t_W], A.copy, Wv, psW[:, 0:256])
            else:
                O('dve', [tpW, t_Xc], [t_W], V.tensor_tensor, Wv, psW[:, 0:256], Xcf, ALU.add)
            psU, tpU = nps()
            for (c, hp, h, hl) in heads:
                mm(psU[:, hl * 64:(hl + 1) * 64], PTr[:, h, :], Wtr[:, hl, :], [t_PT, t_W], [tpU])
            O('act', [tpU], [t_U], A.copy, Uall[:, 4 * hh:4 * hh + 4, :].rearrange("p h v -> p (h v)"), psU[:, 0:256])
            if kind == 1:
                for (c, hp, h, hl) in heads:
                    cross_select(Rz[:, c, hp, :], t_Rz, St, t_S, c, Xc[:, hl, :], t_Xc)
            psO, tpO = nps()
            for (c, hp, h, hl) in heads:
                o_ = psO[:, hl * 64:(hl + 1) * 64]
                if kind == 0:
                    mm(o_, Rz[:, c, hp, :], St[:, c, :], [t_Rz, t_S], [tpO], start=True, stop=False)
                mm(o_, NrbT[:, hl, :], Uall[:, h, :], [t_LK, t_U], [tpO], start=(kind == 1), stop=False)
                mm(o_, NrkT[:, hl, :], Vtok[:, h * 64:(h + 1) * 64], [t_LK, t_V], [tpO], start=False, stop=True)
            Ov = Oall[:, 4 * hh:4 * hh + 4, :].rearrange("p h v -> p (h v)")
            if kind == 0:
                O('act', [tpO], [t_O], A.copy, Ov, psO[:, 0:256])
            else:
                O('dve', [tpO, t_Xc], [t_O], V.tensor_tensor, Ov, psO[:, 0:256], Xcf, ALU.add)
            for cc in range(2):
                c = 2 * hh + cc
                if kind == 0:
                    psH, tpH = nps()
                    for hp in range(2):
                        h = 2 * c + hp
                        mm(psH[:, 0:64], Bhz[:, c, hp, :], Uall[:, h, :], [t_Bhz, t_U], [tpH], start=(hp == 0), stop=False)
                        mm(psH[:, 0:64], Khz[:, c, hp, :], Vtok[:, h * 64:(h + 1) * 64], [t_Khz, t_V], [tpH],
                           start=False, stop=(hp == 1))
                    O('dve', [t_S, t_gC, tpH], [t_S], V.scalar_tensor_tensor, St[:, c, :], St[:, c, :], gC[:, c, 0:1],
                      psH[:, 0:64], ALU.mult, ALU.add)
                else:
                    phs = [nps(), nps()]
                    for hp in range(2):
                        h = 2 * c + hp
                        expand_seq(Uall[:, h, :], t_U, 0, 'pool')
                        expand_seq(Vtok[:, h * 64:(h + 1) * 64], t_V, 1, 'dve')
                        for j in range(2):
                            psH, tpH = phs[j]
                            mm(psH[:], Bhz[:, c, hp, :], exflat(0, j), [t_Bhz, t_Mb[0]], [tpH], start=(hp == 0), stop=False)
                            mm(psH[:], Khz[:, c, hp, :], exflat(1, j), [t_Khz, t_Mb[1]], [tpH], start=False, stop=(hp == 1))
                    O('pool', [t_S, t_gC], [t_S], Pl.tensor_tensor, St[:, c], St[:, c],
                      bc(gC[:, c, :].unsqueeze(2), [128, 16, 64]), ALU.mult)
                    for j in range(2):
                        psH, tpH = phs[j]
                        O('dve', [t_S, tpH], [t_S], V.tensor_tensor, St[:, c, 8 * j:8 * j + 8, :], St[:, c, 8 * j:8 * j + 8, :],
                          psH[:].rearrange("p (s v) -> p s v", v=64), ALU.add)
        if gtile in (0, 1, 16):
            dbgdump(f"Oall{gtile}", Oall[:], [128, 8, 64], [t_O])
        head_norm_core(A_GN_EPS)
        norm_out_T(PP_LXG, PP_LXB)
        O('pool', [t_T1, t_Mx], [t_T1], Pl.tensor_tensor, T1[:], T1[:], Mx[:, 8:12, :], ALU.add)
        O('dve', [t_T1, t_g], [t_yT], V.tensor_tensor, yT[:, 0:4, cs], T1[:], gT[:], ALU.mult)

    def ret_tile(kind, gtile, ti, big):
        cs = slice(ti * 128, (ti + 1) * 128)
        St, t_S = (Rp, t_R) if kind == 0 else (Rs, t_R)
        Pq = big[:, 14:18, cs]
        Pk = big[:, 18:22, cs]
        Pg = big[:, 26:30, cs]
        sy.dma('sp', [], [t_rope], rope[:], c_rope_d[gtile])
        Qr, t_Qr = Bt, t_Bt
        Kr, t_Kr = Kt, t_Kt
        Qrz, t_Qrz = Az, t_Az
        Qdz, t_Qdz = Rz, t_Rz
        Kdz, t_Kdz = Khz, t_Khz
        for (src, dst, t_d, ci) in ((Pq, Qr, t_Qr, 0), (Pk, Kr, t_Kr, 2)):
            ps, tp = nps()
            for c in range(4):
                mm(ps[:, c * 128:(c + 1) * 128], rotT, src[:, c, :], [t_big, t_const], [tp])
            O('pool', [t_big, t_rope], [t_T1], Pl.tensor_tensor, T1[:], src, bc(rope[:, ci, :].unsqueeze(1), [128, 4, 128]), ALU.mult)
            O('dve', [tp, t_rope], [t_E], V.tensor_tensor, E[:], ps[:].rearrange("p (c t) -> p c t", c=4),
              bc(rope[:, ci + 1, :].unsqueeze(1), [128, 4, 128]), ALU.mult)
            O('pool', [t_T1, t_E], [t_d], Pl.tensor_tensor, dst[:], T1[:], E[:], ALU.add)
        for hp in range(2):
            O('act' if hp else 'dve', [t_Qr, t_const], [t_Qrz], *((A.activation, Qrz[:, :, hp, :], Qr[:], AF.Identity) if hp else (V.tensor_scalar, Qrz[:, :, hp, :], Qr[:], chm[:, hp:hp + 1], None, ALU.mult)), **(dict(scale=chm[:, hp:hp + 1]) if hp else dict()))
        O('pool', [t_Qr, t_kc], [t_T1], Pl.tensor_tensor, T1[:], Qr[:], cdec[:, 0], ALU.mult)
        for hp in range(2):
            O('act' if hp else 'dve', [t_T1, t_const], [t_Qdz], *((A.activation, Qdz[:, :, hp, :], T1[:], AF.Identity) if hp else (V.tensor_scalar, Qdz[:, :, hp, :], T1[:], chm[:, hp:hp + 1], None, ALU.mult)), **(dict(scale=chm[:, hp:hp + 1]) if hp else dict()))
        O('pool', [t_Kr, t_kc], [t_Mx], Pl.tensor_tensor, Mx[:, 0:4, :], Kr[:], cdec[:, 1], ALU.mult)
        ps, tp = nps()
        for c in range(4):
            tr(ps[:, c * 128:(c + 1) * 128], Mx[:, c, :], ident, [t_Mx, t_const], [tp])
        psv = ps[:].rearrange("p (c f) -> p c f", c=4)
        for hp in range(2):
            e = 'act' if hp == 0 else 'dve'
            fn = A.copy if hp == 0 else V.tensor_copy
            O(e, [tp], [t_Kdz], fn, Kdz[:, :, hp, hp * 64:(hp + 1) * 64], psv[:, :, hp * 64:(hp + 1) * 64])
        ps, tp = nps()
        for c in range(4):
            tr(ps[:, c * 128:(c + 1) * 128], big[:, 22 + c, cs], ident, [t_big, t_const], [tp])
        O('act', [tp], [t_V], A.copy, Vtok[:], ps[:])
        O('act', [t_big], [t_Mx], A.activation, Mx[:, 4:8, :], Pg, AF.Sigmoid)
        O('dve', [t_big, t_Mx], [t_Mx], V.tensor_tensor, Mx[:, 4:8, :], Mx[:, 4:8, :], Pg, ALU.mult)
        scT = LK[:, 0]
        Xcf = Xc[:].rearrange("p h v -> p (h v)")
        for hh in range(2):
            heads = [(2 * hh + (hl // 2), hl % 2, 4 * hh + hl, hl) for hl in range(4)]
            ps, tp = nps()
            for (c, hp, h, hl) in heads:
                mm(ps[:, hl * 128:(hl + 1) * 128], Kr[:, c, :], Qrz[:, c, hp, :], [t_Kr, t_Qrz], [tp])
            O('dve', [tp, t_kc], [t_LK], V.tensor_tensor, scT, ps[:].rearrange("p (h t) -> p h t", h=4),
              cdmask[:, 4 * hh:4 * hh + 4, :], ALU.mult)
            if kind == 1:
                for (c, hp, h, hl) in heads:
                    cross_select(Qdz[:, c, hp, :], t_Qdz, St, t_S, c, Xc[:, hl, :], t_Xc)
            psO, tpO = nps()
            for (c, hp, h, hl) in heads:
                o_ = psO[:, hl * 64:(hl + 1) * 64]
                if kind == 0:
                    mm(o_, Qdz[:, c, hp, :], St[:, c, :], [t_Qdz, t_S], [tpO], start=True, stop=False)
                mm(o_, scT[:, hl, :], Vtok[:, h * 64:(h + 1) * 64], [t_LK, t_V], [tpO], start=(kind == 1), stop=True)
            Ov = Oall[:, 4 * hh:4 * hh + 4, :].rearrange("p h v -> p (h v)")
            if kind == 0:
                O('act', [tpO], [t_O], A.copy, Ov, psO[:, 0:256])
            else:
                O('dve', [tpO, t_Xc], [t_O], V.tensor_tensor, Ov, psO[:, 0:256], Xcf, ALU.add)
            for cc in range(2):
                c = 2 * hh + cc
                if kind == 0:
                    psH, tpH = nps()
                    for hp in range(2):
                        h = 2 * c + hp
                        mm(psH[:, 0:64], Kdz[:, c, hp, :], Vtok[:, h * 64:(h + 1) * 64], [t_Kdz, t_V], [tpH],
                           start=(hp == 0), stop=(hp == 1))
                    O('dve', [t_S, t_kc, tpH], [t_S], V.scalar_tensor_tensor, St[:, c, :], St[:, c, :],
                      ccdec[:, c:c + 1], psH[:, 0:64], ALU.mult, ALU.add)
                else:
                    phs = [nps(), nps()]
                    for hp in range(2):
                        h = 2 * c + hp
                        expand_seq(Vtok[:, h * 64:(h + 1) * 64], t_V, hp, 'dve' if hp else 'pool')
                        for j in range(2):
                            psH, tpH = phs[j]
                            mm(psH[:], Kdz[:, c, hp, :], exflat(hp, j), [t_Kdz, t_Mb[hp]], [tpH], start=(hp == 0), stop=(hp == 1))
                    O('dve', [t_S, t_kc], [t_S], V.tensor_scalar, St[:, c], St[:, c], ccdec[:, c:c + 1], None, ALU.mult)
                    for j in range(2):
                        psH, tpH = phs[j]
                        O('dve', [t_S, tpH], [t_S], V.tensor_tensor, St[:, c, 8 * j:8 * j + 8, :], St[:, c, 8 * j:8 * j + 8, :],
                          psH[:].rearrange("p (s v) -> p s v", v=64), ALU.add)
        if gtile in (0, 1, 16):
            dbgdump(f"Oret{gtile}", Oall[:], [128, 8, 64], [t_O])
        head_norm_core(B_GN_EPS)
        norm_out_T(PP_GNG, PP_GNB)
        O('dve', [t_T1, t_Mx], [t_yT], V.tensor_tensor, yT[:, 4:8, cs], T1[:], Mx[:, 4:8, :], ALU.mult)

    def state_out(kind):
        row0 = 0 if kind == 0 else 1
        nseq = 1 if kind == 0 else 16
        Hst = Hp if kind == 0 else Hs
        Rst = Rp if kind == 0 else Rs
        for c in range(4):
            for hp in range(2):
                h = 2 * c + hp
                if kind == 0:
                    out_evs.append(sy.dma('pool', [t_R], [], ret_out[0, h], Rst[hp * 64:(hp + 1) * 64, c, :]))
                else:
                    out_evs.append(sy.dma('pool', [t_R], [], ret_out[1:17, h].rearrange("s k v -> k s v"),
                                          Rst[hp * 64:(hp + 1) * 64, c, :, :]))
        for s0 in range(0, nseq, 2):
            ns = min(2, nseq - s0)
            pa = [nps(), nps()]
            for si in range(ns):
                for c in range(4):
                    src = Hst[:, c, :] if kind == 0 else Hst[:, c, s0 + si, :]
                    tr(pa[si][0][0:64, c * 128:(c + 1) * 128], src, ident, [t_H, t_const], [pa[si][1]])
            O('act', [pa[0][1]], [t_stage], A.copy, stage[0:64, 0:512], pa[0][0][0:64, :])
            if ns == 2:
                O('dve', [pa[1][1]], [t_stage], V.tensor_copy, stage[0:64, 512:1024], pa[1][0][0:64, :])
            for si in range(ns):
                out_evs.append(sy.dma('pool', [t_stage], [], wkv_out[row0 + s0 + si].rearrange("h v k -> v h k"),
                                      stage[0:64, si * 512:(si + 1) * 512].rearrange("v (h k) -> v h k", k=64)))

    def load_sample_states():
        sy.dma('sp', [], [t_stage], stage[0:16, 0:1024], st_shift[:, 0:1024])
        for j0 in (0, 4):
            ps, tp = nps()
            for j in range(4):
                tr(ps[:, j * 16:(j + 1) * 16], stage[0:16, (j0 + j) * 128:(j0 + j + 1) * 128], ident[0:16, 0:16], [t_stage, t_const], [tp])
            O('act', [tp], [t_prev], A.copy, prevP[:, j0:j0 + 4, :], ps[:, 0:64].rearrange("p (j s) -> p j s", s=16))
        sy.dma('sp', [], [t_stage], stage[0:16, 0:768], st_shift[:, 1024:1792])
        ps, tp = nps()
        for j in range(6):
            tr(ps[:, j * 16:(j + 1) * 16], stage[0:16, j * 128:(j + 1) * 128], ident[0:16, 0:16], [t_stage, t_const], [tp])
        O('act', [tp], [t_prev], A.copy, prevP[:, 8:14, :], ps[:, 0:96].rearrange("p (j s) -> p j s", s=16))
        sy.dma('sp', [], [t_stage], stage[0:32, :], st_conv)
        ps, tp = nps()
        for kc in range(8):
            tr(ps[:, kc * 32:(kc + 1) * 32], stage[0:32, kc * 128:(kc + 1) * 128], ident[0:32, 0:32], [t_stage, t_const], [tp])
        O('act', [tp], [t_prevc], A.copy, prevc[:].rearrange("p c s j -> p c (s j)"), ps[:, 0:256].rearrange("p (c r) -> p c r", c=8))
        for c in range(4):
            for hp in range(2):
                sy.dma('sp', [], [t_R], Rs[hp * 64:(hp + 1) * 64, c, :, :], st_ret[:, 2 * c + hp].rearrange("s k v -> k s v"))
        for s0 in range(0, 16, 2):
            sy.dma('sp', [], [t_stage], stage[0:64, :].rearrange("v (s h k) -> v s h k", s=2, h=8),
                   st_wkv[s0:s0 + 2].rearrange("s h v k -> v s h k"))
            for si in range(2):
                ps, tp = nps()
                for c in range(4):
                    tr(ps[:, c * 64:(c + 1) * 64], stage[0:64, si * 512 + c * 128:si * 512 + (c + 1) * 128], ident[0:64, 0:64],
                       [t_stage, t_const], [tp])
                O('act' if si == 0 else 'dve', [tp], [t_H], A.copy if si == 0 else V.tensor_copy, Hs[:, :, s0 + si, :],
                  ps[:, 0:256].rearrange("p (c v) -> p c v", c=4))

    groups = []
    t0 = 0
    while t0 < SEQ:
        groups.append((0, t0, GT))
        t0 += GT
    groups.append((1, SEQ, 128))
    if ngroups_limit is not None:
        groups = groups[:ngroups_limit]
    if only_sample:
        groups = groups[-1:]
    cur_kind = None

    def load_x(gidx):
        k_, t0_, N_ = groups[gidx]
        for ti_ in range(N_ // 128):
            xs_, t_xs = XS[ti_]
            sy.dma('sp', [], [t_xs], xs_, xin[t0_ + ti_ * 128:t0_ + (ti_ + 1) * 128, :])

    for gi_, (kind, t0, N) in enumerate(groups):
        ntile = N // 128
        nseq = 1 if kind == 0 else 16
        C = N // nseq
        if kind != cur_kind:
            if kind == 1:
                sy.barrier()
            load_kind_consts(kind)
            cur_kind = kind
        if kind == 0:
            big = bigflat[:].rearrange("p (s t) -> p s t", t=GT)
            hT = bigbf[:, 0:32 * GT].rearrange("p (s t) -> p s t", t=GT)
        else:
            big = bigflat[:, 0:NSLOT * 128].rearrange("p (s t) -> p s t", t=128)
            hT = bigbf[:, 0:32 * 128].rearrange("p (s t) -> p s t", t=128)
            load_sample_states()
        if gi_ == 0:
            load_x(0)
        for ti in range(ntile):
            xs_, t_xs = XS[ti]
            for hb in range(2):
                ps, tp = nps()
                for c in range(4):
                    tr(ps[:, c * 128:(c + 1) * 128], xs_[:, (hb * 4 + c) * 128:(hb * 4 + c + 1) * 128], ident, [t_xs, t_const], [tp])
                O('act' if hb == 0 else 'dve', [tp], t_xTc[hb * 4:hb * 4 + 4], A.copy if hb == 0 else V.tensor_copy,
                  xT[:, hb * 4:hb * 4 + 4, ti * 128:(ti + 1) * 128], ps[:].rearrange("p (c t) -> p c t", c=4))

        O('act', t_xTc, t_xTbc, A.copy, xTb[:, :, 0:N], xT[:, :, 0:N])

        def cb_proj(oc, ps, tps):
            e = 'act' if oc % 2 == 0 else 'dve'
            O(e, [tps], [t_big], A.copy if e == 'act' else V.tensor_copy, big[:, oc, 0:N], ps)

        def cb_res(oc, ps, tps):
            O('dve', [t_xTc[oc], tps], [t_xTc[oc]], V.scalar_tensor_tensor, xT[:, oc, 0:N], xT[:, oc, 0:N], ALPHA, ps, ALU.mult, ALU.add)
            ln_square(oc, N)

        def cb_up(oc, ps, tps):
            sc_ = big[:, 16 + oc % 8, 0:N]
            O('act', [tps], [t_big], A.activation, sc_, ps, AF.Relu)
            O('pool' if oc % 2 else 'dve', [t_big], [t_big], (Pl if oc % 2 else V).tensor_tensor, hT[:, oc, 0:N], sc_, sc_, ALU.mult)

        def mlp(l):
            dense(Wb['w_up'][l], t_Wb['w_up'], D, 4 * D, lambda kc: xTb[:, kc, 0:N], lambda kc: [t_xTbc[kc]], N, cb_up)
            dense(Wb['w_down'][l], t_Wb['w_down'], 4 * D, D, lambda kc: hT[:, kc, 0:N], lambda kc: [t_big], N, cb_res)
            layer_norm(N, PP_LN + 32 + 8 * l, PP_LN + 48 + 8 * l, big)

        dense(Wb['w_in_ab'], t_Wb['w_in_ab'], D, 3840, lambda kc: xTb[:, kc, 0:N], lambda kc: [t_xTbc[kc]], N, cb_proj)
        if gi_ == 0:
            dbgdump("proj", big[:, 0:30, 0:N], [128, 30, N], [t_big])
        for ti in range(ntile):
            gtile = t0 // 128 + ti
            rwkv_tile(kind, gtile, ti, big, is_last_prompt=(kind == 0 and gtile == SEQ // 128 - 1))
            ret_tile(kind, gtile, ti, big)
        if kind == 1 or (t0 + N == SEQ):
            state_out(kind)
        if gi_ + 1 < len(groups):
            load_x(gi_ + 1)
        dense(Wb['w_out_ab'], t_Wb['w_out_ab'], D, D, lambda kc: yT[:, kc, 0:N], lambda kc: [t_yT], N, cb_res)
        layer_norm(N, PP_LN + 0, PP_LN + 16, big)
        if gi_ == 0:
            dbgdump("x1", xT[:, :, 0:N], [128, 8, N], t_xTc)
        mlp(0)
        if gi_ == 0:
            dbgdump("x2", xT[:, :, 0:N], [128, 8, N], t_xTc)
        dense(Wb['w_in_conv'], t_Wb['w_in_conv'], D, 3 * D, lambda kc: xTb[:, kc, 0:N], lambda kc: [t_xTbc[kc]], N, cb_proj)
        bg = big[:, 0:8, 0:N]
        u = big[:, 8:16, 0:N]
        hh_ = big[:, 16:24, 0:N]
        tmp = big[:, 24:32, 0:N]
        O('pool', [t_big], [t_big], Pl.tensor_tensor, u, u, hh_, ALU.mult)
        u4 = u.rearrange("p c (s t) -> p c s t", s=nseq)
        acc4 = hh_.rearrange("p c (s t) -> p c s t", s=nseq)
        tmp4 = tmp.rearrange("p c (s t) -> p c s t", s=nseq)

        def cw(j, shp):
            a_ = pp[:, PP_CW + 8 * j:PP_CW + 8 * j + 8].unsqueeze(2)
            if len(shp) == 4:
                a_ = a_.unsqueeze(3)
            return bc(a_, shp)
        O('act', [t_big], [t_clast], A.copy, clast[:, :, 0:nseq, :], u4[:, :, :, C - 2:C])
        O('dve', [t_big, t_pp], [t_big], V.tensor_tensor, acc4, u4, cw(2, [128, 8, nseq, C]), ALU.mult)
        O('pool', [t_big, t_pp], [t_big], Pl.tensor_tensor, tmp4[:, :, :, 1:C], u4[:, :, :, 0:C - 1], cw(1, [128, 8, nseq, C - 1]), ALU.mult)
        O('dve', [t_big], [t_big], V.tensor_tensor, acc4[:, :, :, 1:C], acc4[:, :, :, 1:C], tmp4[:, :, :, 1:C], ALU.add)
        O('pool', [t_big, t_pp], [t_big], Pl.tensor_tensor, tmp4[:, :, :, 2:C], u4[:, :, :, 0:C - 2], cw(0, [128, 8, nseq, C - 2]), ALU.mult)
        O('dve', [t_big], [t_big], V.tensor_tensor, acc4[:, :, :, 2:C], acc4[:, :, :, 2:C], tmp4[:, :, :, 2:C], ALU.add)
        pc = prevc[:, :, 0:nseq, :]
        O('pool', [t_prevc, t_pp], [t_big], Pl.tensor_tensor, tmp4[:, :, :, 0], pc[:, :, :, 1], cw(1, [128, 8, nseq]), ALU.mult)
        O('dve', [t_big], [t_big], V.tensor_tensor, acc4[:, :, :, 0], acc4[:, :, :, 0], tmp4[:, :, :, 0], ALU.add)
        O('pool', [t_prevc, t_pp], [t_big], Pl.tensor_tensor, tmp4[:, :, :, 0], pc[:, :, :, 0], cw(0, [128, 8, nseq]), ALU.mult)
        O('dve', [t_big], [t_big], V.tensor_tensor, acc4[:, :, :, 0], acc4[:, :, :, 0], tmp4[:, :, :, 0], ALU.add)
        O('pool', [t_prevc, t_pp], [t_big], Pl.tensor_tensor, tmp4[:, :, :, 1], pc[:, :, :, 1], cw(0, [128, 8, nseq]), ALU.mult)
        O('dve', [t_big], [t_big], V.tensor_tensor, acc4[:, :, :, 1], acc4[:, :, :, 1], tmp4[:, :, :, 1], ALU.add)
        if kind == 0:
            O('act', [t_clast], [t_prevc], A.copy, prevc[:, :, 0:1, :], clast[:, :, 0:1, :])
        O('dve', [t_big], [t_yT], V.tensor_tensor, yT[:, :, 0:N], bg, hh_, ALU.mult)
        if kind == 1 or (t0 + N == SEQ):
            row0 = 0 if kind == 0 else 2
            nr = 2 * nseq
            pa = [nps(), nps()]
            for kc in range(8):
                pp_, tpp = pa[kc // 4]
                tr(pp_[0:nr, (kc % 4) * 128:(kc % 4 + 1) * 128], clast[:, kc, 0:nseq, :].rearrange("p s j -> p (s j)"), ident,
                   [t_clast, t_const], [tpp])
            O('act', [pa[0][1]], [t_stage], A.copy, stage[0:nr, 0:512], pa[0][0][0:nr, :])
            O('dve', [pa[1][1]], [t_stage], V.tensor_copy, stage[0:nr, 512:1024], pa[1][0][0:nr, :])
            out_evs.append(sy.dma('pool', [t_stage], [], conv_out[row0:row0 + nr, :], stage[0:nr, :]))
        dense(Wb['w_out_conv'], t_Wb['w_out_conv'], D, D, lambda kc: yT[:, kc, 0:N], lambda kc: [t_yT], N, cb_res)
        layer_norm(N, PP_LN + 8, PP_LN + 24, big)
        mlp(1)
        for ti in range(ntile):
            for hb in range(2):
                ps, tp = nps()
                for c in range(4):
                    tr(ps[:, c * 128:(c + 1) * 128], xT[:, hb * 4 + c, ti * 128:(ti + 1) * 128], ident, [t_xTc[hb * 4 + c], t_const], [tp])
                O('act' if hb == 0 else 'dve', [tp], [OS[ti][1]], A.copy if hb == 0 else V.tensor_copy,
                  OS[ti][0][:, hb * 512:(hb + 1) * 512], ps[:])
            out_evs.append(sy.dma('pool', [OS[ti][1]], [], yout[t0 + ti * 128:t0 + (ti + 1) * 128, :], OS[ti][0][:, 0:1024]))
    for ev in out_evs:
        sy._wait('sp', ev)
    sy.barrier()
    print('sbuf bytes remaining', nc.sbuf_bytes_remaining, sy.cnt, sy.nwait, flush=True)
    return nc, sy


_PROG = {}


def _prep_inputs(inp):
    f = lambda a: np.ascontiguousarray(np.asarray(a, dtype=np.float32))
    consts = host_consts()
    col = lambda v, n: f(v).reshape(n, 128).T
    pp = np.concatenate([
        col(inp['mu_a'][0], 14), col(inp['k_k'][0], 4), col(inp['k_a'][0], 4), col(np.asarray(inp['r_k'][0]).reshape(512), 4),
        col(inp['a0'][0], 4),
        col(inp['ln1_g'][0], 8), col(inp['ln1_g'][1], 8), col(inp['ln1_b'][0], 8), col(inp['ln1_b'][1], 8),
        col(inp['ln2_g'][0], 8), col(inp['ln2_g'][1], 8), col(inp['ln2_b'][0], 8), col(inp['ln2_b'][1], 8),
        col(inp['conv_w'][0][0], 8), col(inp['conv_w'][0][1], 8), col(inp['conv_w'][0][2], 8),
        col(inp['lnx_g'][0], 4), col(inp['lnx_b'][0], 4), col(inp['gn_g'][0], 4), col(inp['gn_b'][0], 4)], axis=1)
    assert pp.shape == (128, PP_N)
    rowb = lambda v: np.broadcast_to(f(v).reshape(1, 512), (128, 512))
    tokb = rowb(inp['w0'][0])
    z64 = np.zeros((64, 512), np.float32)
    smallw = np.stack([np.concatenate([f(inp['w2'][0]), z64], 0), np.concatenate([z64, f(inp['a2'][0])], 0), f(inp['g2'][0])], 1)
    shared = dict(pp=f(pp), tokb=f(tokb), smallw=f(smallw))
    shared.update({k: f(v) for k, v in consts.items()})
    shared['w_in_ab'] = f(inp['w_in_ab'][0])
    shared['w_out_ab'] = f(inp['w_out_ab'][0])
    shared['w_up'] = f(inp['w_up'])
    shared['w_down'] = f(inp['w_down'])
    shared['w_in_conv'] = f(inp['w_in_conv'][0])
    shared['w_out_conv'] = f(inp['w_out_conv'][0])
    xp = f(inp['x_prompt'])
    xs = f(inp['x_sample'])
    in_maps = []
    for c in range(NCORE):
        sl = slice(SB_PER * c, SB_PER * (c + 1))
        m = dict(shared)
        m['xin'] = np.concatenate([xp[c], xs[sl].reshape(SB_PER * DEC_SEQ, D)], 0)
        m['st_shift'] = f(inp['state_shift'][0][sl])
        m['st_wkv'] = f(inp['state_wkv'][0][sl])
        m['st_ret'] = f(inp['state_ret'][0][sl])
        m['st_conv'] = f(inp['state_conv'][0][sl]).reshape(32, D)
        in_maps.append(m)
    return in_maps


def kernel(**inputs):
    if 'nc' not in _PROG:
        _PROG['nc'] = build_program()[0]
    nc = _PROG['nc']
    in_maps = _prep_inputs(inputs)
    res = run_bass_kernel_spmd(nc, in_maps, core_ids=list(range(NCORE)))
    R = res.results
    y_prompt = np.stack([R[c]['yout'][:SEQ] for c in range(NCORE)], 0)
    y_sample = np.concatenate([R[c]['yout'][SEQ:].reshape(SB_PER, DEC_SEQ, D) for c in range(NCORE)], 0)
    p_shift = np.stack([R[c]['shift_out'][0] for c in range(NCORE)], 0)[None]
    s_shift = np.concatenate([R[c]['shift_out'][1:] for c in range(NCORE)], 0)[None]
    p_wkv = np.stack([R[c]['wkv_out'][0] for c in range(NCORE)], 0)[None]
    s_wkv = np.concatenate([R[c]['wkv_out'][1:] for c in range(NCORE)], 0)[None]
    p_ret = np.stack([R[c]['ret_out'][0] for c in range(NCORE)], 0)[None]
    s_ret = np.concatenate([R[c]['ret_out'][1:] for c in range(NCORE)], 0)[None]
    p_conv = np.stack([R[c]['conv_out'][0:2] for c in range(NCORE)], 0)[None]
    s_conv = np.concatenate([R[c]['conv_out'][2:].reshape(SB_PER, 2, D) for c in range(NCORE)], 0)[None]
    outs = (y_prompt, y_sample, p_shift, p_wkv, p_ret, p_conv, s_shift, s_wkv, s_ret, s_conv)
    return tuple(np.ascontiguousarray(o, dtype=np.float32) for o in outs)
```
